# Optimizing a Trainium2 kernel written in Bass

```python
import math
import jax, jax.numpy as jnp
from jax import lax
import numpy as np

D_MODEL = 1024
BATCH = 16
SEQ = 2048
DEPTH = 1
DEC_BATCH = 32
DEC_SEQ = 2048
PAST_LEN = 128

HEAD_DIM = 64
DIL_PATTERNS = ((128, 1), (512, 4), (2048, 16))
N_GROUPS_A = 3
HEADS_PER_GROUP_A = 6
N_HEADS_A = N_GROUPS_A * HEADS_PER_GROUP_A
WIDTH_A = N_HEADS_A * HEAD_DIM
DA_QBLOCK = 64
N_HEADS_B = 14
WIDTH_B = N_HEADS_B * HEAD_DIM
GRID_W = 64
NA_KH = 8
NA_KW = 16
NA_QCB = 16
IN_COLS = 4 * WIDTH_A + 4 * WIDTH_B + 2 * D_MODEL
SPLIT_POINTS = (WIDTH_A, 2 * WIDTH_A, 3 * WIDTH_A, 4 * WIDTH_A,
                4 * WIDTH_A + WIDTH_B, 4 * WIDTH_A + 2 * WIDTH_B, 4 * WIDTH_A + 3 * WIDTH_B, 4 * WIDTH_A + 4 * WIDTH_B,
                4 * WIDTH_A + 4 * WIDTH_B + D_MODEL)
RMS_EPS = 1e-6
NEG_INF = -1e30

kernel_name = "hybrid_dilated_neighbourhood_encoder"


def rms_norm(x, g):
    xf = x.astype(jnp.float32)
    y = xf * lax.rsqrt(jnp.mean(xf * xf, axis=-1, keepdims=True) + RMS_EPS)
    return (y * g.astype(jnp.float32)).astype(x.dtype)


def alibi_slopes():
    return jnp.asarray((2.0 ** (-8.0 * np.arange(1, N_HEADS_A + 1) / N_HEADS_A)).astype(np.float32))


def dilated_window_attention(q, k, v, window, dilation, slopes):
    b, t, h, e = q.shape
    d = dilation
    L = t // d
    half = (window // 2) // d
    qblk = math.gcd(L, DA_QBLOCK)
    nb = L // qblk
    kl = qblk + 2 * half
    qs = q.reshape(b, nb, qblk, d, h, e)
    pad = ((0, 0), (half, half), (0, 0), (0, 0), (0, 0))
    kp = jnp.pad(k.reshape(b, L, d, h, e), pad)
    vp = jnp.pad(v.reshape(b, L, d, h, e), pad)
    kidx = np.arange(nb)[:, None] * qblk + np.arange(kl)[None, :]
    kb = kp[:, kidx]
    vb = vp[:, kidx]
    s = jnp.einsum('bnqrhe,bnkrhe->bnrhqk', qs, kb, preferred_element_type=jnp.float32) * (e ** -0.5)
    off = np.arange(kl)[None, :] - half - np.arange(qblk)[:, None]
    kpos = kidx - half
    valid = (np.abs(off) <= half)[None] & ((kpos >= 0) & (kpos < L))[:, None, :]
    dist = (np.abs(off) * d).astype(np.float32)
    s = s - slopes[:, None, None] * dist
    s = jnp.where(valid[None, :, None, None], s, NEG_INF)
    lse = jax.nn.logsumexp(s, axis=-1)
    p = jnp.exp(s - lse[..., None])
    o = jnp.einsum('bnrhqk,bnkrhe->bnqrhe', p.astype(v.dtype), vb).reshape(b, t, h, e)
    lse = lse.transpose(0, 1, 4, 2, 3).reshape(b, t, h)
    return o, lse


def neighbourhood_attention(q, k, v, rpb):
    b, t, h, e = q.shape
    rows = t // GRID_W
    kh = min(NA_KH, rows)
    kw = NA_KW
    slab = NA_QCB + kw
    ncb = GRID_W // NA_QCB
    r = np.arange(rows)
    rs = np.clip(r - kh // 2, 0, rows - kh)
    key_rows = rs[:, None] + np.arange(kh)[None, :]
    c0 = np.arange(ncb) * NA_QCB
    s0 = np.clip(c0 - kw // 2, 0, GRID_W - slab)
    key_cols = s0[:, None] + np.arange(slab)[None, :]
    nk = kh * slab
    idx = (key_rows[:, None, :, None] * GRID_W + key_cols[None, :, None, :]).reshape(rows, ncb, nk)
    qcol = c0[:, None] + np.arange(NA_QCB)[None, :]
    cs = np.clip(qcol - kw // 2, 0, GRID_W - kw)
    col_valid = (key_cols[:, None, :] >= cs[:, :, None]) & (key_cols[:, None, :] < cs[:, :, None] + kw)
    valid = np.broadcast_to(col_valid[:, :, None, :], (ncb, NA_QCB, kh, slab)).reshape(ncb, NA_QCB, nk)
    dr = key_rows - r[:, None] + NA_KH - 1
    dc = np.clip(key_cols[:, None, :] - qcol[:, :, None], -(kw - 1), kw - 1) + kw - 1
    bias = rpb[:, dr[:, None, None, :, None], dc[None, :, :, None, :]]
    bias = bias.astype(jnp.float32).reshape(h, rows, ncb, NA_QCB, nk).transpose(1, 2, 0, 3, 4)
    qb = q.reshape(b, rows, ncb, NA_QCB, h, e)
    kb = k[:, idx]
    vb = v[:, idx]
    s = jnp.einsum('brcqhe,brckhe->brchqk', qb, kb, preferred_element_type=jnp.float32) * (e ** -0.5)
    s = s + bias[None]
    s = jnp.where(valid[None, None, :, None, :, :], s, NEG_INF)
    p = jax.nn.softmax(s, axis=-1)
    return jnp.einsum('brchqk,brckhe->brcqhe', p.astype(v.dtype), vb).reshape(b, t, h, e)


def encoder_layer(x, norm_pre, w_in, b_gate, rpb, w_proj_a, w_proj_b, w_out, norm_post):
    b, t, _ = x.shape
    hn = rms_norm(x, norm_pre)
    proj = hn @ w_in
    qa, ka, va, za, qb, kb, vb, zb, ga, gb = jnp.split(proj, SPLIT_POINTS, axis=-1)
    qa = qa.reshape(b, t, N_HEADS_A, HEAD_DIM)
    ka = ka.reshape(b, t, N_HEADS_A, HEAD_DIM)
    va = va.reshape(b, t, N_HEADS_A, HEAD_DIM)
    slopes = alibi_slopes()
    outs, lses = [], []
    for g, (window, dilation) in enumerate(DIL_PATTERNS):
        sl = slice(g * HEADS_PER_GROUP_A, (g + 1) * HEADS_PER_GROUP_A)
        o, lse = dilated_window_attention(qa[:, :, sl], ka[:, :, sl], va[:, :, sl], window, dilation,
                                          slopes[g::N_GROUPS_A])
        outs.append(o)
        lses.append(lse)
    alpha = jax.nn.softmax(jnp.stack(lses, axis=0), axis=0)
    oa = jnp.concatenate([o * alpha[g][..., None].astype(o.dtype) for g, o in enumerate(outs)], axis=2)
    ya = (oa.reshape(b, t, WIDTH_A) * jax.nn.silu(za)) @ w_proj_a
    ob = neighbourhood_attention(qb.reshape(b, t, N_HEADS_B, HEAD_DIM), kb.reshape(b, t, N_HEADS_B, HEAD_DIM),
                                 vb.reshape(b, t, N_HEADS_B, HEAD_DIM), rpb)
    yb = (ob.reshape(b, t, WIDTH_B) * jax.nn.silu(zb)) @ w_proj_b
    merged = jax.nn.sigmoid(ga + b_gate[0]) * ya + jax.nn.sigmoid(gb + b_gate[1]) * yb
    out = merged @ w_out
    return x + rms_norm(out, norm_post)


def setup_inputs(seed: int = 0) -> dict:
    key = jax.random.key(seed)
    ks = jax.random.split(key, 10)
    f32 = jnp.float32
    return {
        "x_prompt": jax.random.normal(ks[0], (BATCH, SEQ, D_MODEL), f32),
        "x_sample": jax.random.normal(ks[1], (DEC_BATCH, DEC_SEQ, D_MODEL), f32),
        "norm_pre": 1.0 + 0.1 * jax.random.normal(ks[2], (DEPTH, D_MODEL), f32),
        "w_in": jax.random.normal(ks[3], (DEPTH, D_MODEL, IN_COLS), f32) * D_MODEL ** -0.5,
        "b_gate": 0.1 * jax.random.normal(ks[4], (DEPTH, 2, D_MODEL), f32),
        "rpb": 0.1 * jax.random.normal(ks[5], (DEPTH, N_HEADS_B, 2 * NA_KH - 1, 2 * NA_KW - 1), f32),
        "w_proj_a": jax.random.normal(ks[6], (DEPTH, WIDTH_A, D_MODEL), f32) * WIDTH_A ** -0.5,
        "w_proj_b": jax.random.normal(ks[7], (DEPTH, WIDTH_B, D_MODEL), f32) * WIDTH_B ** -0.5,
        "w_out": jax.random.normal(ks[8], (DEPTH, D_MODEL, D_MODEL), f32) * D_MODEL ** -0.5,
        "norm_post": 1.0 + 0.1 * jax.random.normal(ks[9], (DEPTH, D_MODEL), f32),
    }


def reference(x_prompt, x_sample, norm_pre, w_in, b_gate, rpb, w_proj_a, w_proj_b, w_out, norm_post):
    y_prompt = x_prompt
    y_sample = x_sample
    for l in range(DEPTH):
        y_prompt = encoder_layer(y_prompt, norm_pre[l], w_in[l], b_gate[l], rpb[l], w_proj_a[l], w_proj_b[l],
                                 w_out[l], norm_post[l])
        y_sample = encoder_layer(y_sample, norm_pre[l], w_in[l], b_gate[l], rpb[l], w_proj_a[l], w_proj_b[l],
                                 w_out[l], norm_post[l])
    return (y_prompt, y_sample)
```

```python
import numpy as np
import concourse.bass as bass
import concourse.mybir as mybir
from concourse.bass_utils import run_bass_kernel_spmd

F32 = mybir.dt.float32
BF16 = mybir.dt.bfloat16
ALU = mybir.AluOpType
ACTF = mybir.ActivationFunctionType

T = 2048
D = 1024
NCORES = 8
HEAD = 64
WA = 1152
WB = 896
NCH_A = 9
NCH_B = 7
NCH = 16
EPS = 1e-6
DIL = (1, 4, 16)
TBL_IDX = {("e", 3): 0, ("e", 2): 1, ("i", 2): 2, ("i", 1): 3, ("i", 0): 4, ("i", -1): 5, ("i", -2): 6,
           ("e", -2): 7, ("e", -3): 8}


class Buf:
    __slots__ = ("name", "last_w", "readers")

    def __init__(self, name=""):
        self.name = name
        self.last_w = None
        self.readers = []


class Op:
    __slots__ = ("eng", "fn", "deps", "flag", "val", "sem", "is_dma")

    def __init__(self, eng, fn, is_dma=False):
        self.eng = eng
        self.fn = fn
        self.deps = []
        self.flag = False
        self.val = None
        self.sem = None
        self.is_dma = is_dma


class Prog:
    ENGS = ("pe", "act", "dve", "pool", "sp")

    def __init__(self, nc):
        self.nc = nc
        self.ops = {e: [] for e in self.ENGS}

    def _add_dep(self, op, prod):
        if prod is None or prod is op:
            return
        if prod.eng == "pe" and op.eng == "pe" and not prod.is_dma and not op.is_dma:
            return
        if prod not in op.deps:
            op.deps.append(prod)
            prod.flag = True

    def op(self, eng, meth, kw, reads=(), writes=(), dma_key=None):
        fn = (lambda e, meth=meth, kw=kw: getattr(e, meth)(**kw))
        reads = [b for x in reads for b in (x if isinstance(x, (list, tuple)) else [x])]
        writes = [b for x in writes for b in (x if isinstance(x, (list, tuple)) else [x])]
        o = Op(eng, fn, is_dma=dma_key is not None)
        if dma_key is not None:
            o.sem = dma_key
            o.flag = True
        for b in reads:
            self._add_dep(o, b.last_w)
        for b in writes:
            self._add_dep(o, b.last_w)
            for r in b.readers:
                self._add_dep(o, r)
        for b in reads:
            b.readers.append(o)
        for b in writes:
            b.last_w = o
            b.readers = []
        self.ops[eng].append(o)
        return o

    def emit(self):
        nc = self.nc
        eng_sem = {e: nc.alloc_semaphore("es_" + e) for e in self.ENGS}
        keys = []
        for e in self.ENGS:
            for o in self.ops[e]:
                if o.is_dma and o.sem not in keys:
                    keys.append(o.sem)
        dsem = {k: [nc.alloc_semaphore("ds_%d" % i), 0] for i, k in enumerate(keys)}
        for e in self.ENGS:
            cnt = 0
            for o in self.ops[e]:
                if o.is_dma:
                    d = dsem[o.sem]
                    d[1] += 16
                    o.val = d[1]
                    o.sem = d[0]
                elif o.flag:
                    cnt += 1
                    o.val = cnt
                    o.sem = eng_sem[e]
        all_dma_final = [(d[0], d[1]) for d in dsem.values()]
        with nc.Block() as block:
            def run(e):
                def body(eng):
                    known = {}
                    for o in self.ops[e]:
                        for p in o.deps:
                            sid = p.sem.num
                            if known.get(sid, 0) < p.val:
                                eng.wait_ge(p.sem, p.val)
                                known[sid] = p.val
                        ins = o.fn(eng)
                        if o.is_dma:
                            ins.then_inc(o.sem, 16)
                        elif o.flag:
                            ins.then_inc(o.sem, 1)
                    if e == "sp":
                        for (s, v) in all_dma_final:
                            if known.get(s.num, 0) < v:
                                eng.wait_ge(s, v)
                return body
            block.tensor(run("pe"))
            block.scalar(run("act"))
            block.vector(run("dve"))
            block.gpsimd(run("pool"))
            block.sync(run("sp"))


NEG = -240000.0


def _alibi_tables():
    slopes = (2.0 ** (-8.0 * np.arange(1, 19) / 18)).astype(np.float64)
    out = np.full((NCH_A, 128, 1024), NEG, np.float32)
    j = np.arange(128)[:, None].astype(np.float64)
    i = np.arange(128)[None, :].astype(np.float64)
    for c in range(NCH_A):
        g, s = c // 3, c % 3
        d = DIL[g]
        for hh in range(2):
            sl = float(slopes[g + 3 * (2 * s + hh)])
            if g < 2:
                bh = np.where(j <= i, -8.0 * sl * d * np.abs(i - j - 64.0), NEG)
                ah = np.where(j >= i, -8.0 * sl * d * np.abs(i - j + 64.0), NEG)
                main = np.concatenate([bh, ah], 1)
                bh2 = bh.copy(); bh2[64:, :] = NEG
                ah2 = ah.copy(); ah2[:64, :] = NEG
                bnd = np.concatenate([bh2, ah2], 1)
                out[c, :, hh * 256:(hh + 1) * 256] = main
                out[c, :, 512 + hh * 256:512 + (hh + 1) * 256] = bnd
            else:
                e2 = np.where(np.abs(i - j) <= 64, -8.0 * sl * d * np.abs(i - j), NEG)
                out[c, :, hh * 128:(hh + 1) * 128] = e2
    return np.maximum(out, NEG).astype(np.float32)


def _b_pairs():
    rows = 32
    rs = np.clip(np.arange(rows) - 4, 0, rows - 8)
    pairs = {}
    for u in range(16):
        for v in range(16):
            pat = np.zeros((2, 2), bool)
            for rl_k in range(2):
                for rl_q in range(2):
                    r = 2 * v + rl_q
                    rp = 2 * u + rl_k
                    pat[rl_k, rl_q] = (rs[r] <= rp < rs[r] + 8)
            if pat.any():
                pairs[(u, v)] = pat
    return pairs


def _b_tables(rpb):
    pairs = _b_pairs()
    tbl_pat = {}
    tbl_delta = {}
    pair_tbl = {}
    for (u, v), pat in pairs.items():
        dl = u - v
        ipat = np.zeros((2, 2), bool)
        for a in range(2):
            for b in range(2):
                dr = 2 * dl + a - b
                ipat[a, b] = (-4 <= dr <= 3)
        if abs(dl) <= 2 and (pat == ipat).all():
            key = ("i", dl)
        else:
            assert pat.all(), (u, v, pat)
            key = ("e", dl)
        assert key in TBL_IDX, key
        if key in tbl_pat:
            assert (tbl_pat[key] == pat).all()
        tbl_pat[key] = pat
        tbl_delta[key] = dl
        pair_tbl[(u, v)] = TBL_IDX[key]
    assert len(tbl_pat) == 9
    cp = np.arange(64)[:, None]
    cq = np.arange(64)[None, :]
    cs = np.clip(cq - 8, 0, 48)
    colvalid = (cp >= cs) & (cp < cs + 16)
    dc = np.clip(cp - cq, -15, 15) + 15
    bias = np.zeros((14, 9, 128, 128), np.float32)
    mask = np.zeros((9, 128, 128), np.float32)
    for key, idx in TBL_IDX.items():
        pat = tbl_pat[key]
        dl = tbl_delta[key]
        for a in range(2):
            for b in range(2):
                if not pat[a, b]:
                    continue
                dr = 2 * dl + a - b + 7
                assert 0 <= dr <= 14
                mask[idx, a * 64:(a + 1) * 64, b * 64:(b + 1) * 64] = colvalid
                bias[:, idx, a * 64:(a + 1) * 64, b * 64:(b + 1) * 64] = rpb[:, dr][:, dc]
    bias = bias.reshape(7, 2, 9, 128, 128).transpose(0, 3, 1, 2, 4).reshape(7, 128, 2304)
    mask = (mask - 1.0) * (-NEG)
    maskf = np.broadcast_to(mask[None, None], (7, 2, 9, 128, 128)).transpose(0, 3, 1, 2, 4).reshape(7, 128, 2304)
    return np.ascontiguousarray(bias), np.ascontiguousarray(maskf), pair_tbl


def build(nseq, nchunks=NCH):
    nc = bass.Bass("TRN2", target_bir_lowering=False)
    P = Prog(nc)
    pair_tbl = _b_tables(np.zeros((14, 15, 31), np.float32))[2]
    pairs = sorted(pair_tbl.keys())
    v_of_u = {u: [v for (uu, v) in pairs if uu == u] for u in range(16)}
    for u in range(16):
        assert v_of_u[u] == list(range(v_of_u[u][0], v_of_u[u][-1] + 1))
    first_u = {v: min(u for (u, vv) in pairs if vv == v) for v in range(16)}
    last_u = {v: max(u for (u, vv) in pairs if vv == v) for v in range(16)}

    def din(name, shape, dt=F32):
        return nc.dram_tensor(name, list(shape), dt, kind="ExternalInput").ap()

    x_d = din("x", [nseq, T, D])
    y_d = nc.dram_tensor("y", [nseq, T, D], F32, kind="ExternalOutput").ap()
    wch_d = din("wch", [NCH, 128, 4096])
    wf_d = din("wf", [8, 128, 4096])
    wo_d = din("wo", [2, 128, 4096])
    eta_d = din("eta", [NCH_A, 128, 1024])
    bb_d = din("bbias", [NCH_B, 128, 2304])
    bm_d = din("bmask", [NCH_B, 128, 2304])
    gpre_d = din("gpre", [128, 8])
    bg_d = din("bgate", [128, 16])
    gpost_d = din("gpost", [128, D])
    id_d = din("ident", [128, 128])
    wch_s = nc.dram_tensor("wch_s", [NCH, 128, 4096], BF16, kind="Internal").ap()
    wf_s = nc.dram_tensor("wf_s", [8, 128, 4096], BF16, kind="Internal").ap()
    wo_s = nc.dram_tensor("wo_s", [2, 128, 4096], BF16, kind="Internal").ap()
    eta_s = nc.dram_tensor("eta_s", [NCH_A, 128, 1024], BF16, kind="Internal").ap()
    etb_s = nc.dram_tensor("etb_s", [NCH_B, 128, 2304], BF16, kind="Internal").ap()

    def sb(name, shape, dt):
        return nc.alloc_sbuf_tensor(name, list(shape), dt).ap()

    hnT = sb("hnT", [128, 8, T], BF16)
    abT = sb("abT", [128, NCH, T], BF16)
    qbuf = [sb("q%d" % i, [128, T], BF16) for i in range(2)]
    kbuf = [sb("k%d" % i, [128, 2560], BF16) for i in range(2)]
    vbuf = [sb("v%d" % i, [128, 20, 192], BF16) for i in range(2)]
    ubuf = [sb("u%d" % i, [128, T], BF16) for i in range(2)]
    scr = sb("scr", [128, 5120], F32)
    wbuf = [sb("w%d" % i, [128, 4096], BF16) for i in range(2)]
    ebuf = [sb("e%d" % i, [128, 2304], BF16) for i in range(2)]
    xr0 = sb("xr0", [128, 1024], F32)
    pbuf = [sb("p%d" % i, [128, 1024], BF16) for i in range(2)]
    tmpf_all = sb("tfall", [128, 2048], F32)
    tmpf = [tmpf_all[:, i * 512:(i + 1) * 512] for i in range(4)]
    xr1 = tmpf_all[:, 1024:2048]
    ident = sb("identb", [128, 128], BF16)
    identf = sb("identf", [128, 128], F32)
    gpre = sb("gpre_s", [128, 8], F32)
    bgate = sb("bgate_s", [128, 16], F32)
    gpost = sb("gpost_s", [128, D], F32)
    stat = sb("stat", [128, 64], F32)
    junk = sb("junk", [128, D], BF16)

    scr_bf = scr.bitcast(BF16)
    numS = [scr_bf[:, g * T:(g + 1) * T] for g in range(3)]
    zsum = scr[:, 3072:5120]
    xst = [scr[:, 0:1024], scr[:, 1024:2048]]
    xsb = [scr_bf[:, 4096:5120], scr_bf[:, 5120:6144]]
    yst = [scr[:, 3072:4096], scr[:, 4096:5120]]

    ps_all = nc.alloc_psum_tensor("ps_all", [128, 4096], F32).ap()
    banks = [ps_all[:, i * 512:(i + 1) * 512] for i in range(8)]
    bankB = [Buf("bank%d" % i) for i in range(8)]

    B = {}
    RQ = {nm: [Buf("%s_q%d" % (nm, q)) for q in range(4)] for nm in ("numS0", "numS1", "numS2", "zsum")}
    ALIAS = {"xst0": RQ["numS0"], "xst1": RQ["numS1"], "xsb0": RQ["numS2"][0:2], "xsb1": RQ["numS2"][2:4],
             "yst0": RQ["zsum"][0:2], "yst1": RQ["zsum"][2:4]}
    PS_ = [[Buf("p%d_%d" % (i, j)) for j in range(2)] for i in range(2)]
    for i in range(2):
        ALIAS["p%d" % i] = PS_[i]
        for j in range(2):
            ALIAS["p%d_%d" % (i, j)] = [PS_[i][j]]
    for nm in ("numS0", "numS1", "numS2", "zsum"):
        ALIAS[nm] = RQ[nm]
        for q in range(4):
            ALIAS["%s_q%d" % (nm, q)] = [RQ[nm][q]]

    def buf(name):
        if name in ALIAS:
            return ALIAS[name]
        if name not in B:
            B[name] = Buf(name)
        return B[name]

    def mm(out, lhsT, rhs, start, stop, reads, writes, skip=False):
        kw = dict(out=out, lhsT=lhsT, rhs=rhs, start=start, stop=stop)
        if skip:
            kw["skip_group_check"] = True
        P.op("pe", "matmul", kw, reads, writes)

    def act(out, in_, func, reads, writes, **kw):
        P.op("act", "activation", dict(out=out, in_=in_, func=func, **kw), reads, writes)

    def tt(out, in0, in1, op, reads, writes, eng="dve"):
        P.op(eng, "tensor_tensor", dict(out=out, in0=in0, in1=in1, op=op), reads, writes)

    def stt(out, in0, scalar, in1, op0, op1, reads, writes, eng="dve"):
        P.op(eng, "scalar_tensor_tensor", dict(out=out, in0=in0, scalar=scalar, in1=in1, op0=op0, op1=op1), reads, writes)

    def ts(out, in0, s1, s2, op0, op1, reads, writes, eng="dve"):
        kw = dict(out=out, in0=in0, scalar1=s1, scalar2=s2, op0=op0)
        if op1 is not None:
            kw["op1"] = op1
        P.op(eng, "tensor_scalar", kw, reads, writes)

    def cp(out, in_, reads, writes, eng="dve"):
        P.op(eng, "tensor_copy", dict(out=out, in_=in_), reads, writes)

    def dma(eng, out, in_, reads, writes, key):
        P.op(eng, "dma_start", dict(out=out, in_=in_), reads, writes, dma_key=key)

    cast_order = [0, 3, 6, 1, 4, 7, 2, 5, 8] + list(range(NCH_A, NCH))

    def emit_casts(lo, hi, gate):
        for n_ in cast_order[lo:hi]:
            if n_ < NCH_A:
                dma("pool", eta_s[n_], eta_d[n_], gate, [buf("eta_c%d" % n_)], "ppe%d" % n_)
            dma("pool", wch_s[n_], wch_d[n_], gate, [buf("wch_c%d" % n_)], "ppc%d" % n_)

    emit_casts(0, 2, [])
    dma("sp", identf, id_d, [], [buf("identf")], "c0")
    dma("sp", gpre, gpre_d, [], [buf("gpre")], "c1")
    dma("sp", bgate, bg_d, [], [buf("bgate")], "c2")
    dma("sp", gpost, gpost_d, [], [buf("gpost")], "c3")
    cp(ident, identf, [buf("identf")], [buf("ident")])
    ts(bgate, bgate, 0.5, None, ALU.mult, None, [buf("bgate")], [buf("bgate")])
    for i in range(2):
        P.op("dve", "memset", dict(ap=kbuf[i], constant=0.0), [], [buf("k%d" % i)])
        P.op("dve", "memset", dict(ap=vbuf[i], constant=0.0), [], [buf("v%d" % i)])
        P.op("dve", "memset", dict(ap=vbuf[i][:, :, 64:128], constant=1.0), [], [buf("v%d" % i)])
        P.op("dve", "memset", dict(ap=qbuf[i], constant=0.0), [], [buf("q%d" % i)])
    epsc = stat[:, 40:41]
    P.op("dve", "memset", dict(ap=epsc, constant=EPS), [], [buf("statinit")])
    ts(gpost, gpost, 0.5, None, ALU.mult, None, [buf("gpost")], [buf("gpost")])
    abhi = abT[:, 9:16, :].rearrange("p c t -> p (c t)")
    abhi_f = abhi.bitcast(F32)
    bt_stage = []
    for i in range(2):
        o = i * 2880
        bt_stage.append((abhi_f[:, o:o + 1152], abhi_f[:, o + 1152:o + 2304], abhi[:, 2 * (o + 2304): 2 * (o + 2304) + 1152]))

    def btab_gen():
        items = [(c, h2) for c in range(NCH_B) for h2 in range(2)]

        def loads(k):
            c, h2 = items[k]
            sl = slice(h2 * 1152, (h2 + 1) * 1152)
            stg, stg2, _ = bt_stage[k % 2]
            dma("sp", stg, bb_d[c, :, sl], [], [buf("btA%d" % (k % 2))], "c4%d" % (k % 2))
            dma("sp", stg2, bm_d[c, :, sl], [], [buf("btB%d" % (k % 2))], "c5%d" % (k % 2))

        loads(0)
        for k in range(len(items)):
            c, h2 = items[k]
            sl = slice(h2 * 1152, (h2 + 1) * 1152)
            if k + 1 < len(items):
                loads(k + 1)
            yield
            stg, stg2, so = bt_stage[k % 2]
            stt(so, stg, 8.0, stg2, ALU.mult, ALU.add, [buf("btA%d" % (k % 2)), buf("btB%d" % (k % 2))], [buf("btO%d" % (k % 2))])
            dma("sp", etb_s[c, :, sl], so, [buf("btO%d" % (k % 2))], [buf("etb_s%d" % (k % 2))], "c6%d" % (k % 2))
            yield

    state = {"ipb": 0, "x": 0}

    def load_chunk_consts(n, slot):
        dma("sp", wbuf[slot], wch_s[n], [buf("wch_c%d" % n)], [buf("w%d" % slot)], "w%d" % slot)
        if n < NCH_A:
            dma("sp", ebuf[slot][:, 0:1024], eta_s[n], [buf("eta_c%d" % n)], [buf("e%d" % slot)], "e%d" % slot)
        else:
            dma("sp", ebuf[slot], etb_s[n - NCH_A], [buf("etb_s0"), buf("etb_s1")], [buf("e%d" % slot)], "e%d" % slot)

    def chunk_geom(n):
        if n < NCH_A:
            g = n // 3
            return ("A2" if g == 2 else "A01"), DIL[g], g
        return "B", 1, 3

    order = []
    for s in range(3):
        for g in range(3):
            order.append(3 * g + s)
    order += list(range(NCH_A, NCH))
    order = order[:nchunks] if nchunks < NCH else order

    def vslots(n):
        kind, d, g = chunk_geom(n)
        res = []
        if kind == "A01":
            L = T // d
            nq = L // 128
            for r in range(d):
                for kb in range(nq + 1):
                    m0 = 128 * kb - 64
                    lo = max(m0, 0)
                    hi = min(m0 + 128, L)
                    res.append((r * (nq + 1) + kb, lo * d + r, d, hi - lo, lo - m0))
        elif kind == "A2":
            for r in range(16):
                res.append((r, r, 16, 128, 0))
        else:
            for u in range(16):
                res.append((u, 128 * u, 1, 128, 0))
        return res

    def vgroups(n):
        sl = vslots(n)
        res = []
        i = 0
        while i < len(sl):
            grp = [sl[i]]
            if sl[i][3] == 128:
                while len(grp) < 4 and i + len(grp) < len(sl) and sl[i + len(grp)][3] == 128 \
                        and sl[i + len(grp)][0] == grp[-1][0] + 1:
                    grp.append(sl[i + len(grp)])
            i += len(grp)
            res.append(grp)
        return res

    def inproj_gen(n, bs, wslot):
        kind, d, g = chunk_geom(n)
        w4 = wbuf[wslot].rearrange("p (m k c) -> p m k c", m=4, k=8)
        wB = buf("w%d" % wslot)
        hB = buf("hnT")
        qB, kB, vB, uB = buf("q%d" % bs), buf("k%d" % bs), buf("v%d" % bs), buf("u%d" % bs)
        qb_, kb_, vb_, ub_ = qbuf[bs], kbuf[bs], vbuf[bs], ubuf[bs]

        def acc_fm(mi, tt_):
            bi = state["ipb"]
            state["ipb"] ^= 1
            ps = banks[bi]
            for kc in range(8):
                mm(ps, w4[:, mi, kc, :], hnT[:, kc, tt_ * 512:(tt_ + 1) * 512], kc == 0, kc == 7, [wB, hB], [bankB[bi]])
                if kc == 3:
                    yield None
            yield (bi, ps)

        for tt_ in range(4):
            for r_ in acc_fm(2, tt_):
                if r_ is None:
                    yield
                else:
                    bi, ps = r_
            tf = tmpf[tt_ % 2]
            tB = buf("tf%d" % (tt_ % 2))
            act(tf, ps, ACTF.Tanh, [bankB[bi]], [tB], scale=0.5)
            stt(ub_[:, tt_ * 512:(tt_ + 1) * 512], tf, 1.0, ps, ALU.add, ALU.mult, [tB, bankB[bi]], [uB])
            yield
        for mi, dstB in ((0, qB), (1, kB)):
            for tt_ in range(4):
                for r_ in acc_fm(mi, tt_):
                    if r_ is None:
                        yield
                    else:
                        bi, ps = r_
                if (kind == "A01" and d == 1) or kind == "B":
                    off = 64 if (mi == 1 and kind == "A01") else 0
                    dst = (kb_ if mi == 1 else qb_)[:, off + tt_ * 512: off + (tt_ + 1) * 512]
                    src = ps
                elif kind == "A01":
                    if mi == 1:
                        dst = kb_.rearrange("p (r m) -> p r m", r=4)[:, :, 64 + 128 * tt_: 64 + 128 * tt_ + 128]
                    else:
                        dst = qb_.rearrange("p (r m) -> p r m", r=4)[:, :, 128 * tt_:128 * tt_ + 128]
                    src = ps.rearrange("p (j r) -> p r j", r=4)
                else:
                    base = (kb_[:, 0:T] if mi == 1 else qb_)
                    dst = base.rearrange("p (r m) -> p r m", r=16)[:, :, 32 * tt_:32 * tt_ + 32]
                    src = ps.rearrange("p (j r) -> p r j", r=16)
                act(dst, src, ACTF.Copy, [bankB[bi]], [dstB])
                yield
        for grp in vgroups(n):
            bi = state["ipb"]
            state["ipb"] ^= 1
            ps = banks[bi]
            for q, (slot, t0, step, nv, row0) in enumerate(grp):
                for kc in range(8):
                    lhs = hnT[:, kc, t0: t0 + (nv - 1) * step + 1: step]
                    mm(ps[0:nv, q * 128:(q + 1) * 128], lhs, w4[:, 3, kc, :], kc == 0, kc == 7, [wB, hB], [bankB[bi]])
                if q == 1 and len(grp) > 2:
                    yield
            slot0, _, _, nv, row0 = grp[0]
            ng = len(grp)
            if nv == 128:
                dst = bass.AP(vb_.tensor, vb_.offset + slot0 * 192, [list(vb_.ap[0]), [192, ng], [128, 2], [1, 64]])
                src = ps[:, 0:ng * 128].rearrange("p (s h c) -> p s h c", s=ng, h=2)
            else:
                dst = bass.AP(vb_.tensor, vb_[row0:row0 + 64, slot0, :].offset, [[vb_.ap[0][0], 64], [128, 2], [1, 64]])
                src = ps[0:64, 0:128].rearrange("p (h c) -> p h c", h=2)
            act(dst, src, ACTF.Copy, [bankB[bi]], [vB])
            yield

    def att_A01(n, bs, eslot):
        kind, d, g = chunk_geom(n)
        L = T // d
        nq = L // 128
        qb_, kb_, vb_, ub_ = qbuf[bs], kbuf[bs], vbuf[bs], ubuf[bs]
        qB, kB, vB, uB = buf("q%d" % bs), buf("k%d" % bs), buf("v%d" % bs), buf("u%d" % bs)
        eB = buf("e%d" % eslot)
        et = ebuf[eslot]
        units = [(r, kb) for r in range(d) for kb in range(nq + 1)]
        sring = [(2, 3), (4, 5)]
        tbank = {0: [6], 1: [7]}

        def scores(ui):
            r, kb = units[ui]
            sp_ = sring[ui % 2]
            qbase = r * L
            kbase = r * (L + 128) + 128 * kb
            if kb == 0:
                q0, nqc, o0 = qbase, 128, 128
            elif kb == nq:
                q0, nqc, o0 = qbase + 128 * (nq - 1), 128, 0
            else:
                q0, nqc, o0 = qbase + 128 * (kb - 1), 256, 0
            tsel = 512 if (kb == 0 or kb == nq) else 0
            for hh in range(2):
                mm(banks[sp_[hh]][:, 0:256], ident, et[:, tsel + hh * 256: tsel + hh * 256 + 256], True, False,
                   [buf("ident"), eB], [bankB[sp_[hh]]], skip=True)
            for hh in range(2):
                rows = slice(hh * 64, (hh + 1) * 64)
                mm(banks[sp_[hh]][:, o0: o0 + nqc], kb_[rows, kbase:kbase + 128], qb_[rows, q0:q0 + nqc],
                   False, True, [kB, qB], [bankB[sp_[hh]]], skip=True)

        def pslot(ui):
            k4 = ui % 4
            return pbuf[k4 // 2][:, (k4 % 2) * 512:(k4 % 2 + 1) * 512], buf("p%d_%d" % (k4 // 2, k4 % 2))

        def pv_first(ui):
            r, kb = units[ui]
            if kb > nq - 1:
                return
            pb, pB = pslot(ui)
            slot = r * (nq + 1) + kb
            for hh in range(2):
                lhs = vb_[:, slot, hh * 64: hh * 64 + 128]
                qb = kb
                tb = tbank[hh][0]
                mm(banks[tb][:, (qb % 4) * 128:(qb % 4 + 1) * 128], lhs, pb[:, hh * 256 + 128: hh * 256 + 256],
                   (qb % 4 == 0), False, [vB, pB], [bankB[tb]], skip=True)

        def front(ui):
            r, kb = units[ui]
            scores(ui)
            sp_ = sring[ui % 2]
            S2 = ps_all[:, sp_[0] * 512:(sp_[0] + 2) * 512].rearrange("p (h c) -> p h c", h=2)[:, :, 0:256]
            pb, pB = pslot(ui)
            act(pb.rearrange("p (h c) -> p h c", h=2), S2, ACTF.Exp, [bankB[sp_[0]], bankB[sp_[1]]], [pB], scale=0.125)

        front(0)
        for ui in range(len(units)):
            r, kb = units[ui]
            if ui + 1 < len(units):
                front(ui + 1)
            if ui >= 1:
                pv_first(ui - 1)
            pb, pB = pslot(ui)
            slot = r * (nq + 1) + kb
            grp_off = r * (nq // 4)
            for hh in range(2):
                lhs = vb_[:, slot, hh * 64: hh * 64 + 128]
                if kb >= 1:
                    qb = kb - 1
                    tb = tbank[hh][0]
                    mm(banks[tb][:, (qb % 4) * 128:(qb % 4 + 1) * 128], lhs, pb[:, hh * 256: hh * 256 + 128],
                       False, True, [vB, pB], [bankB[tb]], skip=True)
            if kb >= 4 and kb % 4 == 0:
                w = (kb - 1) // 4
                if d == 1:
                    tsl = slice(512 * w, 512 * w + 512)
                    uap, nap, zap = ub_[:, tsl], numS[g][:, tsl], zsum[:, tsl]
                    nB_, zB_ = buf("numS%d_q%d" % (g, w)), buf("zsum_q%d" % w)
                else:
                    uap, nap, zap = ub_[:, r:T:4], numS[g][:, r * 512:(r + 1) * 512], zsum[:, r:T:4]
                    nB_, zB_ = buf("numS%d" % g), buf("zsum")
                for hh in range(2):
                    tb = tbank[hh][0]
                    Tt = banks[tb]
                    nrows = slice(hh * 64, (hh + 1) * 64)
                    zrows = slice((1 - hh) * 64, (2 - hh) * 64)
                    tt(nap[nrows], Tt[nrows], uap[nrows], ALU.mult, [bankB[tb], uB], [nB_])
                    if g == 0:
                        ts(zap[nrows], Tt[zrows], 2.0, None, ALU.mult, None, [bankB[tb]], [zB_])
                    else:
                        stt(zap[nrows], Tt[zrows], 2.0, zap[nrows], ALU.mult, ALU.add, [bankB[tb], zB_], [zB_])
            if ui == len(units) - 1:
                pv_first(ui)
            yield

    def att_A2(n, bs, eslot):
        qb_, kb_, vb_, ub_ = qbuf[bs], kbuf[bs], vbuf[bs], ubuf[bs]
        qB, kB, vB, uB = buf("q%d" % bs), buf("k%d" % bs), buf("v%d" % bs), buf("u%d" % bs)
        eB = buf("e%d" % eslot)
        et = ebuf[eslot]
        units = [(r0, hh) for r0 in range(0, 16, 4) for hh in range(2)]
        sring = [2, 3]
        tbank = {0: [4, 6], 1: [5, 7]}

        def scores(ui):
            r0, hh = units[ui]
            si = sring[ui % 2]
            S = banks[si]
            rows = slice(hh * 64, (hh + 1) * 64)
            for q in range(4):
                mm(S[:, q * 128:(q + 1) * 128], ident, et[:, hh * 128:(hh + 1) * 128], q == 0, False,
                   [buf("ident"), eB], [bankB[si]], skip=True)
            for q in range(4):
                c0 = (r0 + q) * 128
                mm(S[:, q * 128:(q + 1) * 128], kb_[rows, c0:c0 + 128], qb_[rows, c0:c0 + 128], False, True, [kB, qB], [bankB[si]], skip=True)

        def tview(ap, r0):
            return ap.rearrange("p (m r) -> p r m", r=16)[:, r0:r0 + 4, :]

        def front(ui):
            r0, hh = units[ui]
            scores(ui)
            si = sring[ui % 2]
            S = banks[si]
            pb = pbuf[ui % 2][:, 0:512]
            pB = buf("p%d" % (ui % 2))
            act(pb, S, ACTF.Exp, [bankB[si]], [pB], scale=0.125)

        front(0)
        for ui in range(len(units)):
            r0, hh = units[ui]
            if ui + 1 < len(units):
                front(ui + 1)
            pb = pbuf[ui % 2][:, 0:512]
            pB = buf("p%d" % (ui % 2))
            tb = tbank[hh][(r0 // 4) % 2]
            for q in range(4):
                lhs = vb_[:, r0 + q, hh * 64: hh * 64 + 128]
                mm(banks[tb][:, q * 128:(q + 1) * 128], lhs, pb[:, q * 128:(q + 1) * 128], (q == 0), True, [vB, pB], [bankB[tb]], skip=True)
            if hh == 1:
                for h2 in range(2):
                    tb2 = tbank[h2][(r0 // 4) % 2]
                    Tt = banks[tb2].rearrange("p (a b) -> p a b", a=4)
                    nrows = slice(h2 * 64, (h2 + 1) * 64)
                    zrows = slice((1 - h2) * 64, (2 - h2) * 64)
                    tt(numS[2][nrows, r0 * 128:(r0 + 4) * 128].rearrange("p (a b) -> p a b", a=4), Tt[nrows], tview(ub_, r0)[nrows], ALU.mult,
                       [bankB[tb2], uB], [buf("numS2")])
                    stt(tview(zsum, r0)[nrows], Tt[zrows], 2.0, tview(zsum, r0)[nrows], ALU.mult, ALU.add, [bankB[tb2], buf("zsum")], [buf("zsum")])
            yield

    def att_B(n, bs, eslot):
        cb = n
        qb_, kb_, vb_, ub_ = qbuf[bs], kbuf[bs], vbuf[bs], ubuf[bs]
        qB, kB, vB, uB = buf("q%d" % bs), buf("k%d" % bs), buf("v%d" % bs), buf("u%d" % bs)
        eB = buf("e%d" % eslot)
        et = ebuf[eslot].rearrange("p (h t i) -> p h t i", h=2, t=9)
        units = [(hh, u) for hh in range(2) for u in range(16)]
        sring = [(2, 3), (4, 5)]
        tring = [6, 7]

        def front(ui):
            hh, u = units[ui]
            sa, sb_ = sring[ui % 2]
            rows = slice(hh * 64, (hh + 1) * 64)
            vs = v_of_u[u]
            v0 = vs[0]
            nv = len(vs)
            n1 = min(nv, 4)
            tids = [pair_tbl[(u, v)] for v in vs]
            runs = []
            st = 0
            for i2 in range(1, nv + 1):
                if i2 == nv or tids[i2] != tids[i2 - 1] + 1 or i2 == 4:
                    runs.append((st, i2))
                    st = i2
            first = {sa: True, sb_: True}
            for (a, b2) in runs:
                bk = sa if a < 4 else sb_
                off = a if a < 4 else a - 4
                t0 = tids[a]
                mm(banks[bk][:, off * 128:(off + b2 - a) * 128], ident,
                   et[:, hh, t0:t0 + (b2 - a), :].rearrange("p t i -> p (t i)"), first[bk], False,
                   [buf("ident"), eB], [bankB[bk]], skip=True)
                first[bk] = False
            mm(banks[sa][:, 0:n1 * 128], kb_[rows, u * 128:(u + 1) * 128], qb_[rows, v0 * 128:(v0 + n1) * 128], False, True,
               [kB, qB], [bankB[sa]], skip=True)
            if nv > 4:
                n2 = nv - 4
                mm(banks[sb_][:, 0:n2 * 128], kb_[rows, u * 128:(u + 1) * 128], qb_[rows, (v0 + 4) * 128:(v0 + nv) * 128], False, True,
                   [kB, qB], [bankB[sb_]], skip=True)
            pb = pbuf[ui % 2]
            pB = buf("p%d" % (ui % 2))
            act(pb[:, 0:n1 * 128], banks[sa][:, 0:n1 * 128], ACTF.Exp, [bankB[sa]], [pB], scale=0.125)
            if nv > 4:
                act(pb[:, 512:nv * 128], banks[sb_][:, 0:(nv - 4) * 128], ACTF.Exp, [bankB[sb_]], [pB], scale=0.125)

        front(0)
        for ui in range(len(units)):
            hh, u = units[ui]
            if ui + 1 < len(units):
                front(ui + 1)
            vs = v_of_u[u]
            pb = pbuf[ui % 2]
            pB = buf("p%d" % (ui % 2))
            lhs = vb_[:, u, hh * 64: hh * 64 + 128]
            for vi, v in enumerate(vs):
                tb = tring[(v // 4) % 2]
                mm(banks[tb][:, (v % 4) * 128:(v % 4 + 1) * 128], lhs, pb[:, vi * 128:(vi + 1) * 128],
                   (u == first_u[v] and v % 4 == 0), u == last_u[v], [vB, pB], [bankB[tb]], skip=True)
            for w in range(4):
                if u == last_u[4 * w + 3]:
                    tb = tring[w % 2]
                    Tt = banks[tb]
                    nrows = slice(hh * 64, (hh + 1) * 64)
                    zrows = slice((1 - hh) * 64, (2 - hh) * 64)
                    tsl = slice(512 * w, 512 * w + 512)
                    nB_, zB_ = buf("numS0_q%d" % w), buf("zsum_q%d" % w)
                    tt(numS[0][nrows, tsl], Tt[nrows], ub_[nrows, tsl], ALU.mult, [bankB[tb], uB], [nB_])
                    ts(zsum[nrows, tsl], Tt[zrows], 2.0, None, ALU.mult, None, [bankB[tb]], [zB_])
                    if hh == 1:
                        P.op("dve", "reciprocal", dict(out=zsum[:, tsl], in_=zsum[:, tsl]), [zB_], [zB_])
                        tt(abT[:, cb, tsl], numS[0][:, tsl], zsum[:, tsl], ALU.mult,
                           [nB_, zB_], [buf("abT_c%d" % cb)], eng="pool")
            yield

    def post_A_gen(s):
        for q in range(4):
            yield
            tsl = slice(512 * q, 512 * q + 512)
            P.op("dve", "reciprocal", dict(out=zsum[:, tsl], in_=zsum[:, tsl]), [buf("zsum_q%d" % q)], [buf("zsum_q%d" % q)])
            for g in range(3):
                if g == 0:
                    tt(abT[:, 3 * g + s, tsl], numS[g][:, tsl], zsum[:, tsl], ALU.mult,
                       [buf("numS%d_q%d" % (g, q)), buf("zsum_q%d" % q)], [buf("abT_c%d" % (3 * g + s))], eng="pool")
                else:
                    dd = DIL[g]
                    mq = 512 // dd
                    src = numS[g].rearrange("p (r m) -> p m r", r=dd)[:, mq * q:mq * (q + 1), :]
                    tt(abT[:, 3 * g + s, tsl].rearrange("p (m r) -> p m r", r=dd), src,
                       zsum[:, tsl].rearrange("p (m r) -> p m r", r=dd), ALU.mult,
                       [buf("numS%d" % g), buf("zsum_q%d" % q)], [buf("abT_c%d" % (3 * g + s))], eng=("pool" if g < 2 else "dve"))

    tpb = [banks[2 + i].bitcast(BF16) for i in range(4)]

    def S1(si, k):
        xi = k % 2
        xt = xst[xi]
        xB = buf("xst%d" % xi)
        dma("sp", xt, x_d[si, k * 128:(k + 1) * 128, :], [], [xB], "x%d" % xi)
        ss = stat[:, xi:xi + 1]
        rs_ = stat[:, 2 + xi:3 + xi]
        act(junk, xt, ACTF.Square, [xB], [buf("ss%d" % xi), buf("junk")], accum_out=ss)
        act(rs_, ss, ACTF.Sqrt, [buf("ss%d" % xi), buf("statinit")], [buf("rs%d" % xi)], scale=1.0 / D, bias=epsc)
        P.op("dve", "reciprocal", dict(out=rs_, in_=rs_), [buf("rs%d" % xi)], [buf("rs%d" % xi)])
        ts(xsb[xi], xt, rs_, None, ALU.mult, None, [xB, buf("rs%d" % xi)], [buf("xsb%d" % xi)])

    def S2(si, k):
        xi = k % 2
        t4 = k % 4
        xb16 = xsb[xi]
        for kc in range(8):
            bk = 2 + kc // 2
            dst = tpb[kc // 2][:, (kc % 2) * 512 + t4 * 128: (kc % 2) * 512 + (t4 + 1) * 128]
            P.op("pe", "transpose", dict(out=dst, in_=xb16[:, kc * 128:(kc + 1) * 128], identity=ident),
                 [buf("xsb%d" % xi), buf("ident")], [bankB[bk]])
        if t4 == 3:
            grp = k // 4
            for kc in range(8):
                bk = 2 + kc // 2
                src = tpb[kc // 2][:, (kc % 2) * 512:(kc % 2 + 1) * 512]
                dst = hnT[:, kc, grp * 512:(grp + 1) * 512]
                act(dst, src, ACTF.Copy, [bankB[bk], buf("gpre")], [buf("hnT")], scale=gpre[:, kc:kc + 1])

    def stageF(si, wslot0):
        mrgT = [qbuf[0], qbuf[1], ubuf[0], ubuf[1], kbuf[0][:, 0:T], kbuf[1][:, 0:T], ebuf[0][:, 0:T], ebuf[1][:, 0:T]]
        mB = [buf(nm) for nm in ("q0", "q1", "u0", "u1", "k0", "k1", "e0", "e1")]
        wslot = wslot0
        for mo in range(8):
            dma("sp", wbuf[wslot], wf_s[mo], [buf("wf_s")], [buf("w%d" % wslot)], "w%d" % wslot)
            wv = wbuf[wslot].rearrange("p (k c) -> p k c", k=32)
            wB = buf("w%d" % wslot)
            for tt_ in range(4):
                tsl = slice(tt_ * 512, (tt_ + 1) * 512)
                pbk = [0, 1, 2, 3] if (tt_ % 2 == 0) else [4, 5, 6, 7]
                ya, yb, ga, gb = [banks[i] for i in pbk]
                for kc in range(9):
                    mm(ya, wv[:, kc, :], abT[:, kc, tsl], kc == 0, kc == 8, [wB, buf("abT_c%d" % kc)], [bankB[pbk[0]]])
                for kc in range(7):
                    mm(yb, wv[:, 9 + kc, :], abT[:, 9 + kc, tsl], kc == 0, kc == 6, [wB, buf("abT_c%d" % (9 + kc))], [bankB[pbk[1]]])
                for kc in range(8):
                    mm(ga, wv[:, 16 + kc, :], hnT[:, kc, tsl], kc == 0, kc == 7, [wB, buf("hnT")], [bankB[pbk[2]]])
                for kc in range(8):
                    mm(gb, wv[:, 24 + kc, :], hnT[:, kc, tsl], kc == 0, kc == 7, [wB, buf("hnT")], [bankB[pbk[3]]])
                ta, tb_ = tmpf[0], tmpf[1]
                m1, m2 = tmpf[2], tmpf[3]
                act(ta, ga, ACTF.Tanh, [bankB[pbk[2]], buf("bgate")], [buf("tf0")], scale=0.5, bias=bgate[:, mo:mo + 1])
                act(tb_, gb, ACTF.Tanh, [bankB[pbk[3]], buf("bgate")], [buf("tf1")], scale=0.5, bias=bgate[:, 8 + mo:9 + mo])
                stt(m1, ta, 1.0, ya, ALU.add, ALU.mult, [buf("tf0"), bankB[pbk[0]]], [buf("tf2")])
                stt(m2, tb_, 1.0, yb, ALU.add, ALU.mult, [buf("tf1"), bankB[pbk[1]]], [buf("tf3")])
                tt(mrgT[mo][:, tsl], m1, m2, ALU.add, [buf("tf2"), buf("tf3")], [mB[mo]])
            wslot ^= 1
        return wslot, mrgT, mB

    xrb = [xr0, xr1]
    xrB = [[buf("xr0")], [buf("tf2"), buf("tf3")]]
    pb2s = [[0, 1], [6, 7]]

    def T0():
        for h in range(2):
            dma("sp", wbuf[h], wo_s[h], [buf("wo_s")], [buf("w%d" % h)], "w%d" % h)

    def T1(si, t16, mrgT, mB):
        wo = [wbuf[h].rearrange("p (k c) -> p k c", k=8) for h in range(2)]
        pb2 = pb2s[t16 % 2]
        yi = t16 % 2
        dma("pool", xrb[yi], x_d[si, t16 * 128:(t16 + 1) * 128, :], [], xrB[yi], "xr%d" % yi)
        for h in range(2):
            for kc in range(8):
                mm(banks[pb2[h]], mrgT[kc][:, t16 * 128:(t16 + 1) * 128], wo[h][:, kc, :], kc == 0, kc == 7,
                   [mB[kc], buf("w%d" % h)], [bankB[pb2[h]]])

    def T2(si, t16):
        pb2 = pb2s[t16 % 2]
        yi = t16 % 2
        ss2 = stat[:, 8 + 2 * yi: 10 + 2 * yi]
        for h in range(2):
            act(junk[:, 0:512], banks[pb2[h]], ACTF.Square, [bankB[pb2[h]]],
                [buf("ssF%d_%d" % (yi, h)), buf("junk")], accum_out=ss2[:, h:h + 1])
        rr = stat[:, 16 + yi:17 + yi]
        tt(rr, ss2[:, 0:1], ss2[:, 1:2], ALU.add, [buf("ssF%d_0" % yi), buf("ssF%d_1" % yi)], [buf("rr%d" % yi)])
        act(rr, rr, ACTF.Sqrt, [buf("rr%d" % yi), buf("statinit")], [buf("rr%d" % yi)], scale=0.25 / D, bias=epsc)
        P.op("dve", "reciprocal", dict(out=rr, in_=rr), [buf("rr%d" % yi)], [buf("rr%d" % yi)])

    def T3(si, t16):
        pb2 = pb2s[t16 % 2]
        yi = t16 % 2
        yt = yst[yi]
        xr = xrb[yi]
        rr = stat[:, 16 + yi:17 + yi]
        for h in range(2):
            stt(yt[:, h * 512:(h + 1) * 512], banks[pb2[h]], rr, gpost[:, h * 512:(h + 1) * 512], ALU.mult, ALU.mult,
                [bankB[pb2[h]], buf("rr%d" % yi), buf("gpost")], [buf("yst%d" % yi)])
        tt(yt, yt, xr, ALU.add, [buf("yst%d" % yi)] + xrB[yi], [buf("yst%d" % yi)])
        dma("sp", y_d[si, t16 * 128:(t16 + 1) * 128, :], yt, [buf("yst%d" % yi)], [buf("ydram")], "ys%d" % yi)

    def boundary(si, mrgT, mB, with_tail, with_s0):
        if with_tail:
            T0()
        for i in range(-1, 16):
            if i + 1 < 16:
                if with_tail:
                    T1(si, i + 1, mrgT, mB)
                if with_s0:
                    S1(si + 1, i + 1)
            if i >= 0:
                if with_tail:
                    T2(si, i)
                if with_s0:
                    S2(si + 1, i)
                if with_tail:
                    T3(si, i)

    def drain(gen):
        for _ in gen:
            pass

    def interleave(main, filler, nunits, nfill):
        fdone = filler is None
        ui = 0
        for _ in main:
            if not fdone:
                k = ((ui + 1) * nfill) // nunits - (ui * nfill) // nunits
                for _k in range(k):
                    try:
                        next(filler)
                    except StopIteration:
                        fdone = True
                        break
            ui += 1
        if not fdone:
            drain(filler)

    def roundrobin(g1, g2):
        d1 = d2 = False
        while not (d1 and d2):
            if not d1:
                try:
                    next(g1)
                except StopIteration:
                    d1 = True
            if not d2:
                try:
                    next(g2)
                except StopIteration:
                    d2 = True

    wslot = 0
    pending = [None]
    btg = btab_gen()
    bt_done = [False]

    def bt_step(k=1):
        for _ in range(k):
            if bt_done[0]:
                return
            try:
                next(btg)
            except StopIteration:
                bt_done[0] = True

    boundary(-1, None, None, False, True)
    for si in range(nseq):
        bs = 0
        load_chunk_consts(order[0], wslot)
        drain(inproj_gen(order[0], bs, wslot))
        for oi, n in enumerate(order):
            kind, d, g = chunk_geom(n)
            nxt = order[oi + 1] if oi + 1 < len(order) else None
            eslot = wslot
            filler = None
            if si == 0:
                if oi + 2 < len(order):
                    emit_casts(oi + 2, oi + 3, [buf("v%d" % bs)])
                if oi == 8:
                    dma("pool", wf_s, wf_d, [buf("v%d" % bs)], [buf("wf_s")], "pp5")
                    dma("pool", wo_s, wo_d, [buf("v%d" % bs)], [buf("wo_s")], "pp6")
            if n == NCH_A and not bt_done[0]:
                bt_step(1000)
            if nxt is not None:
                load_chunk_consts(nxt, wslot ^ 1)
                filler = inproj_gen(nxt, bs ^ 1, wslot ^ 1)
            if kind == "A01":
                main = att_A01(n, bs, eslot)
                nunits = (17 if d == 1 else 20)
            elif kind == "A2":
                main = att_A2(n, bs, eslot)
                nunits = 8
            else:
                main = att_B(n, bs, eslot)
                nunits = 32
            nfill = 24 + (0 if nxt is None else sum(2 if len(g_) > 2 else 1 for g_ in vgroups(nxt)))
            if si == 0 and not bt_done[0]:
                def main_bt(m):
                    for _ in m:
                        bt_step(1)
                        yield
                main = main_bt(main)
            if pending[0] is not None:
                def main_pp(m, pg):
                    for _ in m:
                        try:
                            next(pg)
                        except StopIteration:
                            pass
                        yield
                    for _ in pg:
                        pass
                main = main_pp(main, pending[0])
                pending[0] = None
            interleave(main, filler, nunits, nfill)
            if n < NCH_A and n // 3 == 2:
                pending[0] = post_A_gen(n % 3)
            bs ^= 1
            wslot ^= 1
        wslot, mrgT_, mB_ = stageF(si, wslot)
        boundary(si, mrgT_, mB_, True, si + 1 < nseq)

    P.emit()
    return nc


_ALIAS = {}


def _prep_weights(w_in, w_proj_a, w_proj_b, w_out):
    w_in = np.asarray(w_in, np.float32)[0]
    cols = {}
    offs = [0, WA, 2 * WA, 3 * WA, 4 * WA, 4 * WA + WB, 4 * WA + 2 * WB, 4 * WA + 3 * WB, 4 * WA + 4 * WB,
            4 * WA + 4 * WB + D]
    qa, ka, va, za, qb, kb, vb, zb, ga, gb = offs

    def tile_fm(c0):
        return w_in[:, c0:c0 + 128].reshape(8, 128, 128).transpose(1, 0, 2)

    wch = np.zeros((NCH, 128, 4, 8, 128), np.float32)
    for n in range(NCH):
        if n < NCH_A:
            c = n * 128
            mats = (qa + c, ka + c, za + c, va + c)
        else:
            c = (n - NCH_A) * 128
            mats = (qb + c, kb + c, zb + c, vb + c)
        for mi, c0 in enumerate(mats):
            wch[n, :, mi] = tile_fm(c0)
    wch = wch.reshape(NCH, 128, 4096)
    wa = np.asarray(w_proj_a, np.float32)[0]
    wb = np.asarray(w_proj_b, np.float32)[0]
    wf = np.zeros((8, 128, 32, 128), np.float32)
    for mo in range(8):
        ms = slice(mo * 128, (mo + 1) * 128)
        wf[mo, :, 0:9] = wa[:, ms].reshape(9, 128, 128).transpose(1, 0, 2)
        wf[mo, :, 9:16] = wb[:, ms].reshape(7, 128, 128).transpose(1, 0, 2)
        wf[mo, :, 16:24] = tile_fm(ga + mo * 128)
        wf[mo, :, 24:32] = tile_fm(gb + mo * 128)
    wf = wf.reshape(8, 128, 4096)
    wo_ = np.asarray(w_out, np.float32)[0]
    wo = np.zeros((2, 128, 8, 512), np.float32)
    for h in range(2):
        wo[h] = wo_[:, h * 512:(h + 1) * 512].reshape(8, 128, 512).transpose(1, 0, 2)
    wo = wo.reshape(2, 128, 4096)
    return np.ascontiguousarray(wch), np.ascontiguousarray(wf), np.ascontiguousarray(wo)


_NC_CACHE = {}


def run_layer(x_cores, norm_pre, w_in, b_gate, rpb, w_proj_a, w_proj_b, w_out, norm_post, core_ids=None):
    nseq = x_cores[0].shape[0]
    if nseq not in _NC_CACHE:
        _NC_CACHE[nseq] = build(nseq)
    nc = _NC_CACHE[nseq]
    wch, wf, wo = _prep_weights(w_in, w_proj_a, w_proj_b, w_out)
    eta = _alibi_tables()
    bbias, bmask, _ = _b_tables(np.asarray(rpb, np.float32)[0])
    gpre = np.ascontiguousarray(np.asarray(norm_pre, np.float32)[0].reshape(8, 128).T)
    bg = np.asarray(b_gate, np.float32)[0]
    bgl = np.ascontiguousarray(bg.reshape(2, 8, 128).transpose(2, 0, 1).reshape(128, 16))
    gpost = np.ascontiguousarray(np.broadcast_to(np.asarray(norm_post, np.float32)[0][None, :], (128, D)))
    ident = np.eye(128, dtype=np.float32)
    common = {"wch": wch, "wf": wf, "wo": wo, "eta": eta, "bbias": bbias, "bmask": bmask, "gpre": gpre,
              "bgate": bgl, "gpost": gpost, "ident": ident}
    in_maps = []
    for xc in x_cores:
        m = dict(common)
        m["x"] = np.ascontiguousarray(xc, dtype=np.float32)
        in_maps.append(m)
    if core_ids is None:
        core_ids = list(range(len(x_cores)))
    res = run_bass_kernel_spmd(nc, in_maps, core_ids=core_ids)
    return [r["y"] for r in res.results]


def kernel(x_prompt, x_sample, norm_pre, w_in, b_gate, rpb, w_proj_a, w_proj_b, w_out, norm_post):
    xp = np.asarray(x_prompt, np.float32)
    xs = np.asarray(x_sample, np.float32)
    x_cores = []
    for c in range(NCORES):
        x_cores.append(np.concatenate([xp[2 * c:2 * c + 2], xs[4 * c:4 * c + 4]], axis=0))
    ys = run_layer(x_cores, norm_pre, w_in, b_gate, rpb, w_proj_a, w_proj_b, w_out, norm_post)
    yp = np.concatenate([y[0:2] for y in ys], axis=0)
    ysm = np.concatenate([y[2:6] for y in ys], axis=0)
    return (yp.astype(np.float32), ysm.astype(np.float32))
```

```python
import numpy as np
import concourse.bass as bass
import concourse.mybir as mybir
from concourse.bass_utils import run_bass_kernel_spmd

F32 = mybir.dt.float32
BF16 = mybir.dt.bfloat16
ALU = mybir.AluOpType
ACTF = mybir.ActivationFunctionType

T = 2048
D = 1024
NCORES = 8
HEAD = 64
WA = 1152
WB = 896
NCH_A = 9
NCH_B = 7
NCH = 16
EPS = 1e-6
DIL = (1, 4, 16)
TBL_IDX = {("e", 3): 0, ("e", 2): 1, ("i", 2): 2, ("i", 1): 3, ("i", 0): 4, ("i", -1): 5, ("i", -2): 6,
           ("e", -2): 7, ("e", -3): 8}


class Buf:
    __slots__ = ("name", "last_w", "readers")

    def __init__(self, name=""):
        self.name = name
        self.last_w = None
        self.readers = []


class Op:
    __slots__ = ("eng", "fn", "deps", "flag", "val", "sem", "is_dma")

    def __init__(self, eng, fn, is_dma=False):
        self.eng = eng
        self.fn = fn
        self.deps = []
        self.flag = False
        self.val = None
        self.sem = None
        self.is_dma = is_dma


class Prog:
    ENGS = ("pe", "act", "dve", "pool", "sp")

    def __init__(self, nc):
        self.nc = nc
        self.ops = {e: [] for e in self.ENGS}
        self.all_ops = []

    def _add_dep(self, op, prod):
        if prod is None or prod is op:
            return
        if prod.eng == "pe" and op.eng == "pe" and not prod.is_dma and not op.is_dma:
            return
        if prod not in op.deps:
            op.deps.append(prod)
            prod.flag = True

    def op(self, eng, meth, kw, reads=(), writes=(), dma_key=None):
        fn = (lambda e, meth=meth, kw=kw: getattr(e, meth)(**kw))
        reads = [b for x in reads for b in (x if isinstance(x, (list, tuple)) else [x])]
        writes = [b for x in writes for b in (x if isinstance(x, (list, tuple)) else [x])]
        o = Op(eng, fn, is_dma=dma_key is not None)
        if dma_key is not None:
            o.sem = dma_key
            o.flag = True
        for b in reads:
            self._add_dep(o, b.last_w)
        for b in writes:
            self._add_dep(o, b.last_w)
            for r in b.readers:
                self._add_dep(o, r)
        for b in reads:
            b.readers.append(o)
        for b in writes:
            b.last_w = o
            b.readers = []
        self.ops[eng].append(o)
        self.all_ops.append(o)
        return o

    def emit(self):
        nc = self.nc
        eng_sem = {e: nc.alloc_semaphore("es_" + e) for e in self.ENGS}
        keys = []
        for e in self.ENGS:
            for o in self.ops[e]:
                if o.is_dma and o.sem not in keys:
                    keys.append(o.sem)
        dsem = {k: [nc.alloc_semaphore("ds_%d" % i), 0] for i, k in enumerate(keys)}
        for e in self.ENGS:
            cnt = 0
            for o in self.ops[e]:
                if o.is_dma:
                    d = dsem[o.sem]
                    d[1] += 16
                    o.val = d[1]
                    o.sem = d[0]
                elif o.flag:
                    cnt += 1
                    o.val = cnt
                    o.sem = eng_sem[e]
        all_dma_final = [(d[0], d[1]) for d in dsem.values()]
        known_e = {e: {} for e in self.ENGS}
        kn_of = {}
        waits_of = {}
        for o in self.all_ops:
            kn = known_e[o.eng]
            ws = []
            for p in o.deps:
                sid = p.sem.num
                if kn.get(sid, 0) < p.val:
                    ws.append((p.sem, p.val))
                    kn[sid] = p.val
                pk = kn_of.get(id(p))
                if pk is not None:
                    for k_, v_ in pk.items():
                        if kn.get(k_, 0) < v_:
                            kn[k_] = v_
            waits_of[id(o)] = ws
            if o.flag:
                snap = dict(kn)
                snap[o.sem.num] = max(snap.get(o.sem.num, 0), o.val)
                kn_of[id(o)] = snap
        with nc.Block() as block:
            def run(e):
                def body(eng):
                    known = known_e[e]
                    for o in self.ops[e]:
                        for (ws_, wv_) in waits_of[id(o)]:
                            eng.wait_ge(ws_, wv_)
                        ins = o.fn(eng)
                        if o.is_dma:
                            ins.then_inc(o.sem, 16)
                        elif o.flag:
                            ins.then_inc(o.sem, 1)
                    if e == "sp":
                        for (s, v) in all_dma_final:
                            if known.get(s.num, 0) < v:
                                eng.wait_ge(s, v)
                return body
            block.tensor(run("pe"))
            block.scalar(run("act"))
            block.vector(run("dve"))
            block.gpsimd(run("pool"))
            block.sync(run("sp"))


NEG = -240000.0


def _alibi_tables():
    slopes = (2.0 ** (-8.0 * np.arange(1, 19) / 18)).astype(np.float64)
    out = np.full((NCH_A, 128, 1024), NEG, np.float32)
    j = np.arange(128)[:, None].astype(np.float64)
    i = np.arange(128)[None, :].astype(np.float64)
    for c in range(NCH_A):
        g, s = c // 3, c % 3
        d = DIL[g]
        for hh in range(2):
            sl = float(slopes[g + 3 * (2 * s + hh)])
            if g < 2:
                bh = np.where(j <= i, -8.0 * sl * d * np.abs(i - j - 64.0), NEG)
                ah = np.where(j >= i, -8.0 * sl * d * np.abs(i - j + 64.0), NEG)
                main = np.concatenate([bh, ah], 1)
                bh2 = bh.copy(); bh2[64:, :] = NEG
                ah2 = ah.copy(); ah2[:64, :] = NEG
                bnd = np.concatenate([bh2, ah2], 1)
                out[c, :, hh * 256:(hh + 1) * 256] = main
                out[c, :, 512 + hh * 256:512 + (hh + 1) * 256] = bnd
            else:
                e2 = np.where(np.abs(i - j) <= 64, -8.0 * sl * d * np.abs(i - j), NEG)
                out[c, :, hh * 128:(hh + 1) * 128] = e2
    return np.maximum(out, NEG).astype(np.float32)


def _b_pairs():
    rows = 32
    rs = np.clip(np.arange(rows) - 4, 0, rows - 8)
    pairs = {}
    for u in range(16):
        for v in range(16):
            pat = np.zeros((2, 2), bool)
            for rl_k in range(2):
                for rl_q in range(2):
                    r = 2 * v + rl_q
                    rp = 2 * u + rl_k
                    pat[rl_k, rl_q] = (rs[r] <= rp < rs[r] + 8)
            if pat.any():
                pairs[(u, v)] = pat
    return pairs


def _b_tables(rpb):
    pairs = _b_pairs()
    tbl_pat = {}
    tbl_delta = {}
    pair_tbl = {}
    for (u, v), pat in pairs.items():
        dl = u - v
        ipat = np.zeros((2, 2), bool)
        for a in range(2):
            for b in range(2):
                dr = 2 * dl + a - b
                ipat[a, b] = (-4 <= dr <= 3)
        if abs(dl) <= 2 and (pat == ipat).all():
            key = ("i", dl)
        else:
            assert pat.all(), (u, v, pat)
            key = ("e", dl)
        assert key in TBL_IDX, key
        if key in tbl_pat:
            assert (tbl_pat[key] == pat).all()
        tbl_pat[key] = pat
        tbl_delta[key] = dl
        pair_tbl[(u, v)] = TBL_IDX[key]
    assert len(tbl_pat) == 9
    cp = np.arange(64)[:, None]
    cq = np.arange(64)[None, :]
    cs = np.clip(cq - 8, 0, 48)
    colvalid = (cp >= cs) & (cp < cs + 16)
    dc = np.clip(cp - cq, -15, 15) + 15
    bias = np.zeros((14, 9, 128, 128), np.float32)
    mask = np.zeros((9, 128, 128), np.float32)
    for key, idx in TBL_IDX.items():
        pat = tbl_pat[key]
        dl = tbl_delta[key]
        for a in range(2):
            for b in range(2):
                if not pat[a, b]:
                    continue
                dr = 2 * dl + a - b + 7
                assert 0 <= dr <= 14
                mask[idx, a * 64:(a + 1) * 64, b * 64:(b + 1) * 64] = colvalid
                bias[:, idx, a * 64:(a + 1) * 64, b * 64:(b + 1) * 64] = rpb[:, dr][:, dc]
    bias = bias.reshape(7, 2, 9, 128, 128).transpose(0, 3, 1, 2, 4).reshape(7, 128, 2304)
    mask = (mask - 1.0) * (-NEG)
    maskf = np.broadcast_to(mask[None, None], (7, 2, 9, 128, 128)).transpose(0, 3, 1, 2, 4).reshape(7, 128, 2304)
    return np.ascontiguousarray(bias), np.ascontiguousarray(maskf), pair_tbl


def build(nseq, nchunks=NCH):
    nc = bass.Bass("TRN2", target_bir_lowering=False)
    P = Prog(nc)
    pair_tbl = _b_tables(np.zeros((14, 15, 31), np.float32))[2]
    pairs = sorted(pair_tbl.keys())
    v_of_u = {u: [v for (uu, v) in pairs if uu == u] for u in range(16)}
    for u in range(16):
        assert v_of_u[u] == list(range(v_of_u[u][0], v_of_u[u][-1] + 1))
    first_u = {v: min(u for (u, vv) in pairs if vv == v) for v in range(16)}
    last_u = {v: max(u for (u, vv) in pairs if vv == v) for v in range(16)}

    def din(name, shape, dt=F32):
        return nc.dram_tensor(name, list(shape), dt, kind="ExternalInput").ap()

    x_d = din("x", [nseq, T, D])
    y_d = nc.dram_tensor("y", [nseq, T, D], F32, kind="ExternalOutput").ap()
    wch_d = din("wch", [NCH, 128, 4096])
    wf_d = din("wf", [8, 128, 4096])
    wo_d = din("wo", [2, 128, 4096])
    eta_d = din("eta", [NCH_A, 128, 1024])
    bb_d = din("bbias", [NCH_B, 128, 2304])
    bm_d = din("bmask", [NCH_B, 128, 2304])
    gpre_d = din("gpre", [128, 8])
    bg_d = din("bgate", [128, 16])
    gpost_d = din("gpost", [128, D])
    id_d = din("ident", [128, 128])
    wch_s = nc.dram_tensor("wch_s", [NCH, 128, 4096], BF16, kind="Internal").ap()
    wf_s = nc.dram_tensor("wf_s", [8, 128, 4096], BF16, kind="Internal").ap()
    wo_s = nc.dram_tensor("wo_s", [2, 128, 4096], BF16, kind="Internal").ap()
    eta_s = nc.dram_tensor("eta_s", [NCH_A, 128, 1024], BF16, kind="Internal").ap()
    etb_s = nc.dram_tensor("etb_s", [NCH_B, 128, 2304], BF16, kind="Internal").ap()

    def sb(name, shape, dt):
        return nc.alloc_sbuf_tensor(name, list(shape), dt).ap()

    hnT = sb("hnT", [128, 8, T], BF16)
    abT = sb("abT", [128, NCH, T], BF16)
    qbuf = [sb("q%d" % i, [128, T], BF16) for i in range(2)]
    kbuf = [sb("k%d" % i, [128, 2560], BF16) for i in range(2)]
    vbuf = [sb("v%d" % i, [128, 20, 192], BF16) for i in range(2)]
    ubuf = [sb("u%d" % i, [128, T], BF16) for i in range(2)]
    scr = sb("scr", [128, 5120], F32)
    wbuf = [sb("w%d" % i, [128, 4096], BF16) for i in range(2)]
    ebuf = [sb("e%d" % i, [128, 2304], BF16) for i in range(2)]
    xr0 = sb("xr0", [128, 1024], F32)
    pbuf = [sb("p%d" % i, [128, 1024], BF16) for i in range(2)]
    tmpf_all = sb("tfall", [128, 2048], F32)
    tmpf = [tmpf_all[:, i * 512:(i + 1) * 512] for i in range(4)]
    xr1 = tmpf_all[:, 1024:2048]
    ident = sb("identb", [128, 128], BF16)
    identf = sb("identf", [128, 128], F32)
    gpre = sb("gpre_s", [128, 8], F32)
    bgate = sb("bgate_s", [128, 16], F32)
    gpost = sb("gpost_s", [128, D], F32)
    stat = sb("stat", [128, 64], F32)
    junk = sb("junk", [128, D], BF16)

    scr_bf = scr.bitcast(BF16)
    numS = [scr_bf[:, g * T:(g + 1) * T] for g in range(3)]
    zsum = scr[:, 3072:5120]
    xst = [scr[:, 0:1024], scr[:, 1024:2048]]
    xsb = [scr_bf[:, 4096:5120], scr_bf[:, 5120:6144]]
    yst = [scr[:, 3072:4096], scr[:, 4096:5120]]

    ps_all = nc.alloc_psum_tensor("ps_all", [128, 4096], F32).ap()
    banks = [ps_all[:, i * 512:(i + 1) * 512] for i in range(8)]
    bankB = [Buf("bank%d" % i) for i in range(8)]

    B = {}
    RQ = {nm: [Buf("%s_q%d" % (nm, q)) for q in range(4)] for nm in ("numS0", "numS1", "numS2", "zsum")}
    ALIAS = {"xst0": RQ["numS0"], "xst1": RQ["numS1"], "xsb0": RQ["numS2"][0:2], "xsb1": RQ["numS2"][2:4],
             "yst0": RQ["zsum"][0:2], "yst1": RQ["zsum"][2:4]}
    PS_ = [[Buf("p%d_%d" % (i, j)) for j in range(2)] for i in range(2)]
    for i in range(2):
        ALIAS["p%d" % i] = PS_[i]
        for j in range(2):
            ALIAS["p%d_%d" % (i, j)] = [PS_[i][j]]
    for nm in ("numS0", "numS1", "numS2", "zsum"):
        ALIAS[nm] = RQ[nm]
        for q in range(4):
            ALIAS["%s_q%d" % (nm, q)] = [RQ[nm][q]]

    def buf(name):
        if name in ALIAS:
            return ALIAS[name]
        if name not in B:
            B[name] = Buf(name)
        return B[name]

    def mm(out, lhsT, rhs, start, stop, reads, writes, skip=False):
        kw = dict(out=out, lhsT=lhsT, rhs=rhs, start=start, stop=stop)
        if skip:
            kw["skip_group_check"] = True
        P.op("pe", "matmul", kw, reads, writes)

    def act(out, in_, func, reads, writes, **kw):
        P.op("act", "activation", dict(out=out, in_=in_, func=func, **kw), reads, writes)

    def tt(out, in0, in1, op, reads, writes, eng="dve"):
        P.op(eng, "tensor_tensor", dict(out=out, in0=in0, in1=in1, op=op), reads, writes)

    def stt(out, in0, scalar, in1, op0, op1, reads, writes, eng="dve"):
        P.op(eng, "scalar_tensor_tensor", dict(out=out, in0=in0, scalar=scalar, in1=in1, op0=op0, op1=op1), reads, writes)

    def ts(out, in0, s1, s2, op0, op1, reads, writes, eng="dve"):
        kw = dict(out=out, in0=in0, scalar1=s1, scalar2=s2, op0=op0)
        if op1 is not None:
            kw["op1"] = op1
        P.op(eng, "tensor_scalar", kw, reads, writes)

    def cp(out, in_, reads, writes, eng="dve"):
        P.op(eng, "tensor_copy", dict(out=out, in_=in_), reads, writes)

    def dma(eng, out, in_, reads, writes, key):
        P.op(eng, "dma_start", dict(out=out, in_=in_), reads, writes, dma_key=key)

    cast_order = [0, 3, 6, 1, 4, 7, 2, 5, 8] + list(range(NCH_A, NCH))

    def emit_casts(lo, hi, gate):
        for n_ in cast_order[lo:hi]:
            if n_ < NCH_A:
                dma("pool", eta_s[n_], eta_d[n_], gate, [buf("eta_c%d" % n_)], "ppe%d" % n_)
            dma("pool", wch_s[n_], wch_d[n_], gate, [buf("wch_c%d" % n_)], "ppc%d" % n_)

    emit_casts(0, 2, [])
    dma("sp", identf, id_d, [], [buf("identf")], "c0")
    dma("sp", gpre, gpre_d, [], [buf("gpre")], "c1")
    dma("sp", bgate, bg_d, [], [buf("bgate")], "c2")
    dma("sp", gpost, gpost_d, [], [buf("gpost")], "c3")
    cp(ident, identf, [buf("identf")], [buf("ident")])
    ts(bgate, bgate, 0.5, None, ALU.mult, None, [buf("bgate")], [buf("bgate")])
    for i in range(2):
        P.op("dve", "memset", dict(ap=kbuf[i], constant=0.0), [], [buf("k%d" % i)])
        P.op("dve", "memset", dict(ap=vbuf[i], constant=0.0), [], [buf("v%d" % i)])
        P.op("dve", "memset", dict(ap=vbuf[i][:, :, 64:128], constant=1.0), [], [buf("v%d" % i)])
        P.op("dve", "memset", dict(ap=qbuf[i], constant=0.0), [], [buf("q%d" % i)])
    epsc = stat[:, 40:41]
    P.op("dve", "memset", dict(ap=epsc, constant=EPS), [], [buf("statinit")])
    ts(gpost, gpost, 0.5, None, ALU.mult, None, [buf("gpost")], [buf("gpost")])
    abhi = abT[:, 9:16, :].rearrange("p c t -> p (c t)")
    abhi_f = abhi.bitcast(F32)
    bt_stage = []
    for i in range(2):
        o = i * 2880
        bt_stage.append((abhi_f[:, o:o + 1152], abhi_f[:, o + 1152:o + 2304], abhi[:, 2 * (o + 2304): 2 * (o + 2304) + 1152]))

    def btab_gen():
        items = [(c, h2) for c in range(NCH_B) for h2 in range(2)]

        def loads(k):
            c, h2 = items[k]
            sl = slice(h2 * 1152, (h2 + 1) * 1152)
            stg, stg2, _ = bt_stage[k % 2]
            dma("sp", stg, bb_d[c, :, sl], [], [buf("btA%d" % (k % 2))], "c4%d" % (k % 2))
            dma("sp", stg2, bm_d[c, :, sl], [], [buf("btB%d" % (k % 2))], "c5%d" % (k % 2))

        loads(0)
        for k in range(len(items)):
            c, h2 = items[k]
            sl = slice(h2 * 1152, (h2 + 1) * 1152)
            if k + 1 < len(items):
                loads(k + 1)
            yield
            stg, stg2, so = bt_stage[k % 2]
            stt(so, stg, 8.0, stg2, ALU.mult, ALU.add, [buf("btA%d" % (k % 2)), buf("btB%d" % (k % 2))], [buf("btO%d" % (k % 2))])
            dma("sp", etb_s[c, :, sl], so, [buf("btO%d" % (k % 2))], [buf("etb_s%d" % (k % 2))], "c6%d" % (k % 2))
            yield

    state = {"ipb": 0, "x": 0}

    def load_chunk_consts(n, slot):
        dma("sp", wbuf[slot], wch_s[n], [buf("wch_c%d" % n)], [buf("w%d" % slot)], "w%d" % slot)
        if n < NCH_A:
            dma("sp", ebuf[slot][:, 0:1024], eta_s[n], [buf("eta_c%d" % n)], [buf("e%d" % slot)], "e%d" % slot)
        else:
            dma("sp", ebuf[slot], etb_s[n - NCH_A], [buf("etb_s0"), buf("etb_s1")], [buf("e%d" % slot)], "e%d" % slot)

    def chunk_geom(n):
        if n < NCH_A:
            g = n // 3
            return ("A2" if g == 2 else "A01"), DIL[g], g
        return "B", 1, 3

    order = []
    for s in range(3):
        for g in range(3):
            order.append(3 * g + s)
    order += list(range(NCH_A, NCH))
    order = order[:nchunks] if nchunks < NCH else order

    def vslots(n):
        kind, d, g = chunk_geom(n)
        res = []
        if kind == "A01":
            L = T // d
            nq = L // 128
            for r in range(d):
                for kb in range(nq + 1):
                    m0 = 128 * kb - 64
                    lo = max(m0, 0)
                    hi = min(m0 + 128, L)
                    res.append((r * (nq + 1) + kb, lo * d + r, d, hi - lo, lo - m0))
        elif kind == "A2":
            for r in range(16):
                res.append((r, r, 16, 128, 0))
        else:
            for u in range(16):
                res.append((u, 128 * u, 1, 128, 0))
        return res

    def vgroups(n):
        sl = vslots(n)
        res = []
        i = 0
        while i < len(sl):
            grp = [sl[i]]
            if sl[i][3] == 128:
                while len(grp) < 4 and i + len(grp) < len(sl) and sl[i + len(grp)][3] == 128 \
                        and sl[i + len(grp)][0] == grp[-1][0] + 1:
                    grp.append(sl[i + len(grp)])
            i += len(grp)
            res.append(grp)
        return res

    def inproj_gen(n, bs, wslot):
        kind, d, g = chunk_geom(n)
        w4 = wbuf[wslot].rearrange("p (m k c) -> p m k c", m=4, k=8)
        wB = buf("w%d" % wslot)
        hB = buf("hnT")
        qB, kB, vB, uB = buf("q%d" % bs), buf("k%d" % bs), buf("v%d" % bs), buf("u%d" % bs)
        qb_, kb_, vb_, ub_ = qbuf[bs], kbuf[bs], vbuf[bs], ubuf[bs]

        def acc_fm(mi, tt_):
            bi = state["ipb"]
            state["ipb"] ^= 1
            ps = banks[bi]
            for kc in range(8):
                mm(ps, w4[:, mi, kc, :], hnT[:, kc, tt_ * 512:(tt_ + 1) * 512], kc == 0, kc == 7, [wB, hB], [bankB[bi]])
                if kc == 3:
                    yield None
            yield (bi, ps)

        for tt_ in range(4):
            for r_ in acc_fm(2, tt_):
                if r_ is None:
                    yield
                else:
                    bi, ps = r_
            tf = tmpf[tt_ % 2]
            tB = buf("tf%d" % (tt_ % 2))
            act(tf, ps, ACTF.Tanh, [bankB[bi]], [tB], scale=0.5)
            stt(ub_[:, tt_ * 512:(tt_ + 1) * 512], tf, 1.0, ps, ALU.add, ALU.mult, [tB, bankB[bi]], [uB])
            yield
        for mi, dstB in ((0, qB), (1, kB)):
            for tt_ in range(4):
                for r_ in acc_fm(mi, tt_):
                    if r_ is None:
                        yield
                    else:
                        bi, ps = r_
                if (kind == "A01" and d == 1) or kind == "B":
                    off = 64 if (mi == 1 and kind == "A01") else 0
                    dst = (kb_ if mi == 1 else qb_)[:, off + tt_ * 512: off + (tt_ + 1) * 512]
                    src = ps
                elif kind == "A01":
                    if mi == 1:
                        dst = kb_.rearrange("p (r m) -> p r m", r=4)[:, :, 64 + 128 * tt_: 64 + 128 * tt_ + 128]
                    else:
                        dst = qb_.rearrange("p (r m) -> p r m", r=4)[:, :, 128 * tt_:128 * tt_ + 128]
                    src = ps.rearrange("p (j r) -> p r j", r=4)
                else:
                    base = (kb_[:, 0:T] if mi == 1 else qb_)
                    dst = base.rearrange("p (r m) -> p r m", r=16)[:, :, 32 * tt_:32 * tt_ + 32]
                    src = ps.rearrange("p (j r) -> p r j", r=16)
                act(dst, src, ACTF.Copy, [bankB[bi]], [dstB])
                yield
        for grp in vgroups(n):
            bi = state["ipb"]
            state["ipb"] ^= 1
            ps = banks[bi]
            for q, (slot, t0, step, nv, row0) in enumerate(grp):
                for kc in range(8):
                    lhs = hnT[:, kc, t0: t0 + (nv - 1) * step + 1: step]
                    mm(ps[0:nv, q * 128:(q + 1) * 128], lhs, w4[:, 3, kc, :], kc == 0, kc == 7, [wB, hB], [bankB[bi]])
                if q == 1 and len(grp) > 2:
                    yield
            slot0, _, _, nv, row0 = grp[0]
            ng = len(grp)
            if nv == 128:
                dst = bass.AP(vb_.tensor, vb_.offset + slot0 * 192, [list(vb_.ap[0]), [192, ng], [128, 2], [1, 64]])
                src = ps[:, 0:ng * 128].rearrange("p (s h c) -> p s h c", s=ng, h=2)
            else:
                dst = bass.AP(vb_.tensor, vb_[row0:row0 + 64, slot0, :].offset, [[vb_.ap[0][0], 64], [128, 2], [1, 64]])
                src = ps[0:64, 0:128].rearrange("p (h c) -> p h c", h=2)
            act(dst, src, ACTF.Copy, [bankB[bi]], [vB])
            yield

    def att_A01(n, bs, eslot):
        kind, d, g = chunk_geom(n)
        L = T // d
        nq = L // 128
        qb_, kb_, vb_, ub_ = qbuf[bs], kbuf[bs], vbuf[bs], ubuf[bs]
        qB, kB, vB, uB = buf("q%d" % bs), buf("k%d" % bs), buf("v%d" % bs), buf("u%d" % bs)
        eB = buf("e%d" % eslot)
        et = ebuf[eslot]
        units = [(r, kb) for r in range(d) for kb in range(nq + 1)]
        sring = [(2, 3), (4, 5)]
        tbank = {0: [6], 1: [7]}

        def scores(ui):
            r, kb = units[ui]
            sp_ = sring[ui % 2]
            qbase = r * L
            kbase = r * (L + 128) + 128 * kb
            if kb == 0:
                q0, nqc, o0 = qbase, 128, 128
            elif kb == nq:
                q0, nqc, o0 = qbase + 128 * (nq - 1), 128, 0
            else:
                q0, nqc, o0 = qbase + 128 * (kb - 1), 256, 0
            tsel = 512 if (kb == 0 or kb == nq) else 0
            for hh in range(2):
                mm(banks[sp_[hh]][:, 0:256], ident, et[:, tsel + hh * 256: tsel + hh * 256 + 256], True, False,
                   [buf("ident"), eB], [bankB[sp_[hh]]], skip=True)
            for hh in range(2):
                rows = slice(hh * 64, (hh + 1) * 64)
                mm(banks[sp_[hh]][:, o0: o0 + nqc], kb_[rows, kbase:kbase + 128], qb_[rows, q0:q0 + nqc],
                   False, True, [kB, qB], [bankB[sp_[hh]]], skip=True)

        def pslot(ui):
            k4 = ui % 4
            return pbuf[k4 // 2][:, (k4 % 2) * 512:(k4 % 2 + 1) * 512], buf("p%d_%d" % (k4 // 2, k4 % 2))

        def pv_first(ui):
            r, kb = units[ui]
            if kb > nq - 1:
                return
            pb, pB = pslot(ui)
            slot = r * (nq + 1) + kb
            for hh in range(2):
                lhs = vb_[:, slot, hh * 64: hh * 64 + 128]
                qb = kb
                tb = tbank[hh][0]
                mm(banks[tb][:, (qb % 4) * 128:(qb % 4 + 1) * 128], lhs, pb[:, hh * 256 + 128: hh * 256 + 256],
                   (qb % 4 == 0), False, [vB, pB], [bankB[tb]], skip=True)

        def front(ui):
            r, kb = units[ui]
            scores(ui)
            sp_ = sring[ui % 2]
            S2 = ps_all[:, sp_[0] * 512:(sp_[0] + 2) * 512].rearrange("p (h c) -> p h c", h=2)[:, :, 0:256]
            pb, pB = pslot(ui)
            act(pb.rearrange("p (h c) -> p h c", h=2), S2, ACTF.Exp, [bankB[sp_[0]], bankB[sp_[1]]], [pB], scale=0.125)

        front(0)
        for ui in range(len(units)):
            r, kb = units[ui]
            if ui + 1 < len(units):
                front(ui + 1)
            if ui >= 1:
                pv_first(ui - 1)
            pb, pB = pslot(ui)
            slot = r * (nq + 1) + kb
            grp_off = r * (nq // 4)
            for hh in range(2):
                lhs = vb_[:, slot, hh * 64: hh * 64 + 128]
                if kb >= 1:
                    qb = kb - 1
                    tb = tbank[hh][0]
                    mm(banks[tb][:, (qb % 4) * 128:(qb % 4 + 1) * 128], lhs, pb[:, hh * 256: hh * 256 + 128],
                       False, True, [vB, pB], [bankB[tb]], skip=True)
            if kb >= 4 and kb % 4 == 0:
                w = (kb - 1) // 4
                if d == 1:
                    tsl = slice(512 * w, 512 * w + 512)
                    uap, nap, zap = ub_[:, tsl], numS[g][:, tsl], zsum[:, tsl]
                    nB_, zB_ = buf("numS%d_q%d" % (g, w)), buf("zsum_q%d" % w)
                else:
                    uap, nap, zap = ub_[:, r:T:4], numS[g][:, r * 512:(r + 1) * 512], zsum[:, r:T:4]
                    nB_, zB_ = buf("numS%d" % g), buf("zsum")
                for hh in range(2):
                    tb = tbank[hh][0]
                    Tt = banks[tb]
                    nrows = slice(hh * 64, (hh + 1) * 64)
                    zrows = slice((1 - hh) * 64, (2 - hh) * 64)
                    tt(nap[nrows], Tt[nrows], uap[nrows], ALU.mult, [bankB[tb], uB], [nB_])
                    if g == 0:
                        ts(zap[nrows], Tt[zrows], 2.0, None, ALU.mult, None, [bankB[tb]], [zB_])
                    else:
                        stt(zap[nrows], Tt[zrows], 2.0, zap[nrows], ALU.mult, ALU.add, [bankB[tb], zB_], [zB_])
            if ui == len(units) - 1:
                pv_first(ui)
            yield

    def att_A2(n, bs, eslot):
        qb_, kb_, vb_, ub_ = qbuf[bs], kbuf[bs], vbuf[bs], ubuf[bs]
        qB, kB, vB, uB = buf("q%d" % bs), buf("k%d" % bs), buf("v%d" % bs), buf("u%d" % bs)
        eB = buf("e%d" % eslot)
        et = ebuf[eslot]
        units = [(r0, hh) for r0 in range(0, 16, 4) for hh in range(2)]
        sring = [2, 3]
        tbank = {0: [4, 6], 1: [5, 7]}

        def scores(ui):
            r0, hh = units[ui]
            si = sring[ui % 2]
            S = banks[si]
            rows = slice(hh * 64, (hh + 1) * 64)
            for q in range(4):
                mm(S[:, q * 128:(q + 1) * 128], ident, et[:, hh * 128:(hh + 1) * 128], q == 0, False,
                   [buf("ident"), eB], [bankB[si]], skip=True)
            for q in range(4):
                c0 = (r0 + q) * 128
                mm(S[:, q * 128:(q + 1) * 128], kb_[rows, c0:c0 + 128], qb_[rows, c0:c0 + 128], False, True, [kB, qB], [bankB[si]], skip=True)

        def tview(ap, r0):
            return ap.rearrange("p (m r) -> p r m", r=16)[:, r0:r0 + 4, :]

        def front(ui):
            r0, hh = units[ui]
            scores(ui)
            si = sring[ui % 2]
            S = banks[si]
            pb = pbuf[ui % 2][:, 0:512]
            pB = buf("p%d" % (ui % 2))
            act(pb, S, ACTF.Exp, [bankB[si]], [pB], scale=0.125)

        front(0)
        for ui in range(len(units)):
            r0, hh = units[ui]
            if ui + 1 < len(units):
                front(ui + 1)
            pb = pbuf[ui % 2][:, 0:512]
            pB = buf("p%d" % (ui % 2))
            tb = tbank[hh][(r0 // 4) % 2]
            for q in range(4):
                lhs = vb_[:, r0 + q, hh * 64: hh * 64 + 128]
                mm(banks[tb][:, q * 128:(q + 1) * 128], lhs, pb[:, q * 128:(q + 1) * 128], (q == 0), True, [vB, pB], [bankB[tb]], skip=True)
            if hh == 1:
                for h2 in range(2):
                    tb2 = tbank[h2][(r0 // 4) % 2]
                    Tt = banks[tb2].rearrange("p (a b) -> p a b", a=4)
                    nrows = slice(h2 * 64, (h2 + 1) * 64)
                    zrows = slice((1 - h2) * 64, (2 - h2) * 64)
                    tt(numS[2][nrows, r0 * 128:(r0 + 4) * 128].rearrange("p (a b) -> p a b", a=4), Tt[nrows], tview(ub_, r0)[nrows], ALU.mult,
                       [bankB[tb2], uB], [buf("numS2")])
                    stt(tview(zsum, r0)[nrows], Tt[zrows], 2.0, tview(zsum, r0)[nrows], ALU.mult, ALU.add, [bankB[tb2], buf("zsum")], [buf("zsum")])
            yield

    def att_B(n, bs, eslot):
        cb = n
        qb_, kb_, vb_, ub_ = qbuf[bs], kbuf[bs], vbuf[bs], ubuf[bs]
        qB, kB, vB, uB = buf("q%d" % bs), buf("k%d" % bs), buf("v%d" % bs), buf("u%d" % bs)
        eB = buf("e%d" % eslot)
        et = ebuf[eslot].rearrange("p (h t i) -> p h t i", h=2, t=9)
        units = [(hh, u) for hh in range(2) for u in range(16)]
        sring = [(2, 3), (4, 5)]
        tring = [6, 7]

        def front(ui):
            hh, u = units[ui]
            sa, sb_ = sring[ui % 2]
            rows = slice(hh * 64, (hh + 1) * 64)
            vs = v_of_u[u]
            v0 = vs[0]
            nv = len(vs)
            n1 = min(nv, 4)
            tids = [pair_tbl[(u, v)] for v in vs]
            runs = []
            st = 0
            for i2 in range(1, nv + 1):
                if i2 == nv or tids[i2] != tids[i2 - 1] + 1 or i2 == 4:
                    runs.append((st, i2))
                    st = i2
            first = {sa: True, sb_: True}
            for (a, b2) in runs:
                bk = sa if a < 4 else sb_
                off = a if a < 4 else a - 4
                t0 = tids[a]
                mm(banks[bk][:, off * 128:(off + b2 - a) * 128], ident,
                   et[:, hh, t0:t0 + (b2 - a), :].rearrange("p t i -> p (t i)"), first[bk], False,
                   [buf("ident"), eB], [bankB[bk]], skip=True)
                first[bk] = False
            mm(banks[sa][:, 0:n1 * 128], kb_[rows, u * 128:(u + 1) * 128], qb_[rows, v0 * 128:(v0 + n1) * 128], False, True,
               [kB, qB], [bankB[sa]], skip=True)
            if nv > 4:
                n2 = nv - 4
                mm(banks[sb_][:, 0:n2 * 128], kb_[rows, u * 128:(u + 1) * 128], qb_[rows, (v0 + 4) * 128:(v0 + nv) * 128], False, True,
                   [kB, qB], [bankB[sb_]], skip=True)
            pb = pbuf[ui % 2]
            pB = buf("p%d" % (ui % 2))
            act(pb[:, 0:n1 * 128], banks[sa][:, 0:n1 * 128], ACTF.Exp, [bankB[sa]], [pB], scale=0.125)
            if nv > 4:
                act(pb[:, 512:nv * 128], banks[sb_][:, 0:(nv - 4) * 128], ACTF.Exp, [bankB[sb_]], [pB], scale=0.125)

        front(0)
        for ui in range(len(units)):
            hh, u = units[ui]
            if ui + 1 < len(units):
                front(ui + 1)
            vs = v_of_u[u]
            pb = pbuf[ui % 2]
            pB = buf("p%d" % (ui % 2))
            lhs = vb_[:, u, hh * 64: hh * 64 + 128]
            for vi, v in enumerate(vs):
                tb = tring[(v // 4) % 2]
                mm(banks[tb][:, (v % 4) * 128:(v % 4 + 1) * 128], lhs, pb[:, vi * 128:(vi + 1) * 128],
                   (u == first_u[v] and v % 4 == 0), u == last_u[v], [vB, pB], [bankB[tb]], skip=True)
            for w in range(4):
                if u == last_u[4 * w + 3]:
                    tb = tring[w % 2]
                    Tt = banks[tb]
                    nrows = slice(hh * 64, (hh + 1) * 64)
                    zrows = slice((1 - hh) * 64, (2 - hh) * 64)
                    tsl = slice(512 * w, 512 * w + 512)
                    nB_, zB_ = buf("numS0_q%d" % w), buf("zsum_q%d" % w)
                    tt(numS[0][nrows, tsl], Tt[nrows], ub_[nrows, tsl], ALU.mult, [bankB[tb], uB], [nB_])
                    ts(zsum[nrows, tsl], Tt[zrows], 2.0, None, ALU.mult, None, [bankB[tb]], [zB_])
                    if hh == 1:
                        P.op("dve", "reciprocal", dict(out=zsum[:, tsl], in_=zsum[:, tsl]), [zB_], [zB_])
                        tt(abT[:, cb, tsl], numS[0][:, tsl], zsum[:, tsl], ALU.mult,
                           [nB_, zB_], [buf("abT_c%d" % cb)], eng="pool")
            yield

    def post_A_gen(s):
        for q in range(4):
            yield
            tsl = slice(512 * q, 512 * q + 512)
            P.op("dve", "reciprocal", dict(out=zsum[:, tsl], in_=zsum[:, tsl]), [buf("zsum_q%d" % q)], [buf("zsum_q%d" % q)])
            for g in range(3):
                if g == 0:
                    tt(abT[:, 3 * g + s, tsl], numS[g][:, tsl], zsum[:, tsl], ALU.mult,
                       [buf("numS%d_q%d" % (g, q)), buf("zsum_q%d" % q)], [buf("abT_c%d" % (3 * g + s))], eng="pool")
                else:
                    dd = DIL[g]
                    mq = 512 // dd
                    src = numS[g].rearrange("p (r m) -> p m r", r=dd)[:, mq * q:mq * (q + 1), :]
                    tt(abT[:, 3 * g + s, tsl].rearrange("p (m r) -> p m r", r=dd), src,
                       zsum[:, tsl].rearrange("p (m r) -> p m r", r=dd), ALU.mult,
                       [buf("numS%d" % g), buf("zsum_q%d" % q)], [buf("abT_c%d" % (3 * g + s))], eng=("pool" if g < 2 else "dve"))

    tpb = [banks[2 + i].bitcast(BF16) for i in range(4)]

    def S1(si, k):
        xi = k % 2
        xt = xst[xi]
        xB = buf("xst%d" % xi)
        dma("sp", xt, x_d[si, k * 128:(k + 1) * 128, :], [], [xB], "x%d" % xi)
        ss = stat[:, xi:xi + 1]
        rs_ = stat[:, 2 + xi:3 + xi]
        act(junk, xt, ACTF.Square, [xB], [buf("ss%d" % xi), buf("junk")], accum_out=ss)
        act(rs_, ss, ACTF.Sqrt, [buf("ss%d" % xi), buf("statinit")], [buf("rs%d" % xi)], scale=1.0 / D, bias=epsc)
        P.op("dve", "reciprocal", dict(out=rs_, in_=rs_), [buf("rs%d" % xi)], [buf("rs%d" % xi)])
        ts(xsb[xi], xt, rs_, None, ALU.mult, None, [xB, buf("rs%d" % xi)], [buf("xsb%d" % xi)])

    def S2(si, k):
        xi = k % 2
        t4 = k % 4
        xb16 = xsb[xi]
        for kc in range(8):
            bk = 2 + kc // 2
            dst = tpb[kc // 2][:, (kc % 2) * 512 + t4 * 128: (kc % 2) * 512 + (t4 + 1) * 128]
            P.op("pe", "transpose", dict(out=dst, in_=xb16[:, kc * 128:(kc + 1) * 128], identity=ident),
                 [buf("xsb%d" % xi), buf("ident")], [bankB[bk]])
        if t4 == 3:
            grp = k // 4
            for kc in range(8):
                bk = 2 + kc // 2
                src = tpb[kc // 2][:, (kc % 2) * 512:(kc % 2 + 1) * 512]
                dst = hnT[:, kc, grp * 512:(grp + 1) * 512]
                act(dst, src, ACTF.Copy, [bankB[bk], buf("gpre")], [buf("hnT")], scale=gpre[:, kc:kc + 1])

    def stageF(si, wslot0):
        mrgT = [qbuf[0], qbuf[1], ubuf[0], ubuf[1], kbuf[0][:, 0:T], kbuf[1][:, 0:T], ebuf[0][:, 0:T], ebuf[1][:, 0:T]]
        mB = [buf(nm) for nm in ("q0", "q1", "u0", "u1", "k0", "k1", "e0", "e1")]
        wslot = wslot0
        for mo in range(8):
            dma("sp", wbuf[wslot], wf_s[mo], [buf("wf_s")], [buf("w%d" % wslot)], "w%d" % wslot)
            wv = wbuf[wslot].rearrange("p (k c) -> p k c", k=32)
            wB = buf("w%d" % wslot)
            for tt_ in range(4):
                tsl = slice(tt_ * 512, (tt_ + 1) * 512)
                pbk = [0, 1, 2, 3] if (tt_ % 2 == 0) else [4, 5, 6, 7]
                ya, yb, ga, gb = [banks[i] for i in pbk]
                for kc in range(9):
                    mm(ya, wv[:, kc, :], abT[:, kc, tsl], kc == 0, kc == 8, [wB, buf("abT_c%d" % kc)], [bankB[pbk[0]]])
                for kc in range(7):
                    mm(yb, wv[:, 9 + kc, :], abT[:, 9 + kc, tsl], kc == 0, kc == 6, [wB, buf("abT_c%d" % (9 + kc))], [bankB[pbk[1]]])
                for kc in range(8):
                    mm(ga, wv[:, 16 + kc, :], hnT[:, kc, tsl], kc == 0, kc == 7, [wB, buf("hnT")], [bankB[pbk[2]]])
                for kc in range(8):
                    mm(gb, wv[:, 24 + kc, :], hnT[:, kc, tsl], kc == 0, kc == 7, [wB, buf("hnT")], [bankB[pbk[3]]])
                ta, tb_ = tmpf[0], tmpf[1]
                m1, m2 = tmpf[2], tmpf[3]
                act(ta, ga, ACTF.Tanh, [bankB[pbk[2]], buf("bgate")], [buf("tf0")], scale=0.5, bias=bgate[:, mo:mo + 1])
                act(tb_, gb, ACTF.Tanh, [bankB[pbk[3]], buf("bgate")], [buf("tf1")], scale=0.5, bias=bgate[:, 8 + mo:9 + mo])
                stt(m1, ta, 1.0, ya, ALU.add, ALU.mult, [buf("tf0"), bankB[pbk[0]]], [buf("tf2")])
                stt(m2, tb_, 1.0, yb, ALU.add, ALU.mult, [buf("tf1"), bankB[pbk[1]]], [buf("tf3")])
                tt(mrgT[mo][:, tsl], m1, m2, ALU.add, [buf("tf2"), buf("tf3")], [mB[mo]])
            wslot ^= 1
        return wslot, mrgT, mB

    xrb = [xr0, xr1]
    xrB = [[buf("xr0")], [buf("tf2"), buf("tf3")]]
    pb2s = [[0, 1], [6, 7]]

    def T0():
        for h in range(2):
            dma("sp", wbuf[h], wo_s[h], [buf("wo_s")], [buf("w%d" % h)], "w%d" % h)

    def T1(si, t16, mrgT, mB):
        wo = [wbuf[h].rearrange("p (k c) -> p k c", k=8) for h in range(2)]
        pb2 = pb2s[t16 % 2]
        yi = t16 % 2
        dma("pool", xrb[yi], x_d[si, t16 * 128:(t16 + 1) * 128, :], [], xrB[yi], "xr%d" % yi)
        for h in range(2):
            for kc in range(8):
                mm(banks[pb2[h]], mrgT[kc][:, t16 * 128:(t16 + 1) * 128], wo[h][:, kc, :], kc == 0, kc == 7,
                   [mB[kc], buf("w%d" % h)], [bankB[pb2[h]]])

    def T2(si, t16):
        pb2 = pb2s[t16 % 2]
        yi = t16 % 2
        ss2 = stat[:, 8 + 2 * yi: 10 + 2 * yi]
        for h in range(2):
            act(junk[:, 0:512], banks[pb2[h]], ACTF.Square, [bankB[pb2[h]]],
                [buf("ssF%d_%d" % (yi, h)), buf("junk")], accum_out=ss2[:, h:h + 1])
        rr = stat[:, 16 + yi:17 + yi]
        tt(rr, ss2[:, 0:1], ss2[:, 1:2], ALU.add, [buf("ssF%d_0" % yi), buf("ssF%d_1" % yi)], [buf("rr%d" % yi)])
        act(rr, rr, ACTF.Sqrt, [buf("rr%d" % yi), buf("statinit")], [buf("rr%d" % yi)], scale=0.25 / D, bias=epsc)
        P.op("dve", "reciprocal", dict(out=rr, in_=rr), [buf("rr%d" % yi)], [buf("rr%d" % yi)])

    def T3(si, t16):
        pb2 = pb2s[t16 % 2]
        yi = t16 % 2
        yt = yst[yi]
        xr = xrb[yi]
        rr = stat[:, 16 + yi:17 + yi]
        for h in range(2):
            stt(yt[:, h * 512:(h + 1) * 512], banks[pb2[h]], rr, gpost[:, h * 512:(h + 1) * 512], ALU.mult, ALU.mult,
                [bankB[pb2[h]], buf("rr%d" % yi), buf("gpost")], [buf("yst%d" % yi)])
        tt(yt, yt, xr, ALU.add, [buf("yst%d" % yi)] + xrB[yi], [buf("yst%d" % yi)])
        dma("pool", y_d[si, t16 * 128:(t16 + 1) * 128, :], yt, [buf("yst%d" % yi)], [buf("ydram")], "ys%d" % yi)

    def boundary(si, mrgT, mB, with_tail, with_s0):
        if with_tail:
            T0()
        for i in range(-1, 16):
            if i + 1 < 16:
                if with_tail:
                    T1(si, i + 1, mrgT, mB)
                if with_s0:
                    S1(si + 1, i + 1)
            if i >= 0:
                if with_tail:
                    T2(si, i)
                if with_s0:
                    S2(si + 1, i)
                if with_tail:
                    T3(si, i)

    def drain(gen):
        for _ in gen:
            pass

    def interleave(main, filler, nunits, nfill):
        fdone = filler is None
        ui = 0
        for _ in main:
            if not fdone:
                k = ((ui + 1) * nfill) // nunits - (ui * nfill) // nunits
                for _k in range(k):
                    try:
                        next(filler)
                    except StopIteration:
                        fdone = True
                        break
            ui += 1
        if not fdone:
            drain(filler)

    def roundrobin(g1, g2):
        d1 = d2 = False
        while not (d1 and d2):
            if not d1:
                try:
                    next(g1)
                except StopIteration:
                    d1 = True
            if not d2:
                try:
                    next(g2)
                except StopIteration:
                    d2 = True

    wslot = 0
    pending = [None]
    btg = btab_gen()
    bt_done = [False]

    def bt_step(k=1):
        for _ in range(k):
            if bt_done[0]:
                return
            try:
                next(btg)
            except StopIteration:
                bt_done[0] = True

    boundary(-1, None, None, False, True)
    for si in range(nseq):
        bs = 0
        load_chunk_consts(order[0], wslot)
        drain(inproj_gen(order[0], bs, wslot))
        for oi, n in enumerate(order):
            kind, d, g = chunk_geom(n)
            nxt = order[oi + 1] if oi + 1 < len(order) else None
            eslot = wslot
            filler = None
            if si == 0:
                if oi + 2 < len(order):
                    emit_casts(oi + 2, oi + 3, [buf("v%d" % bs)])
                if oi == 8:
                    dma("pool", wf_s, wf_d, [buf("v%d" % bs)], [buf("wf_s")], "pp5")
                    dma("pool", wo_s, wo_d, [buf("v%d" % bs)], [buf("wo_s")], "pp6")
            if n == NCH_A and not bt_done[0]:
                bt_step(1000)
            if nxt is not None:
                load_chunk_consts(nxt, wslot ^ 1)
                filler = inproj_gen(nxt, bs ^ 1, wslot ^ 1)
            if kind == "A01":
                main = att_A01(n, bs, eslot)
                nunits = (17 if d == 1 else 20)
            elif kind == "A2":
                main = att_A2(n, bs, eslot)
                nunits = 8
            else:
                main = att_B(n, bs, eslot)
                nunits = 32
            nfill = 24 + (0 if nxt is None else sum(2 if len(g_) > 2 else 1 for g_ in vgroups(nxt)))
            if si == 0 and not bt_done[0]:
                def main_bt(m):
                    for _ in m:
                        bt_step(1)
                        yield
                main = main_bt(main)
            if pending[0] is not None:
                def main_pp(m, pg):
                    for _ in m:
                        try:
                            next(pg)
                        except StopIteration:
                            pass
                        yield
                    for _ in pg:
                        pass
                main = main_pp(main, pending[0])
                pending[0] = None
            interleave(main, filler, nunits, nfill)
            if n < NCH_A and n // 3 == 2:
                pending[0] = post_A_gen(n % 3)
            bs ^= 1
            wslot ^= 1
        wslot, mrgT_, mB_ = stageF(si, wslot)
        boundary(si, mrgT_, mB_, True, si + 1 < nseq)

    P.emit()
    return nc


_ALIAS = {}


def _prep_weights(w_in, w_proj_a, w_proj_b, w_out):
    w_in = np.asarray(w_in, np.float32)[0]
    cols = {}
    offs = [0, WA, 2 * WA, 3 * WA, 4 * WA, 4 * WA + WB, 4 * WA + 2 * WB, 4 * WA + 3 * WB, 4 * WA + 4 * WB,
            4 * WA + 4 * WB + D]
    qa, ka, va, za, qb, kb, vb, zb, ga, gb = offs

    def tile_fm(c0):
        return w_in[:, c0:c0 + 128].reshape(8, 128, 128).transpose(1, 0, 2)

    wch = np.zeros((NCH, 128, 4, 8, 128), np.float32)
    for n in range(NCH):
        if n < NCH_A:
            c = n * 128
            mats = (qa + c, ka + c, za + c, va + c)
        else:
            c = (n - NCH_A) * 128
            mats = (qb + c, kb + c, zb + c, vb + c)
        for mi, c0 in enumerate(mats):
            wch[n, :, mi] = tile_fm(c0)
    wch = wch.reshape(NCH, 128, 4096)
    wa = np.asarray(w_proj_a, np.float32)[0]
    wb = np.asarray(w_proj_b, np.float32)[0]
    wf = np.zeros((8, 128, 32, 128), np.float32)
    for mo in range(8):
        ms = slice(mo * 128, (mo + 1) * 128)
        wf[mo, :, 0:9] = wa[:, ms].reshape(9, 128, 128).transpose(1, 0, 2)
        wf[mo, :, 9:16] = wb[:, ms].reshape(7, 128, 128).transpose(1, 0, 2)
        wf[mo, :, 16:24] = tile_fm(ga + mo * 128)
        wf[mo, :, 24:32] = tile_fm(gb + mo * 128)
    wf = wf.reshape(8, 128, 4096)
    wo_ = np.asarray(w_out, np.float32)[0]
    wo = np.zeros((2, 128, 8, 512), np.float32)
    for h in range(2):
        wo[h] = wo_[:, h * 512:(h + 1) * 512].reshape(8, 128, 512).transpose(1, 0, 2)
    wo = wo.reshape(2, 128, 4096)
    return np.ascontiguousarray(wch), np.ascontiguousarray(wf), np.ascontiguousarray(wo)


_NC_CACHE = {}


def run_layer(x_cores, norm_pre, w_in, b_gate, rpb, w_proj_a, w_proj_b, w_out, norm_post, core_ids=None):
    nseq = x_cores[0].shape[0]
    if nseq not in _NC_CACHE:
        _NC_CACHE[nseq] = build(nseq)
    nc = _NC_CACHE[nseq]
    wch, wf, wo = _prep_weights(w_in, w_proj_a, w_proj_b, w_out)
    eta = _alibi_tables()
    bbias, bmask, _ = _b_tables(np.asarray(rpb, np.float32)[0])
    gpre = np.ascontiguousarray(np.asarray(norm_pre, np.float32)[0].reshape(8, 128).T)
    bg = np.asarray(b_gate, np.float32)[0]
    bgl = np.ascontiguousarray(bg.reshape(2, 8, 128).transpose(2, 0, 1).reshape(128, 16))
    gpost = np.ascontiguousarray(np.broadcast_to(np.asarray(norm_post, np.float32)[0][None, :], (128, D)))
    ident = np.eye(128, dtype=np.float32)
    common = {"wch": wch, "wf": wf, "wo": wo, "eta": eta, "bbias": bbias, "bmask": bmask, "gpre": gpre,
              "bgate": bgl, "gpost": gpost, "ident": ident}
    in_maps = []
    for xc in x_cores:
        m = dict(common)
        m["x"] = np.ascontiguousarray(xc, dtype=np.float32)
        in_maps.append(m)
    if core_ids is None:
        core_ids = list(range(len(x_cores)))
    res = run_bass_kernel_spmd(nc, in_maps, core_ids=core_ids)
    return [r["y"] for r in res.results]


def kernel(x_prompt, x_sample, norm_pre, w_in, b_gate, rpb, w_proj_a, w_proj_b, w_out, norm_post):
    xp = np.asarray(x_prompt, np.float32)
    xs = np.asarray(x_sample, np.float32)
    x_cores = []
    for c in range(NCORES):
        x_cores.append(np.concatenate([xp[2 * c:2 * c + 2], xs[4 * c:4 * c + 4]], axis=0))
    ys = run_layer(x_cores, norm_pre, w_in, b_gate, rpb, w_proj_a, w_proj_b, w_out, norm_post)
    yp = np.concatenate([y[0:2] for y in ys], axis=0)
    ysm = np.concatenate([y[2:6] for y in ys], axis=0)
    return (yp.astype(np.float32), ysm.astype(np.float32))
```

```python
import numpy as np
import concourse.bass as bass
import concourse.mybir as mybir
from concourse.bass_utils import run_bass_kernel_spmd

F32 = mybir.dt.float32
BF16 = mybir.dt.bfloat16
ALU = mybir.AluOpType
ACTF = mybir.ActivationFunctionType

T = 2048
D = 1024
NCORES = 8
HEAD = 64
WA = 1152
WB = 896
NCH_A = 9
NCH_B = 7
NCH = 16
EPS = 1e-6
DIL = (1, 4, 16)
TBL_IDX = {("e", 3): 0, ("e", 2): 1, ("i", 2): 2, ("i", 1): 3, ("i", 0): 4, ("i", -1): 5, ("i", -2): 6,
           ("e", -2): 7, ("e", -3): 8}


class Buf:
    __slots__ = ("name", "last_w", "readers")

    def __init__(self, name=""):
        self.name = name
        self.last_w = None
        self.readers = []


class Op:
    __slots__ = ("eng", "fn", "deps", "flag", "val", "sem", "is_dma")

    def __init__(self, eng, fn, is_dma=False):
        self.eng = eng
        self.fn = fn
        self.deps = []
        self.flag = False
        self.val = None
        self.sem = None
        self.is_dma = is_dma


class Prog:
    ENGS = ("pe", "act", "dve", "pool", "sp")

    def __init__(self, nc):
        self.nc = nc
        self.ops = {e: [] for e in self.ENGS}
        self.all_ops = []

    def _add_dep(self, op, prod):
        if prod is None or prod is op:
            return
        if prod.eng == "pe" and op.eng == "pe" and not prod.is_dma and not op.is_dma:
            return
        if prod not in op.deps:
            op.deps.append(prod)
            prod.flag = True

    def op(self, eng, meth, kw, reads=(), writes=(), dma_key=None):
        fn = (lambda e, meth=meth, kw=kw: getattr(e, meth)(**kw))
        reads = [b for x in reads for b in (x if isinstance(x, (list, tuple)) else [x])]
        writes = [b for x in writes for b in (x if isinstance(x, (list, tuple)) else [x])]
        o = Op(eng, fn, is_dma=dma_key is not None)
        if dma_key is not None:
            o.sem = dma_key
            o.flag = True
        for b in reads:
            self._add_dep(o, b.last_w)
        for b in writes:
            self._add_dep(o, b.last_w)
            for r in b.readers:
                self._add_dep(o, r)
        for b in reads:
            b.readers.append(o)
        for b in writes:
            b.last_w = o
            b.readers = []
        self.ops[eng].append(o)
        self.all_ops.append(o)
        return o

    def emit(self):
        nc = self.nc
        eng_sem = {e: nc.alloc_semaphore("es_" + e) for e in self.ENGS}
        keys = []
        for e in self.ENGS:
            for o in self.ops[e]:
                if o.is_dma and o.sem not in keys:
                    keys.append(o.sem)
        dsem = {k: [nc.alloc_semaphore("ds_%d" % i), 0] for i, k in enumerate(keys)}
        for e in self.ENGS:
            cnt = 0
            for o in self.ops[e]:
                if o.is_dma:
                    d = dsem[o.sem]
                    d[1] += 16
                    o.val = d[1]
                    o.sem = d[0]
                elif o.flag:
                    cnt += 1
                    o.val = cnt
                    o.sem = eng_sem[e]
        all_dma_final = [(d[0], d[1]) for d in dsem.values()]
        known_e = {e: {} for e in self.ENGS}
        kn_of = {}
        waits_of = {}
        for o in self.all_ops:
            kn = known_e[o.eng]
            ws = []
            for p in o.deps:
                sid = p.sem.num
                if kn.get(sid, 0) < p.val:
                    ws.append((p.sem, p.val))
                    kn[sid] = p.val
                pk = kn_of.get(id(p))
                if pk is not None:
                    for k_, v_ in pk.items():
                        if kn.get(k_, 0) < v_:
                            kn[k_] = v_
            waits_of[id(o)] = ws
            if o.flag:
                snap = dict(kn)
                snap[o.sem.num] = max(snap.get(o.sem.num, 0), o.val)
                kn_of[id(o)] = snap
        with nc.Block() as block:
            def run(e):
                def body(eng):
                    known = known_e[e]
                    for o in self.ops[e]:
                        for (ws_, wv_) in waits_of[id(o)]:
                            eng.wait_ge(ws_, wv_)
                        ins = o.fn(eng)
                        if o.is_dma:
                            ins.then_inc(o.sem, 16)
                        elif o.flag:
                            ins.then_inc(o.sem, 1)
                    if e == "sp":
                        for (s, v) in all_dma_final:
                            if known.get(s.num, 0) < v:
                                eng.wait_ge(s, v)
                return body
            block.tensor(run("pe"))
            block.scalar(run("act"))
            block.vector(run("dve"))
            block.gpsimd(run("pool"))
            block.sync(run("sp"))


NEG = -240000.0


def _alibi_tables():
    slopes = (2.0 ** (-8.0 * np.arange(1, 19) / 18)).astype(np.float64)
    out = np.full((NCH_A, 128, 1024), NEG, np.float32)
    j = np.arange(128)[:, None].astype(np.float64)
    i = np.arange(128)[None, :].astype(np.float64)
    for c in range(NCH_A):
        g, s = c // 3, c % 3
        d = DIL[g]
        for hh in range(2):
            sl = float(slopes[g + 3 * (2 * s + hh)])
            if g < 2:
                bh = np.where(j <= i, -8.0 * sl * d * np.abs(i - j - 64.0), NEG)
                ah = np.where(j >= i, -8.0 * sl * d * np.abs(i - j + 64.0), NEG)
                main = np.concatenate([bh, ah], 1)
                bh2 = bh.copy(); bh2[64:, :] = NEG
                ah2 = ah.copy(); ah2[:64, :] = NEG
                bnd = np.concatenate([bh2, ah2], 1)
                out[c, :, hh * 256:(hh + 1) * 256] = main
                out[c, :, 512 + hh * 256:512 + (hh + 1) * 256] = bnd
            else:
                e2 = np.where(np.abs(i - j) <= 64, -8.0 * sl * d * np.abs(i - j), NEG)
                out[c, :, hh * 128:(hh + 1) * 128] = e2
    return np.maximum(out, NEG).astype(np.float32)


def _b_pairs():
    rows = 32
    rs = np.clip(np.arange(rows) - 4, 0, rows - 8)
    pairs = {}
    for u in range(16):
        for v in range(16):
            pat = np.zeros((2, 2), bool)
            for rl_k in range(2):
                for rl_q in range(2):
                    r = 2 * v + rl_q
                    rp = 2 * u + rl_k
                    pat[rl_k, rl_q] = (rs[r] <= rp < rs[r] + 8)
            if pat.any():
                pairs[(u, v)] = pat
    return pairs


def _b_tables(rpb):
    pairs = _b_pairs()
    tbl_pat = {}
    tbl_delta = {}
    pair_tbl = {}
    for (u, v), pat in pairs.items():
        dl = u - v
        ipat = np.zeros((2, 2), bool)
        for a in range(2):
            for b in range(2):
                dr = 2 * dl + a - b
                ipat[a, b] = (-4 <= dr <= 3)
        if abs(dl) <= 2 and (pat == ipat).all():
            key = ("i", dl)
        else:
            assert pat.all(), (u, v, pat)
            key = ("e", dl)
        assert key in TBL_IDX, key
        if key in tbl_pat:
            assert (tbl_pat[key] == pat).all()
        tbl_pat[key] = pat
        tbl_delta[key] = dl
        pair_tbl[(u, v)] = TBL_IDX[key]
    assert len(tbl_pat) == 9
    cp = np.arange(64)[:, None]
    cq = np.arange(64)[None, :]
    cs = np.clip(cq - 8, 0, 48)
    colvalid = (cp >= cs) & (cp < cs + 16)
    dc = np.clip(cp - cq, -15, 15) + 15
    bias = np.zeros((14, 9, 128, 128), np.float32)
    mask = np.zeros((9, 128, 128), np.float32)
    for key, idx in TBL_IDX.items():
        pat = tbl_pat[key]
        dl = tbl_delta[key]
        for a in range(2):
            for b in range(2):
                if not pat[a, b]:
                    continue
                dr = 2 * dl + a - b + 7
                assert 0 <= dr <= 14
                mask[idx, a * 64:(a + 1) * 64, b * 64:(b + 1) * 64] = colvalid
                bias[:, idx, a * 64:(a + 1) * 64, b * 64:(b + 1) * 64] = rpb[:, dr][:, dc]
    bias = bias.reshape(7, 2, 9, 128, 128).transpose(0, 3, 1, 2, 4).reshape(7, 128, 2304)
    mask = (mask - 1.0) * (-NEG)
    maskf = np.broadcast_to(mask[None, None], (7, 2, 9, 128, 128)).transpose(0, 3, 1, 2, 4).reshape(7, 128, 2304)
    return np.ascontiguousarray(bias), np.ascontiguousarray(maskf), pair_tbl


def build(nseq, nchunks=NCH):
    nc = bass.Bass("TRN2", target_bir_lowering=False)
    P = Prog(nc)
    pair_tbl = _b_tables(np.zeros((14, 15, 31), np.float32))[2]
    pairs = sorted(pair_tbl.keys())
    v_of_u = {u: [v for (uu, v) in pairs if uu == u] for u in range(16)}
    for u in range(16):
        assert v_of_u[u] == list(range(v_of_u[u][0], v_of_u[u][-1] + 1))
    first_u = {v: min(u for (u, vv) in pairs if vv == v) for v in range(16)}
    last_u = {v: max(u for (u, vv) in pairs if vv == v) for v in range(16)}

    def din(name, shape, dt=F32):
        return nc.dram_tensor(name, list(shape), dt, kind="ExternalInput").ap()

    x_d = din("x", [nseq, T, D])
    y_d = nc.dram_tensor("y", [nseq, T, D], F32, kind="ExternalOutput").ap()
    wch_d = din("wch", [NCH, 128, 4096])
    wf_d = din("wf", [8, 128, 4096])
    wo_d = din("wo", [2, 128, 4096])
    eta_d = din("eta", [NCH_A, 128, 1024])
    bb_d = din("bbias", [NCH_B, 128, 2304])
    bm_d = din("bmask", [NCH_B, 128, 2304])
    gpre_d = din("gpre", [128, 8])
    bg_d = din("bgate", [128, 16])
    gpost_d = din("gpost", [128, D])
    id_d = din("ident", [128, 128])
    wch_s = nc.dram_tensor("wch_s", [NCH, 128, 4096], BF16, kind="Internal").ap()
    wf_s = nc.dram_tensor("wf_s", [8, 128, 4096], BF16, kind="Internal").ap()
    wo_s = nc.dram_tensor("wo_s", [2, 128, 4096], BF16, kind="Internal").ap()
    eta_s = nc.dram_tensor("eta_s", [NCH_A, 128, 1024], BF16, kind="Internal").ap()
    etb_s = nc.dram_tensor("etb_s", [NCH_B, 128, 2304], BF16, kind="Internal").ap()

    def sb(name, shape, dt):
        return nc.alloc_sbuf_tensor(name, list(shape), dt).ap()

    hnT = sb("hnT", [128, 8, T], BF16)
    abT = sb("abT", [128, NCH, T], BF16)
    qbuf = [sb("q%d" % i, [128, T], BF16) for i in range(2)]
    kbuf = [sb("k%d" % i, [128, 2560], BF16) for i in range(2)]
    vbuf = [sb("v%d" % i, [128, 20, 192], BF16) for i in range(2)]
    ubuf = [sb("u%d" % i, [128, T], BF16) for i in range(2)]
    scr = sb("scr", [128, 5120], F32)
    wbuf = [sb("w%d" % i, [128, 4096], BF16) for i in range(2)]
    ebuf = [sb("e%d" % i, [128, 2304], BF16) for i in range(2)]
    xr0 = sb("xr0", [128, 1024], F32)
    pbuf = [sb("p%d" % i, [128, 1024], BF16) for i in range(2)]
    tmpf_all = sb("tfall", [128, 2048], F32)
    tmpf = [tmpf_all[:, i * 512:(i + 1) * 512] for i in range(4)]
    xr1 = tmpf_all[:, 1024:2048]
    ident = sb("identb", [128, 128], BF16)
    identf = sb("identf", [128, 128], F32)
    gpre = sb("gpre_s", [128, 8], F32)
    bgate = sb("bgate_s", [128, 16], F32)
    gpost = sb("gpost_s", [128, D], F32)
    stat = sb("stat", [128, 64], F32)
    junk = sb("junk", [128, D], BF16)

    scr_bf = scr.bitcast(BF16)
    numS = [scr_bf[:, g * T:(g + 1) * T] for g in range(3)]
    zsum = scr[:, 3072:5120]
    xst = [scr[:, 0:1024], scr[:, 1024:2048]]
    xsb = [scr_bf[:, 4096:5120], scr_bf[:, 5120:6144]]
    yst = [scr[:, 3072:4096], scr[:, 4096:5120]]

    ps_all = nc.alloc_psum_tensor("ps_all", [128, 4096], F32).ap()
    banks = [ps_all[:, i * 512:(i + 1) * 512] for i in range(8)]
    bankB = [Buf("bank%d" % i) for i in range(8)]

    B = {}
    RQ = {nm: [Buf("%s_q%d" % (nm, q)) for q in range(4)] for nm in ("numS0", "numS1", "numS2", "zsum")}
    ALIAS = {"xst0": RQ["numS0"], "xst1": RQ["numS1"], "xsb0": RQ["numS2"][0:2], "xsb1": RQ["numS2"][2:4],
             "yst0": RQ["zsum"][0:2], "yst1": RQ["zsum"][2:4]}
    PS_ = [[Buf("p%d_%d" % (i, j)) for j in range(2)] for i in range(2)]
    for i in range(2):
        ALIAS["p%d" % i] = PS_[i]
        for j in range(2):
            ALIAS["p%d_%d" % (i, j)] = [PS_[i][j]]
    for nm in ("numS0", "numS1", "numS2", "zsum"):
        ALIAS[nm] = RQ[nm]
        for q in range(4):
            ALIAS["%s_q%d" % (nm, q)] = [RQ[nm][q]]

    def buf(name):
        if name in ALIAS:
            return ALIAS[name]
        if name not in B:
            B[name] = Buf(name)
        return B[name]

    def mm(out, lhsT, rhs, start, stop, reads, writes, skip=False):
        kw = dict(out=out, lhsT=lhsT, rhs=rhs, start=start, stop=stop)
        if skip:
            kw["skip_group_check"] = True
        P.op("pe", "matmul", kw, reads, writes)

    def act(out, in_, func, reads, writes, **kw):
        P.op("act", "activation", dict(out=out, in_=in_, func=func, **kw), reads, writes)

    def tt(out, in0, in1, op, reads, writes, eng="dve"):
        P.op(eng, "tensor_tensor", dict(out=out, in0=in0, in1=in1, op=op), reads, writes)

    def stt(out, in0, scalar, in1, op0, op1, reads, writes, eng="dve"):
        P.op(eng, "scalar_tensor_tensor", dict(out=out, in0=in0, scalar=scalar, in1=in1, op0=op0, op1=op1), reads, writes)

    def ts(out, in0, s1, s2, op0, op1, reads, writes, eng="dve"):
        kw = dict(out=out, in0=in0, scalar1=s1, scalar2=s2, op0=op0)
        if op1 is not None:
            kw["op1"] = op1
        P.op(eng, "tensor_scalar", kw, reads, writes)

    def cp(out, in_, reads, writes, eng="dve"):
        P.op(eng, "tensor_copy", dict(out=out, in_=in_), reads, writes)

    def dma(eng, out, in_, reads, writes, key):
        P.op(eng, "dma_start", dict(out=out, in_=in_), reads, writes, dma_key=key)

    cast_order = [0, 3, 6, 1, 4, 7, 2, 5, 8] + list(range(NCH_A, NCH))

    def emit_casts(lo, hi, gate):
        for n_ in cast_order[lo:hi]:
            if n_ < NCH_A:
                dma("pool", eta_s[n_], eta_d[n_], gate, [buf("eta_c%d" % n_)], "ppe%d" % n_)
            dma("pool", wch_s[n_], wch_d[n_], gate, [buf("wch_c%d" % n_)], "ppc%d" % n_)

    emit_casts(0, 2, [])
    dma("sp", identf, id_d, [], [buf("identf")], "c0")
    dma("sp", gpre, gpre_d, [], [buf("gpre")], "c1")
    dma("sp", bgate, bg_d, [], [buf("bgate")], "c2")
    dma("sp", gpost, gpost_d, [], [buf("gpost")], "c3")
    cp(ident, identf, [buf("identf")], [buf("ident")])
    ts(bgate, bgate, 0.5, None, ALU.mult, None, [buf("bgate")], [buf("bgate")])
    for i in range(2):
        P.op("dve", "memset", dict(ap=kbuf[i], constant=0.0), [], [buf("k%d" % i)])
        P.op("dve", "memset", dict(ap=vbuf[i], constant=0.0), [], [buf("v%d" % i)])
        P.op("dve", "memset", dict(ap=vbuf[i][:, :, 64:128], constant=1.0), [], [buf("v%d" % i)])
        P.op("dve", "memset", dict(ap=qbuf[i], constant=0.0), [], [buf("q%d" % i)])
    epsc = stat[:, 40:41]
    P.op("dve", "memset", dict(ap=epsc, constant=EPS), [], [buf("statinit")])
    ts(gpost, gpost, 0.5, None, ALU.mult, None, [buf("gpost")], [buf("gpost")])
    abhi = abT[:, 9:16, :].rearrange("p c t -> p (c t)")
    abhi_f = abhi.bitcast(F32)
    bt_stage = []
    for i in range(2):
        o = i * 2880
        bt_stage.append((abhi_f[:, o:o + 1152], abhi_f[:, o + 1152:o + 2304], abhi[:, 2 * (o + 2304): 2 * (o + 2304) + 1152]))

    def btab_gen():
        items = [(c, h2) for c in range(NCH_B) for h2 in range(2)]

        def loads(k):
            c, h2 = items[k]
            sl = slice(h2 * 1152, (h2 + 1) * 1152)
            stg, stg2, _ = bt_stage[k % 2]
            dma("sp", stg, bb_d[c, :, sl], [], [buf("btA%d" % (k % 2))], "c4%d" % (k % 2))
            dma("sp", stg2, bm_d[c, :, sl], [], [buf("btB%d" % (k % 2))], "c5%d" % (k % 2))

        loads(0)
        for k in range(len(items)):
            c, h2 = items[k]
            sl = slice(h2 * 1152, (h2 + 1) * 1152)
            if k + 1 < len(items):
                loads(k + 1)
            yield
            stg, stg2, so = bt_stage[k % 2]
            stt(so, stg, 8.0, stg2, ALU.mult, ALU.add, [buf("btA%d" % (k % 2)), buf("btB%d" % (k % 2))], [buf("btO%d" % (k % 2))])
            dma("sp", etb_s[c, :, sl], so, [buf("btO%d" % (k % 2))], [buf("etb_s%d" % (k % 2))], "c6%d" % (k % 2))
            yield

    state = {"ipb": 0, "x": 0}

    def load_chunk_consts(n, slot):
        dma("sp", wbuf[slot], wch_s[n], [buf("wch_c%d" % n)], [buf("w%d" % slot)], "w%d" % slot)
        if n < NCH_A:
            dma("sp", ebuf[slot][:, 0:1024], eta_s[n], [buf("eta_c%d" % n)], [buf("e%d" % slot)], "e%d" % slot)
        else:
            dma("sp", ebuf[slot], etb_s[n - NCH_A], [buf("etb_s0"), buf("etb_s1")], [buf("e%d" % slot)], "e%d" % slot)

    def chunk_geom(n):
        if n < NCH_A:
            g = n // 3
            return ("A2" if g == 2 else "A01"), DIL[g], g
        return "B", 1, 3

    order = []
    for s in range(3):
        for g in range(3):
            order.append(3 * g + s)
    order += list(range(NCH_A, NCH))
    order = order[:nchunks] if nchunks < NCH else order

    def vslots(n):
        kind, d, g = chunk_geom(n)
        res = []
        if kind == "A01":
            L = T // d
            nq = L // 128
            for r in range(d):
                for kb in range(nq + 1):
                    m0 = 128 * kb - 64
                    lo = max(m0, 0)
                    hi = min(m0 + 128, L)
                    res.append((r * (nq + 1) + kb, lo * d + r, d, hi - lo, lo - m0))
        elif kind == "A2":
            for r in range(16):
                res.append((r, r, 16, 128, 0))
        else:
            for u in range(16):
                res.append((u, 128 * u, 1, 128, 0))
        return res

    def vgroups(n):
        sl = vslots(n)
        res = []
        i = 0
        while i < len(sl):
            grp = [sl[i]]
            if sl[i][3] == 128:
                while len(grp) < 4 and i + len(grp) < len(sl) and sl[i + len(grp)][3] == 128 \
                        and sl[i + len(grp)][0] == grp[-1][0] + 1:
                    grp.append(sl[i + len(grp)])
            i += len(grp)
            res.append(grp)
        return res

    def inproj_gen(n, bs, wslot):
        kind, d, g = chunk_geom(n)
        w4 = wbuf[wslot].rearrange("p (m k c) -> p m k c", m=4, k=8)
        wB = buf("w%d" % wslot)
        hB = buf("hnT")
        qB, kB, vB, uB = buf("q%d" % bs), buf("k%d" % bs), buf("v%d" % bs), buf("u%d" % bs)
        qb_, kb_, vb_, ub_ = qbuf[bs], kbuf[bs], vbuf[bs], ubuf[bs]

        def acc_fm(mi, tt_):
            bi = state["ipb"]
            state["ipb"] ^= 1
            ps = banks[bi]
            for kc in range(8):
                mm(ps, w4[:, mi, kc, :], hnT[:, kc, tt_ * 512:(tt_ + 1) * 512], kc == 0, kc == 7, [wB, hB], [bankB[bi]])
                if kc == 3:
                    yield None
            yield (bi, ps)

        for tt_ in range(4):
            for r_ in acc_fm(2, tt_):
                if r_ is None:
                    yield
                else:
                    bi, ps = r_
            tf = tmpf[tt_ % 2]
            tB = buf("tf%d" % (tt_ % 2))
            act(tf, ps, ACTF.Tanh, [bankB[bi]], [tB], scale=0.5)
            stt(ub_[:, tt_ * 512:(tt_ + 1) * 512], tf, 1.0, ps, ALU.add, ALU.mult, [tB, bankB[bi]], [uB])
            yield
        for mi, dstB in ((0, qB), (1, kB)):
            for tt_ in range(4):
                for r_ in acc_fm(mi, tt_):
                    if r_ is None:
                        yield
                    else:
                        bi, ps = r_
                if (kind == "A01" and d == 1) or kind == "B":
                    off = 64 if (mi == 1 and kind == "A01") else 0
                    dst = (kb_ if mi == 1 else qb_)[:, off + tt_ * 512: off + (tt_ + 1) * 512]
                    src = ps
                elif kind == "A01":
                    if mi == 1:
                        dst = kb_.rearrange("p (r m) -> p r m", r=4)[:, :, 64 + 128 * tt_: 64 + 128 * tt_ + 128]
                    else:
                        dst = qb_.rearrange("p (r m) -> p r m", r=4)[:, :, 128 * tt_:128 * tt_ + 128]
                    src = ps.rearrange("p (j r) -> p r j", r=4)
                else:
                    base = (kb_[:, 0:T] if mi == 1 else qb_)
                    dst = base.rearrange("p (r m) -> p r m", r=16)[:, :, 32 * tt_:32 * tt_ + 32]
                    src = ps.rearrange("p (j r) -> p r j", r=16)
                act(dst, src, ACTF.Copy, [bankB[bi]], [dstB])
                yield
        for grp in vgroups(n):
            bi = state["ipb"]
            state["ipb"] ^= 1
            ps = banks[bi]
            for q, (slot, t0, step, nv, row0) in enumerate(grp):
                for kc in range(8):
                    lhs = hnT[:, kc, t0: t0 + (nv - 1) * step + 1: step]
                    mm(ps[0:nv, q * 128:(q + 1) * 128], lhs, w4[:, 3, kc, :], kc == 0, kc == 7, [wB, hB], [bankB[bi]])
                if q == 1 and len(grp) > 2:
                    yield
            slot0, _, _, nv, row0 = grp[0]
            ng = len(grp)
            if nv == 128:
                dst = bass.AP(vb_.tensor, vb_.offset + slot0 * 192, [list(vb_.ap[0]), [192, ng], [128, 2], [1, 64]])
                src = ps[:, 0:ng * 128].rearrange("p (s h c) -> p s h c", s=ng, h=2)
            else:
                dst = bass.AP(vb_.tensor, vb_[row0:row0 + 64, slot0, :].offset, [[vb_.ap[0][0], 64], [128, 2], [1, 64]])
                src = ps[0:64, 0:128].rearrange("p (h c) -> p h c", h=2)
            act(dst, src, ACTF.Copy, [bankB[bi]], [vB])
            yield

    def att_A01(n, bs, eslot):
        kind, d, g = chunk_geom(n)
        L = T // d
        nq = L // 128
        qb_, kb_, vb_, ub_ = qbuf[bs], kbuf[bs], vbuf[bs], ubuf[bs]
        qB, kB, vB, uB = buf("q%d" % bs), buf("k%d" % bs), buf("v%d" % bs), buf("u%d" % bs)
        eB = buf("e%d" % eslot)
        et = ebuf[eslot]
        units = [(r, kb) for r in range(d) for kb in range(nq + 1)]
        sring = [(2, 3), (4, 5)]
        tbank = {0: [6], 1: [7]}

        def scores(ui):
            r, kb = units[ui]
            sp_ = sring[ui % 2]
            qbase = r * L
            kbase = r * (L + 128) + 128 * kb
            if kb == 0:
                q0, nqc, o0 = qbase, 128, 128
            elif kb == nq:
                q0, nqc, o0 = qbase + 128 * (nq - 1), 128, 0
            else:
                q0, nqc, o0 = qbase + 128 * (kb - 1), 256, 0
            tsel = 512 if (kb == 0 or kb == nq) else 0
            for hh in range(2):
                mm(banks[sp_[hh]][:, 0:256], ident, et[:, tsel + hh * 256: tsel + hh * 256 + 256], True, False,
                   [buf("ident"), eB], [bankB[sp_[hh]]], skip=True)
            for hh in range(2):
                rows = slice(hh * 64, (hh + 1) * 64)
                mm(banks[sp_[hh]][:, o0: o0 + nqc], kb_[rows, kbase:kbase + 128], qb_[rows, q0:q0 + nqc],
                   False, True, [kB, qB], [bankB[sp_[hh]]], skip=True)

        def pslot(ui):
            k4 = ui % 4
            return pbuf[k4 // 2][:, (k4 % 2) * 512:(k4 % 2 + 1) * 512], buf("p%d_%d" % (k4 // 2, k4 % 2))

        def pv_first(ui):
            r, kb = units[ui]
            if kb > nq - 1:
                return
            pb, pB = pslot(ui)
            slot = r * (nq + 1) + kb
            for hh in range(2):
                lhs = vb_[:, slot, hh * 64: hh * 64 + 128]
                qb = kb
                tb = tbank[hh][0]
                mm(banks[tb][:, (qb % 4) * 128:(qb % 4 + 1) * 128], lhs, pb[:, hh * 256 + 128: hh * 256 + 256],
                   (qb % 4 == 0), False, [vB, pB], [bankB[tb]], skip=True)

        def front(ui):
            r, kb = units[ui]
            scores(ui)
            sp_ = sring[ui % 2]
            S2 = ps_all[:, sp_[0] * 512:(sp_[0] + 2) * 512].rearrange("p (h c) -> p h c", h=2)[:, :, 0:256]
            pb, pB = pslot(ui)
            act(pb.rearrange("p (h c) -> p h c", h=2), S2, ACTF.Exp, [bankB[sp_[0]], bankB[sp_[1]]], [pB], scale=0.125)

        front(0)
        for ui in range(len(units)):
            r, kb = units[ui]
            if ui + 1 < len(units):
                front(ui + 1)
            if ui >= 1:
                pv_first(ui - 1)
            pb, pB = pslot(ui)
            slot = r * (nq + 1) + kb
            grp_off = r * (nq // 4)
            for hh in range(2):
                lhs = vb_[:, slot, hh * 64: hh * 64 + 128]
                if kb >= 1:
                    qb = kb - 1
                    tb = tbank[hh][0]
                    mm(banks[tb][:, (qb % 4) * 128:(qb % 4 + 1) * 128], lhs, pb[:, hh * 256: hh * 256 + 128],
                       False, True, [vB, pB], [bankB[tb]], skip=True)
            if kb >= 4 and kb % 4 == 0:
                w = (kb - 1) // 4
                if d == 1:
                    tsl = slice(512 * w, 512 * w + 512)
                    uap, nap, zap = ub_[:, tsl], numS[g][:, tsl], zsum[:, tsl]
                    nB_, zB_ = buf("numS%d_q%d" % (g, w)), buf("zsum_q%d" % w)
                else:
                    uap, nap, zap = ub_[:, r:T:4], numS[g][:, r * 512:(r + 1) * 512], zsum[:, r:T:4]
                    nB_, zB_ = buf("numS%d" % g), buf("zsum")
                for hh in range(2):
                    tb = tbank[hh][0]
                    Tt = banks[tb]
                    nrows = slice(hh * 64, (hh + 1) * 64)
                    zrows = slice((1 - hh) * 64, (2 - hh) * 64)
                    tt(nap[nrows], Tt[nrows], uap[nrows], ALU.mult, [bankB[tb], uB], [nB_])
                    if g == 0:
                        ts(zap[nrows], Tt[zrows], 2.0, None, ALU.mult, None, [bankB[tb]], [zB_])
                    else:
                        stt(zap[nrows], Tt[zrows], 2.0, zap[nrows], ALU.mult, ALU.add, [bankB[tb], zB_], [zB_])
            if ui == len(units) - 1:
                pv_first(ui)
            yield

    def att_A2(n, bs, eslot):
        qb_, kb_, vb_, ub_ = qbuf[bs], kbuf[bs], vbuf[bs], ubuf[bs]
        qB, kB, vB, uB = buf("q%d" % bs), buf("k%d" % bs), buf("v%d" % bs), buf("u%d" % bs)
        eB = buf("e%d" % eslot)
        et = ebuf[eslot]
        units = [(r0, hh) for r0 in range(0, 16, 4) for hh in range(2)]
        sring = [2, 3]
        tbank = {0: [4, 6], 1: [5, 7]}

        def scores(ui):
            r0, hh = units[ui]
            si = sring[ui % 2]
            S = banks[si]
            rows = slice(hh * 64, (hh + 1) * 64)
            for q in range(4):
                mm(S[:, q * 128:(q + 1) * 128], ident, et[:, hh * 128:(hh + 1) * 128], q == 0, False,
                   [buf("ident"), eB], [bankB[si]], skip=True)
            for q in range(4):
                c0 = (r0 + q) * 128
                mm(S[:, q * 128:(q + 1) * 128], kb_[rows, c0:c0 + 128], qb_[rows, c0:c0 + 128], False, True, [kB, qB], [bankB[si]], skip=True)

        def tview(ap, r0):
            return ap.rearrange("p (m r) -> p r m", r=16)[:, r0:r0 + 4, :]

        def front(ui):
            r0, hh = units[ui]
            scores(ui)
            si = sring[ui % 2]
            S = banks[si]
            pb = pbuf[ui % 2][:, 0:512]
            pB = buf("p%d" % (ui % 2))
            act(pb, S, ACTF.Exp, [bankB[si]], [pB], scale=0.125)

        front(0)
        for ui in range(len(units)):
            r0, hh = units[ui]
            if ui + 1 < len(units):
                front(ui + 1)
            pb = pbuf[ui % 2][:, 0:512]
            pB = buf("p%d" % (ui % 2))
            tb = tbank[hh][(r0 // 4) % 2]
            for q in range(4):
                lhs = vb_[:, r0 + q, hh * 64: hh * 64 + 128]
                mm(banks[tb][:, q * 128:(q + 1) * 128], lhs, pb[:, q * 128:(q + 1) * 128], (q == 0), True, [vB, pB], [bankB[tb]], skip=True)
            if hh == 1:
                for h2 in range(2):
                    tb2 = tbank[h2][(r0 // 4) % 2]
                    Tt = banks[tb2].rearrange("p (a b) -> p a b", a=4)
                    nrows = slice(h2 * 64, (h2 + 1) * 64)
                    zrows = slice((1 - h2) * 64, (2 - h2) * 64)
                    tt(numS[2][nrows, r0 * 128:(r0 + 4) * 128].rearrange("p (a b) -> p a b", a=4), Tt[nrows], tview(ub_, r0)[nrows], ALU.mult,
                       [bankB[tb2], uB], [buf("numS2")])
                    stt(tview(zsum, r0)[nrows], Tt[zrows], 2.0, tview(zsum, r0)[nrows], ALU.mult, ALU.add, [bankB[tb2], buf("zsum")], [buf("zsum")])
            yield

    def att_B(n, bs, eslot):
        cb = n
        qb_, kb_, vb_, ub_ = qbuf[bs], kbuf[bs], vbuf[bs], ubuf[bs]
        qB, kB, vB, uB = buf("q%d" % bs), buf("k%d" % bs), buf("v%d" % bs), buf("u%d" % bs)
        eB = buf("e%d" % eslot)
        et = ebuf[eslot].rearrange("p (h t i) -> p h t i", h=2, t=9)
        units = [(hh, u) for hh in range(2) for u in range(16)]
        sring = [(2, 3), (4, 5)]
        tring = [6, 7]

        def front(ui):
            hh, u = units[ui]
            sa, sb_ = sring[ui % 2]
            rows = slice(hh * 64, (hh + 1) * 64)
            vs = v_of_u[u]
            v0 = vs[0]
            nv = len(vs)
            n1 = min(nv, 4)
            tids = [pair_tbl[(u, v)] for v in vs]
            runs = []
            st = 0
            for i2 in range(1, nv + 1):
                if i2 == nv or tids[i2] != tids[i2 - 1] + 1 or i2 == 4:
                    runs.append((st, i2))
                    st = i2
            first = {sa: True, sb_: True}
            for (a, b2) in runs:
                bk = sa if a < 4 else sb_
                off = a if a < 4 else a - 4
                t0 = tids[a]
                mm(banks[bk][:, off * 128:(off + b2 - a) * 128], ident,
                   et[:, hh, t0:t0 + (b2 - a), :].rearrange("p t i -> p (t i)"), first[bk], False,
                   [buf("ident"), eB], [bankB[bk]], skip=True)
                first[bk] = False
            mm(banks[sa][:, 0:n1 * 128], kb_[rows, u * 128:(u + 1) * 128], qb_[rows, v0 * 128:(v0 + n1) * 128], False, True,
               [kB, qB], [bankB[sa]], skip=True)
            if nv > 4:
                n2 = nv - 4
                mm(banks[sb_][:, 0:n2 * 128], kb_[rows, u * 128:(u + 1) * 128], qb_[rows, (v0 + 4) * 128:(v0 + nv) * 128], False, True,
                   [kB, qB], [bankB[sb_]], skip=True)
            pb = pbuf[ui % 2]
            pB = buf("p%d" % (ui % 2))
            act(pb[:, 0:n1 * 128], banks[sa][:, 0:n1 * 128], ACTF.Exp, [bankB[sa]], [pB], scale=0.125)
            if nv > 4:
                act(pb[:, 512:nv * 128], banks[sb_][:, 0:(nv - 4) * 128], ACTF.Exp, [bankB[sb_]], [pB], scale=0.125)

        front(0)
        for ui in range(len(units)):
            hh, u = units[ui]
            if ui + 1 < len(units):
                front(ui + 1)
            vs = v_of_u[u]
            pb = pbuf[ui % 2]
            pB = buf("p%d" % (ui % 2))
            lhs = vb_[:, u, hh * 64: hh * 64 + 128]
            for vi, v in enumerate(vs):
                tb = tring[(v // 4) % 2]
                mm(banks[tb][:, (v % 4) * 128:(v % 4 + 1) * 128], lhs, pb[:, vi * 128:(vi + 1) * 128],
                   (u == first_u[v] and v % 4 == 0), u == last_u[v], [vB, pB], [bankB[tb]], skip=True)
            for w in range(4):
                if u == last_u[4 * w + 3]:
                    tb = tring[w % 2]
                    Tt = banks[tb]
                    nrows = slice(hh * 64, (hh + 1) * 64)
                    zrows = slice((1 - hh) * 64, (2 - hh) * 64)
                    tsl = slice(512 * w, 512 * w + 512)
                    nB_, zB_ = buf("numS0_q%d" % w), buf("zsum_q%d" % w)
                    tt(numS[0][nrows, tsl], Tt[nrows], ub_[nrows, tsl], ALU.mult, [bankB[tb], uB], [nB_])
                    ts(zsum[nrows, tsl], Tt[zrows], 2.0, None, ALU.mult, None, [bankB[tb]], [zB_])
                    if hh == 1:
                        P.op("dve", "reciprocal", dict(out=zsum[:, tsl], in_=zsum[:, tsl]), [zB_], [zB_])
                        tt(abT[:, cb, tsl], numS[0][:, tsl], zsum[:, tsl], ALU.mult,
                           [nB_, zB_], [buf("abT_c%d" % cb)], eng="pool")
            yield

    def post_A_gen(s):
        for q in range(4):
            yield
            tsl = slice(512 * q, 512 * q + 512)
            P.op("dve", "reciprocal", dict(out=zsum[:, tsl], in_=zsum[:, tsl]), [buf("zsum_q%d" % q)], [buf("zsum_q%d" % q)])
            for g in range(3):
                if g == 0:
                    tt(abT[:, 3 * g + s, tsl], numS[g][:, tsl], zsum[:, tsl], ALU.mult,
                       [buf("numS%d_q%d" % (g, q)), buf("zsum_q%d" % q)], [buf("abT_c%d" % (3 * g + s))], eng="pool")
                else:
                    dd = DIL[g]
                    mq = 512 // dd
                    src = numS[g].rearrange("p (r m) -> p m r", r=dd)[:, mq * q:mq * (q + 1), :]
                    tt(abT[:, 3 * g + s, tsl].rearrange("p (m r) -> p m r", r=dd), src,
                       zsum[:, tsl].rearrange("p (m r) -> p m r", r=dd), ALU.mult,
                       [buf("numS%d" % g), buf("zsum_q%d" % q)], [buf("abT_c%d" % (3 * g + s))], eng=("pool" if g < 2 else "dve"))

    tpb = [banks[2 + i].bitcast(BF16) for i in range(4)]

    def S1(si, k):
        xi = k % 2
        xt = xst[xi]
        xB = buf("xst%d" % xi)
        dma("sp", xt, x_d[si, k * 128:(k + 1) * 128, :], [], [xB], "x%d" % xi)
        ss = stat[:, xi:xi + 1]
        rs_ = stat[:, 2 + xi:3 + xi]
        act(junk, xt, ACTF.Square, [xB], [buf("ss%d" % xi), buf("junk")], accum_out=ss)
        act(rs_, ss, ACTF.Sqrt, [buf("ss%d" % xi), buf("statinit")], [buf("rs%d" % xi)], scale=1.0 / D, bias=epsc)
        P.op("dve", "reciprocal", dict(out=rs_, in_=rs_), [buf("rs%d" % xi)], [buf("rs%d" % xi)])
        ts(xsb[xi], xt, rs_, None, ALU.mult, None, [xB, buf("rs%d" % xi)], [buf("xsb%d" % xi)])

    def S2(si, k):
        xi = k % 2
        t4 = k % 4
        xb16 = xsb[xi]
        for kc in range(8):
            bk = 2 + kc // 2
            dst = tpb[kc // 2][:, (kc % 2) * 512 + t4 * 128: (kc % 2) * 512 + (t4 + 1) * 128]
            P.op("pe", "transpose", dict(out=dst, in_=xb16[:, kc * 128:(kc + 1) * 128], identity=ident),
                 [buf("xsb%d" % xi), buf("ident")], [bankB[bk]])
        if t4 == 3:
            grp = k // 4
            for kc in range(8):
                bk = 2 + kc // 2
                src = tpb[kc // 2][:, (kc % 2) * 512:(kc % 2 + 1) * 512]
                dst = hnT[:, kc, grp * 512:(grp + 1) * 512]
                if bk < 4:
                    act(dst, src, ACTF.Copy, [bankB[bk], buf("gpre")], [buf("hnT")], scale=gpre[:, kc:kc + 1])
                else:
                    ts(dst, src, gpre[:, kc:kc + 1], None, ALU.mult, None, [bankB[bk], buf("gpre")], [buf("hnT")])

    def stageF(si, wslot0):
        mrgT = [qbuf[0], qbuf[1], ubuf[0], ubuf[1], kbuf[0][:, 0:T], kbuf[1][:, 0:T], ebuf[0][:, 0:T], ebuf[1][:, 0:T]]
        mB = [buf(nm) for nm in ("q0", "q1", "u0", "u1", "k0", "k1", "e0", "e1")]
        wslot = wslot0
        for mo in range(8):
            dma("sp", wbuf[wslot], wf_s[mo], [buf("wf_s")], [buf("w%d" % wslot)], "w%d" % wslot)
            wv = wbuf[wslot].rearrange("p (k c) -> p k c", k=32)
            wB = buf("w%d" % wslot)
            for tt_ in range(4):
                tsl = slice(tt_ * 512, (tt_ + 1) * 512)
                pbk = [0, 1, 2, 3] if (tt_ % 2 == 0) else [4, 5, 6, 7]
                ya, yb, ga, gb = [banks[i] for i in pbk]
                for kc in range(9):
                    mm(ya, wv[:, kc, :], abT[:, kc, tsl], kc == 0, kc == 8, [wB, buf("abT_c%d" % kc)], [bankB[pbk[0]]])
                for kc in range(7):
                    mm(yb, wv[:, 9 + kc, :], abT[:, 9 + kc, tsl], kc == 0, kc == 6, [wB, buf("abT_c%d" % (9 + kc))], [bankB[pbk[1]]])
                for kc in range(8):
                    mm(ga, wv[:, 16 + kc, :], hnT[:, kc, tsl], kc == 0, kc == 7, [wB, buf("hnT")], [bankB[pbk[2]]])
                for kc in range(8):
                    mm(gb, wv[:, 24 + kc, :], hnT[:, kc, tsl], kc == 0, kc == 7, [wB, buf("hnT")], [bankB[pbk[3]]])
                ta, tb_ = tmpf[0], tmpf[1]
                m1, m2 = tmpf[2], tmpf[3]
                act(ta, ga, ACTF.Tanh, [bankB[pbk[2]], buf("bgate")], [buf("tf0")], scale=0.5, bias=bgate[:, mo:mo + 1])
                act(tb_, gb, ACTF.Tanh, [bankB[pbk[3]], buf("bgate")], [buf("tf1")], scale=0.5, bias=bgate[:, 8 + mo:9 + mo])
                stt(m1, ta, 1.0, ya, ALU.add, ALU.mult, [buf("tf0"), bankB[pbk[0]]], [buf("tf2")])
                stt(m2, tb_, 1.0, yb, ALU.add, ALU.mult, [buf("tf1"), bankB[pbk[1]]], [buf("tf3")])
                tt(mrgT[mo][:, tsl], m1, m2, ALU.add, [buf("tf2"), buf("tf3")], [mB[mo]])
            wslot ^= 1
        return wslot, mrgT, mB

    xrb = [xr0, xr1]
    xrB = [[buf("xr0")], [buf("tf2"), buf("tf3")]]
    pb2s = [[0, 1], [6, 7]]

    def T0():
        for h in range(2):
            dma("sp", wbuf[h], wo_s[h], [buf("wo_s")], [buf("w%d" % h)], "w%d" % h)

    def T1(si, t16, mrgT, mB):
        wo = [wbuf[h].rearrange("p (k c) -> p k c", k=8) for h in range(2)]
        pb2 = pb2s[t16 % 2]
        yi = t16 % 2
        dma("pool", xrb[yi], x_d[si, t16 * 128:(t16 + 1) * 128, :], [], xrB[yi], "xr%d" % yi)
        for h in range(2):
            for kc in range(8):
                mm(banks[pb2[h]], mrgT[kc][:, t16 * 128:(t16 + 1) * 128], wo[h][:, kc, :], kc == 0, kc == 7,
                   [mB[kc], buf("w%d" % h)], [bankB[pb2[h]]])

    def T2(si, t16):
        pb2 = pb2s[t16 % 2]
        yi = t16 % 2
        ss2 = stat[:, 8 + 2 * yi: 10 + 2 * yi]
        for h in range(2):
            act(junk[:, 0:512], banks[pb2[h]], ACTF.Square, [bankB[pb2[h]]],
                [buf("ssF%d_%d" % (yi, h)), buf("junk")], accum_out=ss2[:, h:h + 1])
        rr = stat[:, 16 + yi:17 + yi]
        tt(rr, ss2[:, 0:1], ss2[:, 1:2], ALU.add, [buf("ssF%d_0" % yi), buf("ssF%d_1" % yi)], [buf("rr%d" % yi)])
        act(rr, rr, ACTF.Sqrt, [buf("rr%d" % yi), buf("statinit")], [buf("rr%d" % yi)], scale=0.25 / D, bias=epsc)
        P.op("dve", "reciprocal", dict(out=rr, in_=rr), [buf("rr%d" % yi)], [buf("rr%d" % yi)])

    def T3(si, t16):
        pb2 = pb2s[t16 % 2]
        yi = t16 % 2
        yt = yst[yi]
        xr = xrb[yi]
        rr = stat[:, 16 + yi:17 + yi]
        for h in range(2):
            stt(yt[:, h * 512:(h + 1) * 512], banks[pb2[h]], rr, gpost[:, h * 512:(h + 1) * 512], ALU.mult, ALU.mult,
                [bankB[pb2[h]], buf("rr%d" % yi), buf("gpost")], [buf("yst%d" % yi)])
        tt(yt, yt, xr, ALU.add, [buf("yst%d" % yi)] + xrB[yi], [buf("yst%d" % yi)])
        dma("pool", y_d[si, t16 * 128:(t16 + 1) * 128, :], yt, [buf("yst%d" % yi)], [buf("ydram")], "ys%d" % yi)

    def boundary(si, mrgT, mB, with_tail, with_s0):
        if with_tail:
            T0()
        for i in range(-1, 16):
            if i + 1 < 16:
                if with_tail:
                    T1(si, i + 1, mrgT, mB)
                if with_s0:
                    S1(si + 1, i + 1)
            if i >= 0:
                if with_tail:
                    T2(si, i)
                if with_s0:
                    S2(si + 1, i)
                if with_tail:
                    T3(si, i)

    def drain(gen):
        for _ in gen:
            pass

    def interleave(main, filler, nunits, nfill):
        fdone = filler is None
        ui = 0
        for _ in main:
            if not fdone:
                k = ((ui + 1) * nfill) // nunits - (ui * nfill) // nunits
                for _k in range(k):
                    try:
                        next(filler)
                    except StopIteration:
                        fdone = True
                        break
            ui += 1
        if not fdone:
            drain(filler)

    def roundrobin(g1, g2):
        d1 = d2 = False
        while not (d1 and d2):
            if not d1:
                try:
                    next(g1)
                except StopIteration:
                    d1 = True
            if not d2:
                try:
                    next(g2)
                except StopIteration:
                    d2 = True

    wslot = 0
    pending = [None]
    btg = btab_gen()
    bt_done = [False]

    def bt_step(k=1):
        for _ in range(k):
            if bt_done[0]:
                return
            try:
                next(btg)
            except StopIteration:
                bt_done[0] = True

    boundary(-1, None, None, False, True)
    for si in range(nseq):
        bs = 0
        load_chunk_consts(order[0], wslot)
        drain(inproj_gen(order[0], bs, wslot))
        for oi, n in enumerate(order):
            kind, d, g = chunk_geom(n)
            nxt = order[oi + 1] if oi + 1 < len(order) else None
            eslot = wslot
            filler = None
            if si == 0:
                if oi + 2 < len(order):
                    emit_casts(oi + 2, oi + 3, [buf("v%d" % bs)])
                if oi == 8:
                    dma("pool", wf_s, wf_d, [buf("v%d" % bs)], [buf("wf_s")], "pp5")
                    dma("pool", wo_s, wo_d, [buf("v%d" % bs)], [buf("wo_s")], "pp6")
            if n == NCH_A and not bt_done[0]:
                bt_step(1000)
            if nxt is not None:
                load_chunk_consts(nxt, wslot ^ 1)
                filler = inproj_gen(nxt, bs ^ 1, wslot ^ 1)
            if kind == "A01":
                main = att_A01(n, bs, eslot)
                nunits = (17 if d == 1 else 20)
            elif kind == "A2":
                main = att_A2(n, bs, eslot)
                nunits = 8
            else:
                main = att_B(n, bs, eslot)
                nunits = 32
            nfill = 24 + (0 if nxt is None else sum(2 if len(g_) > 2 else 1 for g_ in vgroups(nxt)))
            if si == 0 and not bt_done[0]:
                def main_bt(m):
                    for _ in m:
                        bt_step(1)
                        yield
                main = main_bt(main)
            if pending[0] is not None:
                def main_pp(m, pg):
                    for _ in m:
                        try:
                            next(pg)
                        except StopIteration:
                            pass
                        yield
                    for _ in pg:
                        pass
                main = main_pp(main, pending[0])
                pending[0] = None
            interleave(main, filler, nunits, nfill)
            if n < NCH_A and n // 3 == 2:
                pending[0] = post_A_gen(n % 3)
            bs ^= 1
            wslot ^= 1
        wslot, mrgT_, mB_ = stageF(si, wslot)
        boundary(si, mrgT_, mB_, True, si + 1 < nseq)

    P.emit()
    return nc


_ALIAS = {}


def _prep_weights(w_in, w_proj_a, w_proj_b, w_out):
    w_in = np.asarray(w_in, np.float32)[0]
    cols = {}
    offs = [0, WA, 2 * WA, 3 * WA, 4 * WA, 4 * WA + WB, 4 * WA + 2 * WB, 4 * WA + 3 * WB, 4 * WA + 4 * WB,
            4 * WA + 4 * WB + D]
    qa, ka, va, za, qb, kb, vb, zb, ga, gb = offs

    def tile_fm(c0):
        return w_in[:, c0:c0 + 128].reshape(8, 128, 128).transpose(1, 0, 2)

    wch = np.zeros((NCH, 128, 4, 8, 128), np.float32)
    for n in range(NCH):
        if n < NCH_A:
            c = n * 128
            mats = (qa + c, ka + c, za + c, va + c)
        else:
            c = (n - NCH_A) * 128
            mats = (qb + c, kb + c, zb + c, vb + c)
        for mi, c0 in enumerate(mats):
            wch[n, :, mi] = tile_fm(c0)
    wch = wch.reshape(NCH, 128, 4096)
    wa = np.asarray(w_proj_a, np.float32)[0]
    wb = np.asarray(w_proj_b, np.float32)[0]
    wf = np.zeros((8, 128, 32, 128), np.float32)
    for mo in range(8):
        ms = slice(mo * 128, (mo + 1) * 128)
        wf[mo, :, 0:9] = wa[:, ms].reshape(9, 128, 128).transpose(1, 0, 2)
        wf[mo, :, 9:16] = wb[:, ms].reshape(7, 128, 128).transpose(1, 0, 2)
        wf[mo, :, 16:24] = tile_fm(ga + mo * 128)
        wf[mo, :, 24:32] = tile_fm(gb + mo * 128)
    wf = wf.reshape(8, 128, 4096)
    wo_ = np.asarray(w_out, np.float32)[0]
    wo = np.zeros((2, 128, 8, 512), np.float32)
    for h in range(2):
        wo[h] = wo_[:, h * 512:(h + 1) * 512].reshape(8, 128, 512).transpose(1, 0, 2)
    wo = wo.reshape(2, 128, 4096)
    return np.ascontiguousarray(wch), np.ascontiguousarray(wf), np.ascontiguousarray(wo)


_NC_CACHE = {}


def run_layer(x_cores, norm_pre, w_in, b_gate, rpb, w_proj_a, w_proj_b, w_out, norm_post, core_ids=None):
    nseq = x_cores[0].shape[0]
    if nseq not in _NC_CACHE:
        _NC_CACHE[nseq] = build(nseq)
    nc = _NC_CACHE[nseq]
    wch, wf, wo = _prep_weights(w_in, w_proj_a, w_proj_b, w_out)
    eta = _alibi_tables()
    bbias, bmask, _ = _b_tables(np.asarray(rpb, np.float32)[0])
    gpre = np.ascontiguousarray(np.asarray(norm_pre, np.float32)[0].reshape(8, 128).T)
    bg = np.asarray(b_gate, np.float32)[0]
    bgl = np.ascontiguousarray(bg.reshape(2, 8, 128).transpose(2, 0, 1).reshape(128, 16))
    gpost = np.ascontiguousarray(np.broadcast_to(np.asarray(norm_post, np.float32)[0][None, :], (128, D)))
    ident = np.eye(128, dtype=np.float32)
    common = {"wch": wch, "wf": wf, "wo": wo, "eta": eta, "bbias": bbias, "bmask": bmask, "gpre": gpre,
              "bgate": bgl, "gpost": gpost, "ident": ident}
    in_maps = []
    for xc in x_cores:
        m = dict(common)
        m["x"] = np.ascontiguousarray(xc, dtype=np.float32)
        in_maps.append(m)
    if core_ids is None:
        core_ids = list(range(len(x_cores)))
    res = run_bass_kernel_spmd(nc, in_maps, core_ids=core_ids)
    return [r["y"] for r in res.results]


def kernel(x_prompt, x_sample, norm_pre, w_in, b_gate, rpb, w_proj_a, w_proj_b, w_out, norm_post):
    xp = np.asarray(x_prompt, np.float32)
    xs = np.asarray(x_sample, np.float32)
    x_cores = []
    for c in range(NCORES):
        x_cores.append(np.concatenate([xp[2 * c:2 * c + 2], xs[4 * c:4 * c + 4]], axis=0))
    ys = run_layer(x_cores, norm_pre, w_in, b_gate, rpb, w_proj_a, w_proj_b, w_out, norm_post)
    yp = np.concatenate([y[0:2] for y in ys], axis=0)
    ysm = np.concatenate([y[2:6] for y in ys], axis=0)
    return (yp.astype(np.float32), ysm.astype(np.float32))
```

```python
import numpy as np
import concourse.bass as bass
import concourse.mybir as mybir
from concourse.bass_utils import run_bass_kernel_spmd

F32 = mybir.dt.float32
BF16 = mybir.dt.bfloat16
ALU = mybir.AluOpType
ACTF = mybir.ActivationFunctionType

T = 2048
D = 1024
NCORES = 8
HEAD = 64
WA = 1152
WB = 896
NCH_A = 9
NCH_B = 7
NCH = 16
EPS = 1e-6
DIL = (1, 4, 16)
TBL_IDX = {("e", 3): 0, ("e", 2): 1, ("i", 2): 2, ("i", 1): 3, ("i", 0): 4, ("i", -1): 5, ("i", -2): 6,
           ("e", -2): 7, ("e", -3): 8}


class Buf:
    __slots__ = ("name", "last_w", "readers")

    def __init__(self, name=""):
        self.name = name
        self.last_w = None
        self.readers = []


class Op:
    __slots__ = ("eng", "fn", "deps", "flag", "val", "sem", "is_dma")

    def __init__(self, eng, fn, is_dma=False):
        self.eng = eng
        self.fn = fn
        self.deps = []
        self.flag = False
        self.val = None
        self.sem = None
        self.is_dma = is_dma


class Prog:
    ENGS = ("pe", "act", "dve", "pool", "sp")

    def __init__(self, nc):
        self.nc = nc
        self.ops = {e: [] for e in self.ENGS}
        self.all_ops = []

    def _add_dep(self, op, prod):
        if prod is None or prod is op:
            return
        if prod.eng == "pe" and op.eng == "pe" and not prod.is_dma and not op.is_dma:
            return
        if prod not in op.deps:
            op.deps.append(prod)
            prod.flag = True

    def op(self, eng, meth, kw, reads=(), writes=(), dma_key=None):
        fn = (lambda e, meth=meth, kw=kw: getattr(e, meth)(**kw))
        reads = [b for x in reads for b in (x if isinstance(x, (list, tuple)) else [x])]
        writes = [b for x in writes for b in (x if isinstance(x, (list, tuple)) else [x])]
        o = Op(eng, fn, is_dma=dma_key is not None)
        if dma_key is not None:
            o.sem = dma_key
            o.flag = True
        for b in reads:
            self._add_dep(o, b.last_w)
        for b in writes:
            self._add_dep(o, b.last_w)
            for r in b.readers:
                self._add_dep(o, r)
        for b in reads:
            b.readers.append(o)
        for b in writes:
            b.last_w = o
            b.readers = []
        self.ops[eng].append(o)
        self.all_ops.append(o)
        return o

    def emit(self):
        nc = self.nc
        eng_sem = {e: nc.alloc_semaphore("es_" + e) for e in self.ENGS}
        keys = []
        for e in self.ENGS:
            for o in self.ops[e]:
                if o.is_dma and o.sem not in keys:
                    keys.append(o.sem)
        dsem = {k: [nc.alloc_semaphore("ds_%d" % i), 0] for i, k in enumerate(keys)}
        for e in self.ENGS:
            cnt = 0
            for o in self.ops[e]:
                if o.is_dma:
                    d = dsem[o.sem]
                    d[1] += 16
                    o.val = d[1]
                    o.sem = d[0]
                elif o.flag:
                    cnt += 1
                    o.val = cnt
                    o.sem = eng_sem[e]
        all_dma_final = [(d[0], d[1]) for d in dsem.values()]
        known_e = {e: {} for e in self.ENGS}
        kn_of = {}
        waits_of = {}
        for o in self.all_ops:
            kn = known_e[o.eng]
            ws = []
            for p in o.deps:
                sid = p.sem.num
                if kn.get(sid, 0) < p.val:
                    ws.append((p.sem, p.val))
                    kn[sid] = p.val
                pk = kn_of.get(id(p))
                if pk is not None:
                    for k_, v_ in pk.items():
                        if kn.get(k_, 0) < v_:
                            kn[k_] = v_
            waits_of[id(o)] = ws
            if o.flag:
                snap = dict(kn)
                snap[o.sem.num] = max(snap.get(o.sem.num, 0), o.val)
                kn_of[id(o)] = snap
        with nc.Block() as block:
            def run(e):
                def body(eng):
                    known = known_e[e]
                    for o in self.ops[e]:
                        for (ws_, wv_) in waits_of[id(o)]:
                            eng.wait_ge(ws_, wv_)
                        ins = o.fn(eng)
                        if o.is_dma:
                            ins.then_inc(o.sem, 16)
                        elif o.flag:
                            ins.then_inc(o.sem, 1)
                    if e == "sp":
                        for (s, v) in all_dma_final:
                            if known.get(s.num, 0) < v:
                                eng.wait_ge(s, v)
                return body
            block.tensor(run("pe"))
            block.scalar(run("act"))
            block.vector(run("dve"))
            block.gpsimd(run("pool"))
            block.sync(run("sp"))


NEG = -240000.0


def _alibi_tables():
    slopes = (2.0 ** (-8.0 * np.arange(1, 19) / 18)).astype(np.float64)
    out = np.full((NCH_A, 128, 1024), NEG, np.float32)
    j = np.arange(128)[:, None].astype(np.float64)
    i = np.arange(128)[None, :].astype(np.float64)
    for c in range(NCH_A):
        g, s = c // 3, c % 3
        d = DIL[g]
        for hh in range(2):
            sl = float(slopes[g + 3 * (2 * s + hh)])
            if g < 2:
                bh = np.where(j <= i, -8.0 * sl * d * np.abs(i - j - 64.0), NEG)
                ah = np.where(j >= i, -8.0 * sl * d * np.abs(i - j + 64.0), NEG)
                main = np.concatenate([bh, ah], 1)
                bh2 = bh.copy(); bh2[64:, :] = NEG
                ah2 = ah.copy(); ah2[:64, :] = NEG
                bnd = np.concatenate([bh2, ah2], 1)
                out[c, :, hh * 256:(hh + 1) * 256] = main
                out[c, :, 512 + hh * 256:512 + (hh + 1) * 256] = bnd
            else:
                e2 = np.where(np.abs(i - j) <= 64, -8.0 * sl * d * np.abs(i - j), NEG)
                out[c, :, hh * 512:(hh + 1) * 512] = np.tile(e2, (1, 4))
    return np.maximum(out, NEG).astype(np.float32)


def _b_pairs():
    rows = 32
    rs = np.clip(np.arange(rows) - 4, 0, rows - 8)
    pairs = {}
    for u in range(16):
        for v in range(16):
            pat = np.zeros((2, 2), bool)
            for rl_k in range(2):
                for rl_q in range(2):
                    r = 2 * v + rl_q
                    rp = 2 * u + rl_k
                    pat[rl_k, rl_q] = (rs[r] <= rp < rs[r] + 8)
            if pat.any():
                pairs[(u, v)] = pat
    return pairs


def _b_tables(rpb):
    pairs = _b_pairs()
    tbl_pat = {}
    tbl_delta = {}
    pair_tbl = {}
    for (u, v), pat in pairs.items():
        dl = u - v
        ipat = np.zeros((2, 2), bool)
        for a in range(2):
            for b in range(2):
                dr = 2 * dl + a - b
                ipat[a, b] = (-4 <= dr <= 3)
        if abs(dl) <= 2 and (pat == ipat).all():
            key = ("i", dl)
        else:
            assert pat.all(), (u, v, pat)
            key = ("e", dl)
        assert key in TBL_IDX, key
        if key in tbl_pat:
            assert (tbl_pat[key] == pat).all()
        tbl_pat[key] = pat
        tbl_delta[key] = dl
        pair_tbl[(u, v)] = TBL_IDX[key]
    assert len(tbl_pat) == 9
    cp = np.arange(64)[:, None]
    cq = np.arange(64)[None, :]
    cs = np.clip(cq - 8, 0, 48)
    colvalid = (cp >= cs) & (cp < cs + 16)
    dc = np.clip(cp - cq, -15, 15) + 15
    bias = np.zeros((14, 9, 128, 128), np.float32)
    mask = np.zeros((9, 128, 128), np.float32)
    for key, idx in TBL_IDX.items():
        pat = tbl_pat[key]
        dl = tbl_delta[key]
        for a in range(2):
            for b in range(2):
                if not pat[a, b]:
                    continue
                dr = 2 * dl + a - b + 7
                assert 0 <= dr <= 14
                mask[idx, a * 64:(a + 1) * 64, b * 64:(b + 1) * 64] = colvalid
                bias[:, idx, a * 64:(a + 1) * 64, b * 64:(b + 1) * 64] = rpb[:, dr][:, dc]
    bias = bias.reshape(7, 2, 9, 128, 128).transpose(0, 3, 1, 2, 4).reshape(7, 128, 2304)
    mask = (mask - 1.0) * (-NEG)
    maskf = np.broadcast_to(mask[None, None], (7, 2, 9, 128, 128)).transpose(0, 3, 1, 2, 4).reshape(7, 128, 2304)
    return np.ascontiguousarray(bias), np.ascontiguousarray(maskf), pair_tbl


def build(nseq, nchunks=NCH):
    nc = bass.Bass("TRN2", target_bir_lowering=False)
    P = Prog(nc)
    pair_tbl = _b_tables(np.zeros((14, 15, 31), np.float32))[2]
    pairs = sorted(pair_tbl.keys())
    v_of_u = {u: [v for (uu, v) in pairs if uu == u] for u in range(16)}
    for u in range(16):
        assert v_of_u[u] == list(range(v_of_u[u][0], v_of_u[u][-1] + 1))
    first_u = {v: min(u for (u, vv) in pairs if vv == v) for v in range(16)}
    last_u = {v: max(u for (u, vv) in pairs if vv == v) for v in range(16)}

    def din(name, shape, dt=F32):
        return nc.dram_tensor(name, list(shape), dt, kind="ExternalInput").ap()

    x_d = din("x", [nseq, T, D])
    y_d = nc.dram_tensor("y", [nseq, T, D], F32, kind="ExternalOutput").ap()
    wch_d = din("wch", [NCH, 128, 4096])
    wf_d = din("wf", [8, 128, 4096])
    wo_d = din("wo", [2, 128, 4096])
    eta_d = din("eta", [NCH_A, 128, 1024])
    bb_d = din("bbias", [NCH_B, 128, 2304])
    bm_d = din("bmask", [NCH_B, 128, 2304])
    gpre_d = din("gpre", [128, 8])
    bg_d = din("bgate", [128, 16])
    gpost_d = din("gpost", [128, D])
    id_d = din("ident", [128, 128])
    wch_s = nc.dram_tensor("wch_s", [NCH, 128, 4096], BF16, kind="Internal").ap()
    wf_s = nc.dram_tensor("wf_s", [8, 128, 4096], BF16, kind="Internal").ap()
    wo_s = nc.dram_tensor("wo_s", [2, 128, 4096], BF16, kind="Internal").ap()
    eta_s = nc.dram_tensor("eta_s", [NCH_A, 128, 1024], BF16, kind="Internal").ap()
    etb_s = nc.dram_tensor("etb_s", [NCH_B, 128, 2304], BF16, kind="Internal").ap()

    def sb(name, shape, dt):
        return nc.alloc_sbuf_tensor(name, list(shape), dt).ap()

    hnT = sb("hnT", [128, 8, T], BF16)
    abT = sb("abT", [128, NCH, T], BF16)
    qbuf = [sb("q%d" % i, [128, T], BF16) for i in range(2)]
    kbuf = [sb("k%d" % i, [128, 2560], BF16) for i in range(2)]
    vbuf = [sb("v%d" % i, [128, 20, 192], BF16) for i in range(2)]
    ubuf = [sb("u%d" % i, [128, T], BF16) for i in range(2)]
    scr = sb("scr", [128, 5120], F32)
    wbuf = [sb("w%d" % i, [128, 4096], BF16) for i in range(2)]
    ebuf = [sb("e%d" % i, [128, 2304], BF16) for i in range(2)]
    xr0 = sb("xr0", [128, 1024], F32)
    pbuf = [sb("p%d" % i, [128, 1024], BF16) for i in range(2)]
    tmpf_all = sb("tfall", [128, 2048], F32)
    tmpf = [tmpf_all[:, i * 512:(i + 1) * 512] for i in range(4)]
    xr1 = tmpf_all[:, 1024:2048]
    ident = sb("identb", [128, 128], BF16)
    identf = sb("identf", [128, 128], F32)
    gpre = sb("gpre_s", [128, 8], F32)
    bgate = sb("bgate_s", [128, 16], F32)
    gpost = sb("gpost_s", [128, D], F32)
    stat = sb("stat", [128, 64], F32)
    junk = sb("junk", [128, D], BF16)
    junk2 = sb("junk2", [128, D], BF16)

    scr_bf = scr.bitcast(BF16)
    numS = [scr_bf[:, g * T:(g + 1) * T] for g in range(3)]
    zsum = scr[:, 3072:5120]
    xst = [scr[:, 0:1024], scr[:, 1024:2048]]
    xsb = [scr_bf[:, 4096:5120], scr_bf[:, 5120:6144]]
    yst = [scr[:, 3072:4096], scr[:, 4096:5120]]

    ps_all = nc.alloc_psum_tensor("ps_all", [128, 4096], F32).ap()
    banks = [ps_all[:, i * 512:(i + 1) * 512] for i in range(8)]
    bankB = [Buf("bank%d" % i) for i in range(8)]

    B = {}
    RQ = {nm: [Buf("%s_q%d" % (nm, q)) for q in range(4)] for nm in ("numS0", "numS1", "numS2", "zsum")}
    ALIAS = {"xst0": RQ["numS0"], "xst1": RQ["numS1"], "xsb0": RQ["numS2"][0:2], "xsb1": RQ["numS2"][2:4],
             "yst0": RQ["zsum"][0:2], "yst1": RQ["zsum"][2:4]}
    PS_ = [[Buf("p%d_%d" % (i, j)) for j in range(2)] for i in range(2)]
    for i in range(2):
        ALIAS["p%d" % i] = PS_[i]
        for j in range(2):
            ALIAS["p%d_%d" % (i, j)] = [PS_[i][j]]
    for nm in ("numS0", "numS1", "numS2", "zsum"):
        ALIAS[nm] = RQ[nm]
        for q in range(4):
            ALIAS["%s_q%d" % (nm, q)] = [RQ[nm][q]]

    def buf(name):
        if name in ALIAS:
            return ALIAS[name]
        if name not in B:
            B[name] = Buf(name)
        return B[name]

    def mm(out, lhsT, rhs, start, stop, reads, writes, skip=False):
        kw = dict(out=out, lhsT=lhsT, rhs=rhs, start=start, stop=stop)
        if skip:
            kw["skip_group_check"] = True
        P.op("pe", "matmul", kw, reads, writes)

    def act(out, in_, func, reads, writes, **kw):
        P.op("act", "activation", dict(out=out, in_=in_, func=func, **kw), reads, writes)

    def tt(out, in0, in1, op, reads, writes, eng="dve"):
        P.op(eng, "tensor_tensor", dict(out=out, in0=in0, in1=in1, op=op), reads, writes)

    def stt(out, in0, scalar, in1, op0, op1, reads, writes, eng="dve"):
        P.op(eng, "scalar_tensor_tensor", dict(out=out, in0=in0, scalar=scalar, in1=in1, op0=op0, op1=op1), reads, writes)

    def ts(out, in0, s1, s2, op0, op1, reads, writes, eng="dve"):
        kw = dict(out=out, in0=in0, scalar1=s1, scalar2=s2, op0=op0)
        if op1 is not None:
            kw["op1"] = op1
        P.op(eng, "tensor_scalar", kw, reads, writes)

    def cp(out, in_, reads, writes, eng="dve"):
        P.op(eng, "tensor_copy", dict(out=out, in_=in_), reads, writes)

    def dma(eng, out, in_, reads, writes, key):
        P.op(eng, "dma_start", dict(out=out, in_=in_), reads, writes, dma_key=key)

    cast_order = [0, 3, 6, 1, 4, 7, 2, 5, 8] + list(range(NCH_A, NCH))

    def emit_casts(lo, hi, gate):
        for n_ in cast_order[lo:hi]:
            if n_ < NCH_A:
                dma("pool", eta_s[n_], eta_d[n_], gate, [buf("eta_c%d" % n_)], "ppe%d" % n_)
            dma("pool", wch_s[n_], wch_d[n_], gate, [buf("wch_c%d" % n_)], "ppc%d" % n_)

    emit_casts(0, 2, [])
    dma("sp", identf, id_d, [], [buf("identf")], "c0")
    dma("sp", gpre, gpre_d, [], [buf("gpre")], "c1")
    dma("sp", bgate, bg_d, [], [buf("bgate")], "c2")
    dma("sp", gpost, gpost_d, [], [buf("gpost")], "c3")
    cp(ident, identf, [buf("identf")], [buf("ident")])
    ts(bgate, bgate, 0.5, None, ALU.mult, None, [buf("bgate")], [buf("bgate")])
    for i in range(2):
        P.op("dve", "memset", dict(ap=kbuf[i], constant=0.0), [], [buf("k%d" % i)])
        P.op("dve", "memset", dict(ap=vbuf[i], constant=0.0), [], [buf("v%d" % i)])
        P.op("dve", "memset", dict(ap=vbuf[i][:, :, 64:128], constant=1.0), [], [buf("v%d" % i)])
        P.op("dve", "memset", dict(ap=qbuf[i], constant=0.0), [], [buf("q%d" % i)])
    epsc = stat[:, 40:41]
    P.op("dve", "memset", dict(ap=epsc, constant=EPS), [], [buf("statinit")])
    ts(gpost, gpost, 0.5, None, ALU.mult, None, [buf("gpost")], [buf("gpost")])
    abhi = abT[:, 9:16, :].rearrange("p c t -> p (c t)")
    abhi_f = abhi.bitcast(F32)
    bt_stage = []
    for i in range(2):
        o = i * 2880
        bt_stage.append((abhi_f[:, o:o + 1152], abhi_f[:, o + 1152:o + 2304], abhi[:, 2 * (o + 2304): 2 * (o + 2304) + 1152]))

    def btab_gen():
        items = [(c, h2) for c in range(NCH_B) for h2 in range(2)]

        def loads(k):
            c, h2 = items[k]
            sl = slice(h2 * 1152, (h2 + 1) * 1152)
            stg, stg2, _ = bt_stage[k % 2]
            dma("sp", stg, bb_d[c, :, sl], [], [buf("btA%d" % (k % 2))], "c4%d" % (k % 2))
            dma("sp", stg2, bm_d[c, :, sl], [], [buf("btB%d" % (k % 2))], "c5%d" % (k % 2))

        loads(0)
        for k in range(len(items)):
            c, h2 = items[k]
            sl = slice(h2 * 1152, (h2 + 1) * 1152)
            if k + 1 < len(items):
                loads(k + 1)
            yield
            stg, stg2, so = bt_stage[k % 2]
            stt(so, stg, 8.0, stg2, ALU.mult, ALU.add, [buf("btA%d" % (k % 2)), buf("btB%d" % (k % 2))], [buf("btO%d" % (k % 2))])
            dma("sp", etb_s[c, :, sl], so, [buf("btO%d" % (k % 2))], [buf("etb_s%d" % (k % 2))], "c6%d" % (k % 2))
            yield

    state = {"ipb": 0, "x": 0}

    def load_chunk_consts(n, slot):
        dma("sp", wbuf[slot], wch_s[n], [buf("wch_c%d" % n)], [buf("w%d" % slot)], "w%d" % slot)
        if n < NCH_A:
            dma("sp", ebuf[slot][:, 0:1024], eta_s[n], [buf("eta_c%d" % n)], [buf("e%d" % slot)], "e%d" % slot)
        else:
            dma("sp", ebuf[slot], etb_s[n - NCH_A], [buf("etb_s0"), buf("etb_s1")], [buf("e%d" % slot)], "e%d" % slot)

    def chunk_geom(n):
        if n < NCH_A:
            g = n // 3
            return ("A2" if g == 2 else "A01"), DIL[g], g
        return "B", 1, 3

    order = []
    for s in range(3):
        for g in range(3):
            order.append(3 * g + s)
    order += list(range(NCH_A, NCH))
    order = order[:nchunks] if nchunks < NCH else order

    def vslots(n):
        kind, d, g = chunk_geom(n)
        res = []
        if kind == "A01":
            L = T // d
            nq = L // 128
            for r in range(d):
                for kb in range(nq + 1):
                    m0 = 128 * kb - 64
                    lo = max(m0, 0)
                    hi = min(m0 + 128, L)
                    res.append((r * (nq + 1) + kb, lo * d + r, d, hi - lo, lo - m0))
        elif kind == "A2":
            for r in range(16):
                res.append((r, r, 16, 128, 0))
        else:
            for u in range(16):
                res.append((u, 128 * u, 1, 128, 0))
        return res

    def vgroups(n):
        sl = vslots(n)
        res = []
        i = 0
        while i < len(sl):
            grp = [sl[i]]
            if sl[i][3] == 128:
                while len(grp) < 4 and i + len(grp) < len(sl) and sl[i + len(grp)][3] == 128 \
                        and sl[i + len(grp)][0] == grp[-1][0] + 1:
                    grp.append(sl[i + len(grp)])
            i += len(grp)
            res.append(grp)
        return res

    def inproj_gen(n, bs, wslot):
        kind, d, g = chunk_geom(n)
        w4 = wbuf[wslot].rearrange("p (m k c) -> p m k c", m=4, k=8)
        wB = buf("w%d" % wslot)
        hB = buf("hnT")
        qB, kB, vB, uB = buf("q%d" % bs), buf("k%d" % bs), buf("v%d" % bs), buf("u%d" % bs)
        qb_, kb_, vb_, ub_ = qbuf[bs], kbuf[bs], vbuf[bs], ubuf[bs]

        def acc_fm(mi, tt_):
            bi = state["ipb"]
            state["ipb"] ^= 1
            ps = banks[bi]
            for kc in range(8):
                mm(ps, w4[:, mi, kc, :], hnT[:, kc, tt_ * 512:(tt_ + 1) * 512], kc == 0, kc == 7, [wB, hB], [bankB[bi]])
                if kc == 3:
                    yield None
            yield (bi, ps)

        for tt_ in range(4):
            for r_ in acc_fm(2, tt_):
                if r_ is None:
                    yield
                else:
                    bi, ps = r_
            tf = tmpf[tt_ % 2]
            tB = buf("tf%d" % (tt_ % 2))
            act(tf, ps, ACTF.Tanh, [bankB[bi]], [tB], scale=0.5)
            stt(ub_[:, tt_ * 512:(tt_ + 1) * 512], tf, 1.0, ps, ALU.add, ALU.mult, [tB, bankB[bi]], [uB])
            yield
        for mi, dstB in ((0, qB), (1, kB)):
            for tt_ in range(4):
                for r_ in acc_fm(mi, tt_):
                    if r_ is None:
                        yield
                    else:
                        bi, ps = r_
                if (kind == "A01" and d == 1) or kind == "B":
                    off = 64 if (mi == 1 and kind == "A01") else 0
                    dst = (kb_ if mi == 1 else qb_)[:, off + tt_ * 512: off + (tt_ + 1) * 512]
                    src = ps
                elif kind == "A01":
                    if mi == 1:
                        dst = kb_.rearrange("p (r m) -> p r m", r=4)[:, :, 64 + 128 * tt_: 64 + 128 * tt_ + 128]
                    else:
                        dst = qb_.rearrange("p (r m) -> p r m", r=4)[:, :, 128 * tt_:128 * tt_ + 128]
                    src = ps.rearrange("p (j r) -> p r j", r=4)
                else:
                    base = (kb_[:, 0:T] if mi == 1 else qb_)
                    dst = base.rearrange("p (r m) -> p r m", r=16)[:, :, 32 * tt_:32 * tt_ + 32]
                    src = ps.rearrange("p (j r) -> p r j", r=16)
                act(dst, src, ACTF.Copy, [bankB[bi]], [dstB])
                yield
        for grp in vgroups(n):
            bi = state["ipb"]
            state["ipb"] ^= 1
            ps = banks[bi]
            for q, (slot, t0, step, nv, row0) in enumerate(grp):
                for kc in range(8):
                    lhs = hnT[:, kc, t0: t0 + (nv - 1) * step + 1: step]
                    mm(ps[0:nv, q * 128:(q + 1) * 128], lhs, w4[:, 3, kc, :], kc == 0, kc == 7, [wB, hB], [bankB[bi]])
                if q == 1 and len(grp) > 2:
                    yield
            slot0, _, _, nv, row0 = grp[0]
            ng = len(grp)
            if nv == 128:
                dst = bass.AP(vb_.tensor, vb_.offset + slot0 * 192, [list(vb_.ap[0]), [192, ng], [128, 2], [1, 64]])
                src = ps[:, 0:ng * 128].rearrange("p (s h c) -> p s h c", s=ng, h=2)
            else:
                dst = bass.AP(vb_.tensor, vb_[row0:row0 + 64, slot0, :].offset, [[vb_.ap[0][0], 64], [128, 2], [1, 64]])
                src = ps[0:64, 0:128].rearrange("p (h c) -> p h c", h=2)
            act(dst, src, ACTF.Copy, [bankB[bi]], [vB])
            yield

    def att_A01(n, bs, eslot):
        kind, d, g = chunk_geom(n)
        L = T // d
        nq = L // 128
        qb_, kb_, vb_, ub_ = qbuf[bs], kbuf[bs], vbuf[bs], ubuf[bs]
        qB, kB, vB, uB = buf("q%d" % bs), buf("k%d" % bs), buf("v%d" % bs), buf("u%d" % bs)
        eB = buf("e%d" % eslot)
        et = ebuf[eslot]
        units = [(r, kb) for r in range(d) for kb in range(nq + 1)]
        sring = [(2, 3), (4, 5)]
        tbank = {0: [6], 1: [7]}

        def scores(ui):
            r, kb = units[ui]
            sp_ = sring[ui % 2]
            qbase = r * L
            kbase = r * (L + 128) + 128 * kb
            if kb == 0:
                q0, nqc, o0 = qbase, 128, 128
            elif kb == nq:
                q0, nqc, o0 = qbase + 128 * (nq - 1), 128, 0
            else:
                q0, nqc, o0 = qbase + 128 * (kb - 1), 256, 0
            tsel = 512 if (kb == 0 or kb == nq) else 0
            for hh in range(2):
                mm(banks[sp_[hh]][:, 0:256], ident, et[:, tsel + hh * 256: tsel + hh * 256 + 256], True, False,
                   [buf("ident"), eB], [bankB[sp_[hh]]], skip=True)
            for hh in range(2):
                rows = slice(hh * 64, (hh + 1) * 64)
                mm(banks[sp_[hh]][:, o0: o0 + nqc], kb_[rows, kbase:kbase + 128], qb_[rows, q0:q0 + nqc],
                   False, True, [kB, qB], [bankB[sp_[hh]]], skip=True)

        def pslot(ui):
            k4 = ui % 4
            return pbuf[k4 // 2][:, (k4 % 2) * 512:(k4 % 2 + 1) * 512], buf("p%d_%d" % (k4 // 2, k4 % 2))

        def pv_first(ui):
            r, kb = units[ui]
            if kb > nq - 1:
                return
            pb, pB = pslot(ui)
            slot = r * (nq + 1) + kb
            for hh in range(2):
                lhs = vb_[:, slot, hh * 64: hh * 64 + 128]
                qb = kb
                tb = tbank[hh][0]
                mm(banks[tb][:, (qb % 4) * 128:(qb % 4 + 1) * 128], lhs, pb[:, hh * 256 + 128: hh * 256 + 256],
                   (qb % 4 == 0), False, [vB, pB], [bankB[tb]], skip=True)

        def front(ui):
            r, kb = units[ui]
            scores(ui)
            sp_ = sring[ui % 2]
            S2 = ps_all[:, sp_[0] * 512:(sp_[0] + 2) * 512].rearrange("p (h c) -> p h c", h=2)[:, :, 0:256]
            pb, pB = pslot(ui)
            act(pb.rearrange("p (h c) -> p h c", h=2), S2, ACTF.Exp, [bankB[sp_[0]], bankB[sp_[1]]], [pB], scale=0.125)

        front(0)
        for ui in range(len(units)):
            r, kb = units[ui]
            if ui + 1 < len(units):
                front(ui + 1)
            if ui >= 1:
                pv_first(ui - 1)
            pb, pB = pslot(ui)
            slot = r * (nq + 1) + kb
            grp_off = r * (nq // 4)
            for hh in range(2):
                lhs = vb_[:, slot, hh * 64: hh * 64 + 128]
                if kb >= 1:
                    qb = kb - 1
                    tb = tbank[hh][0]
                    mm(banks[tb][:, (qb % 4) * 128:(qb % 4 + 1) * 128], lhs, pb[:, hh * 256: hh * 256 + 128],
                       False, True, [vB, pB], [bankB[tb]], skip=True)
            if kb >= 4 and kb % 4 == 0:
                w = (kb - 1) // 4
                if d == 1:
                    tsl = slice(512 * w, 512 * w + 512)
                    uap, nap, zap = ub_[:, tsl], numS[g][:, tsl], zsum[:, tsl]
                    nB_, zB_ = buf("numS%d_q%d" % (g, w)), buf("zsum_q%d" % w)
                else:
                    uap, nap, zap = ub_[:, r:T:4], numS[g][:, r * 512:(r + 1) * 512], zsum[:, r:T:4]
                    nB_, zB_ = buf("numS%d" % g), buf("zsum")
                for hh in range(2):
                    tb = tbank[hh][0]
                    Tt = banks[tb]
                    nrows = slice(hh * 64, (hh + 1) * 64)
                    zrows = slice((1 - hh) * 64, (2 - hh) * 64)
                    tt(nap[nrows], Tt[nrows], uap[nrows], ALU.mult, [bankB[tb], uB], [nB_])
                    if g == 0:
                        ts(zap[nrows], Tt[zrows], 2.0, None, ALU.mult, None, [bankB[tb]], [zB_])
                    else:
                        stt(zap[nrows], Tt[zrows], 2.0, zap[nrows], ALU.mult, ALU.add, [bankB[tb], zB_], [zB_])
            if ui == len(units) - 1:
                pv_first(ui)
            yield

    def att_A2(n, bs, eslot):
        qb_, kb_, vb_, ub_ = qbuf[bs], kbuf[bs], vbuf[bs], ubuf[bs]
        qB, kB, vB, uB = buf("q%d" % bs), buf("k%d" % bs), buf("v%d" % bs), buf("u%d" % bs)
        eB = buf("e%d" % eslot)
        et = ebuf[eslot]
        units = [(r0, hh) for r0 in range(0, 16, 4) for hh in range(2)]
        sring = [2, 3]
        tbank = {0: [4, 6], 1: [5, 7]}

        def scores(ui):
            r0, hh = units[ui]
            si = sring[ui % 2]
            S = banks[si]
            rows = slice(hh * 64, (hh + 1) * 64)
            mm(S[:, 0:512], ident, et[:, hh * 512:(hh + 1) * 512], True, False,
               [buf("ident"), eB], [bankB[si]], skip=True)
            for q in range(4):
                c0 = (r0 + q) * 128
                mm(S[:, q * 128:(q + 1) * 128], kb_[rows, c0:c0 + 128], qb_[rows, c0:c0 + 128], False, True, [kB, qB], [bankB[si]], skip=True)

        def tview(ap, r0):
            return ap.rearrange("p (m r) -> p r m", r=16)[:, r0:r0 + 4, :]

        def front(ui):
            r0, hh = units[ui]
            scores(ui)
            si = sring[ui % 2]
            S = banks[si]
            pb = pbuf[ui % 2][:, 0:512]
            pB = buf("p%d" % (ui % 2))
            act(pb, S, ACTF.Exp, [bankB[si]], [pB], scale=0.125)

        front(0)
        for ui in range(len(units)):
            r0, hh = units[ui]
            if ui + 1 < len(units):
                front(ui + 1)
            pb = pbuf[ui % 2][:, 0:512]
            pB = buf("p%d" % (ui % 2))
            tb = tbank[hh][(r0 // 4) % 2]
            for q in range(4):
                lhs = vb_[:, r0 + q, hh * 64: hh * 64 + 128]
                mm(banks[tb][:, q * 128:(q + 1) * 128], lhs, pb[:, q * 128:(q + 1) * 128], (q == 0), True, [vB, pB], [bankB[tb]], skip=True)
            if hh == 1:
                for h2 in range(2):
                    tb2 = tbank[h2][(r0 // 4) % 2]
                    Tt = banks[tb2].rearrange("p (a b) -> p a b", a=4)
                    nrows = slice(h2 * 64, (h2 + 1) * 64)
                    zrows = slice((1 - h2) * 64, (2 - h2) * 64)
                    tt(numS[2][nrows, r0 * 128:(r0 + 4) * 128].rearrange("p (a b) -> p a b", a=4), Tt[nrows], tview(ub_, r0)[nrows], ALU.mult,
                       [bankB[tb2], uB], [buf("numS2")])
                    stt(tview(zsum, r0)[nrows], Tt[zrows], 2.0, tview(zsum, r0)[nrows], ALU.mult, ALU.add, [bankB[tb2], buf("zsum")], [buf("zsum")])
            yield

    def att_B(n, bs, eslot):
        cb = n
        qb_, kb_, vb_, ub_ = qbuf[bs], kbuf[bs], vbuf[bs], ubuf[bs]
        qB, kB, vB, uB = buf("q%d" % bs), buf("k%d" % bs), buf("v%d" % bs), buf("u%d" % bs)
        eB = buf("e%d" % eslot)
        et = ebuf[eslot].rearrange("p (h t i) -> p h t i", h=2, t=9)
        units = [(hh, u) for hh in range(2) for u in range(16)]
        sring = [(2, 3), (4, 5)]
        tring = [6, 7]

        def front(ui):
            hh, u = units[ui]
            sa, sb_ = sring[ui % 2]
            rows = slice(hh * 64, (hh + 1) * 64)
            vs = v_of_u[u]
            v0 = vs[0]
            nv = len(vs)
            n1 = min(nv, 4)
            tids = [pair_tbl[(u, v)] for v in vs]
            runs = []
            st = 0
            for i2 in range(1, nv + 1):
                if i2 == nv or tids[i2] != tids[i2 - 1] + 1 or i2 == 4:
                    runs.append((st, i2))
                    st = i2
            first = {sa: True, sb_: True}
            for (a, b2) in runs:
                bk = sa if a < 4 else sb_
                off = a if a < 4 else a - 4
                t0 = tids[a]
                mm(banks[bk][:, off * 128:(off + b2 - a) * 128], ident,
                   et[:, hh, t0:t0 + (b2 - a), :].rearrange("p t i -> p (t i)"), first[bk], False,
                   [buf("ident"), eB], [bankB[bk]], skip=True)
                first[bk] = False
            mm(banks[sa][:, 0:n1 * 128], kb_[rows, u * 128:(u + 1) * 128], qb_[rows, v0 * 128:(v0 + n1) * 128], False, True,
               [kB, qB], [bankB[sa]], skip=True)
            if nv > 4:
                n2 = nv - 4
                mm(banks[sb_][:, 0:n2 * 128], kb_[rows, u * 128:(u + 1) * 128], qb_[rows, (v0 + 4) * 128:(v0 + nv) * 128], False, True,
                   [kB, qB], [bankB[sb_]], skip=True)
            pb = pbuf[ui % 2]
            pB = buf("p%d" % (ui % 2))
            act(pb[:, 0:n1 * 128], banks[sa][:, 0:n1 * 128], ACTF.Exp, [bankB[sa]], [pB], scale=0.125)
            if nv > 4:
                act(pb[:, 512:nv * 128], banks[sb_][:, 0:(nv - 4) * 128], ACTF.Exp, [bankB[sb_]], [pB], scale=0.125)

        front(0)
        for ui in range(len(units)):
            hh, u = units[ui]
            if ui + 1 < len(units):
                front(ui + 1)
            vs = v_of_u[u]
            pb = pbuf[ui % 2]
            pB = buf("p%d" % (ui % 2))
            lhs = vb_[:, u, hh * 64: hh * 64 + 128]
            for vi, v in enumerate(vs):
                tb = tring[(v // 4) % 2]
                mm(banks[tb][:, (v % 4) * 128:(v % 4 + 1) * 128], lhs, pb[:, vi * 128:(vi + 1) * 128],
                   (u == first_u[v] and v % 4 == 0), u == last_u[v], [vB, pB], [bankB[tb]], skip=True)
            for w in range(4):
                if u == last_u[4 * w + 3]:
                    tb = tring[w % 2]
                    Tt = banks[tb]
                    nrows = slice(hh * 64, (hh + 1) * 64)
                    zrows = slice((1 - hh) * 64, (2 - hh) * 64)
                    tsl = slice(512 * w, 512 * w + 512)
                    nB_, zB_ = buf("numS0_q%d" % w), buf("zsum_q%d" % w)
                    tt(numS[0][nrows, tsl], Tt[nrows], ub_[nrows, tsl], ALU.mult, [bankB[tb], uB], [nB_])
                    ts(zsum[nrows, tsl], Tt[zrows], 2.0, None, ALU.mult, None, [bankB[tb]], [zB_])
                    if hh == 1:
                        P.op("dve", "reciprocal", dict(out=zsum[:, tsl], in_=zsum[:, tsl]), [zB_], [zB_])
                        tt(abT[:, cb, tsl], numS[0][:, tsl], zsum[:, tsl], ALU.mult,
                           [nB_, zB_], [buf("abT_c%d" % cb)], eng="pool")
            yield

    def post_A_gen(s):
        for q in range(4):
            yield
            tsl = slice(512 * q, 512 * q + 512)
            P.op("dve", "reciprocal", dict(out=zsum[:, tsl], in_=zsum[:, tsl]), [buf("zsum_q%d" % q)], [buf("zsum_q%d" % q)])
            for g in range(3):
                if g == 0:
                    tt(abT[:, 3 * g + s, tsl], numS[g][:, tsl], zsum[:, tsl], ALU.mult,
                       [buf("numS%d_q%d" % (g, q)), buf("zsum_q%d" % q)], [buf("abT_c%d" % (3 * g + s))], eng="pool")
                else:
                    dd = DIL[g]
                    mq = 512 // dd
                    src = numS[g].rearrange("p (r m) -> p m r", r=dd)[:, mq * q:mq * (q + 1), :]
                    tt(abT[:, 3 * g + s, tsl].rearrange("p (m r) -> p m r", r=dd), src,
                       zsum[:, tsl].rearrange("p (m r) -> p m r", r=dd), ALU.mult,
                       [buf("numS%d" % g), buf("zsum_q%d" % q)], [buf("abT_c%d" % (3 * g + s))], eng=("pool" if g < 2 else "dve"))

    tpb = [banks[2 + i].bitcast(BF16) for i in range(4)]

    def S1(si, k):
        xi = k % 2
        xt = xst[xi]
        xB = buf("xst%d" % xi)
        dma("sp", xt, x_d[si, k * 128:(k + 1) * 128, :], [], [xB], "x%d" % xi)
        ss = stat[:, xi:xi + 1]
        rs_ = stat[:, 2 + xi:3 + xi]
        act(junk, xt, ACTF.Square, [xB], [buf("ss%d" % xi), buf("junk")], accum_out=ss)
        act(rs_, ss, ACTF.Sqrt, [buf("ss%d" % xi), buf("statinit")], [buf("rs%d" % xi)], scale=1.0 / D, bias=epsc)
        P.op("dve", "reciprocal", dict(out=rs_, in_=rs_), [buf("rs%d" % xi)], [buf("rs%d" % xi)])
        ts(xsb[xi], xt, rs_, None, ALU.mult, None, [xB, buf("rs%d" % xi)], [buf("xsb%d" % xi)])

    def S2(si, k):
        xi = k % 2
        t4 = k % 4
        xb16 = xsb[xi]
        for kc in range(8):
            bk = 2 + kc // 2
            dst = tpb[kc // 2][:, (kc % 2) * 512 + t4 * 128: (kc % 2) * 512 + (t4 + 1) * 128]
            P.op("pe", "transpose", dict(out=dst, in_=xb16[:, kc * 128:(kc + 1) * 128], identity=ident),
                 [buf("xsb%d" % xi), buf("ident")], [bankB[bk]])
        if t4 == 3:
            grp = k // 4
            for kc in range(8):
                bk = 2 + kc // 2
                src = tpb[kc // 2][:, (kc % 2) * 512:(kc % 2 + 1) * 512]
                dst = hnT[:, kc, grp * 512:(grp + 1) * 512]
                if bk < 4:
                    act(dst, src, ACTF.Copy, [bankB[bk], buf("gpre")], [buf("hnT")], scale=gpre[:, kc:kc + 1])
                else:
                    ts(dst, src, gpre[:, kc:kc + 1], None, ALU.mult, None, [bankB[bk], buf("gpre")], [buf("hnT")])

    def stageF(si, wslot0):
        mrgT = [qbuf[0], qbuf[1], ubuf[0], ubuf[1], kbuf[0][:, 0:T], kbuf[1][:, 0:T], ebuf[0][:, 0:T], ebuf[1][:, 0:T]]
        mB = [buf(nm) for nm in ("q0", "q1", "u0", "u1", "k0", "k1", "e0", "e1")]
        wslot = wslot0
        for mo in range(8):
            dma("sp", wbuf[wslot], wf_s[mo], [buf("wf_s")], [buf("w%d" % wslot)], "w%d" % wslot)
            wv = wbuf[wslot].rearrange("p (k c) -> p k c", k=32)
            wB = buf("w%d" % wslot)
            for tt_ in range(4):
                tsl = slice(tt_ * 512, (tt_ + 1) * 512)
                pbk = [0, 1, 2, 3] if (tt_ % 2 == 0) else [4, 5, 6, 7]
                ya, yb, ga, gb = [banks[i] for i in pbk]
                for kc in range(9):
                    mm(ya, wv[:, kc, :], abT[:, kc, tsl], kc == 0, kc == 8, [wB, buf("abT_c%d" % kc)], [bankB[pbk[0]]])
                for kc in range(7):
                    mm(yb, wv[:, 9 + kc, :], abT[:, 9 + kc, tsl], kc == 0, kc == 6, [wB, buf("abT_c%d" % (9 + kc))], [bankB[pbk[1]]])
                for kc in range(8):
                    mm(ga, wv[:, 16 + kc, :], hnT[:, kc, tsl], kc == 0, kc == 7, [wB, buf("hnT")], [bankB[pbk[2]]])
                for kc in range(8):
                    mm(gb, wv[:, 24 + kc, :], hnT[:, kc, tsl], kc == 0, kc == 7, [wB, buf("hnT")], [bankB[pbk[3]]])
                ta, tb_ = tmpf[0], tmpf[1]
                m1, m2 = tmpf[2], tmpf[3]
                act(ta, ga, ACTF.Tanh, [bankB[pbk[2]], buf("bgate")], [buf("tf0")], scale=0.5, bias=bgate[:, mo:mo + 1])
                act(tb_, gb, ACTF.Tanh, [bankB[pbk[3]], buf("bgate")], [buf("tf1")], scale=0.5, bias=bgate[:, 8 + mo:9 + mo])
                stt(m1, ta, 1.0, ya, ALU.add, ALU.mult, [buf("tf0"), bankB[pbk[0]]], [buf("tf2")])
                stt(m2, tb_, 1.0, yb, ALU.add, ALU.mult, [buf("tf1"), bankB[pbk[1]]], [buf("tf3")])
                tt(mrgT[mo][:, tsl], m1, m2, ALU.add, [buf("tf2"), buf("tf3")], [mB[mo]])
            wslot ^= 1
        return wslot, mrgT, mB

    xrb = [xr0, xr1]
    xrB = [[buf("xr0")], [buf("tf2"), buf("tf3")]]
    pb2s = [[0, 1], [6, 7]]

    def T0():
        for h in range(2):
            dma("sp", wbuf[h], wo_s[h], [buf("wo_s")], [buf("w%d" % h)], "w%d" % h)

    def T1(si, t16, mrgT, mB):
        wo = [wbuf[h].rearrange("p (k c) -> p k c", k=8) for h in range(2)]
        pb2 = pb2s[t16 % 2]
        yi = t16 % 2
        dma("pool", xrb[yi], x_d[si, t16 * 128:(t16 + 1) * 128, :], [], xrB[yi], "xr%d" % yi)
        for h in range(2):
            for kc in range(8):
                mm(banks[pb2[h]], mrgT[kc][:, t16 * 128:(t16 + 1) * 128], wo[h][:, kc, :], kc == 0, kc == 7,
                   [mB[kc], buf("w%d" % h)], [bankB[pb2[h]]])

    def T2(si, t16):
        pb2 = pb2s[t16 % 2]
        yi = t16 % 2
        ss2 = stat[:, 8 + 2 * yi: 10 + 2 * yi]
        for h in range(2):
            act(junk2[:, h * 512:(h + 1) * 512], banks[pb2[h]], ACTF.Square, [bankB[pb2[h]]],
                [buf("ssF%d_%d" % (yi, h)), buf("junkT%d" % h)], accum_out=ss2[:, h:h + 1])
        rr = stat[:, 16 + yi:17 + yi]
        tt(rr, ss2[:, 0:1], ss2[:, 1:2], ALU.add, [buf("ssF%d_0" % yi), buf("ssF%d_1" % yi)], [buf("rr%d" % yi)])
        act(rr, rr, ACTF.Sqrt, [buf("rr%d" % yi), buf("statinit")], [buf("rr%d" % yi)], scale=0.25 / D, bias=epsc)
        P.op("dve", "reciprocal", dict(out=rr, in_=rr), [buf("rr%d" % yi)], [buf("rr%d" % yi)])

    def T3(si, t16):
        pb2 = pb2s[t16 % 2]
        yi = t16 % 2
        yt = yst[yi]
        xr = xrb[yi]
        rr = stat[:, 16 + yi:17 + yi]
        for h in range(2):
            stt(yt[:, h * 512:(h + 1) * 512], banks[pb2[h]], rr, gpost[:, h * 512:(h + 1) * 512], ALU.mult, ALU.mult,
                [bankB[pb2[h]], buf("rr%d" % yi), buf("gpost")], [buf("yst%d" % yi)])
        tt(yt, yt, xr, ALU.add, [buf("yst%d" % yi)] + xrB[yi], [buf("yst%d" % yi)])
        dma("pool", y_d[si, t16 * 128:(t16 + 1) * 128, :], yt, [buf("yst%d" % yi)], [buf("ydram")], "ys%d" % yi)

    def boundary(si, mrgT, mB, with_tail, with_s0):
        if with_tail:
            T0()
        for i in range(-1, 16):
            if i + 1 < 16:
                if with_tail:
                    T1(si, i + 1, mrgT, mB)
                if with_s0:
                    S1(si + 1, i + 1)
            if i >= 0:
                if with_tail:
                    T2(si, i)
                if with_s0:
                    S2(si + 1, i)
                if with_tail:
                    T3(si, i)

    def drain(gen):
        for _ in gen:
            pass

    def interleave(main, filler, nunits, nfill):
        fdone = filler is None
        ui = 0
        for _ in main:
            if not fdone:
                k = ((ui + 1) * nfill) // nunits - (ui * nfill) // nunits
                for _k in range(k):
                    try:
                        next(filler)
                    except StopIteration:
                        fdone = True
                        break
            ui += 1
        if not fdone:
            drain(filler)

    def roundrobin(g1, g2):
        d1 = d2 = False
        while not (d1 and d2):
            if not d1:
                try:
                    next(g1)
                except StopIteration:
                    d1 = True
            if not d2:
                try:
                    next(g2)
                except StopIteration:
                    d2 = True

    wslot = 0
    pending = [None]
    btg = btab_gen()
    bt_done = [False]

    def bt_step(k=1):
        for _ in range(k):
            if bt_done[0]:
                return
            try:
                next(btg)
            except StopIteration:
                bt_done[0] = True

    boundary(-1, None, None, False, True)
    for si in range(nseq):
        bs = 0
        load_chunk_consts(order[0], wslot)
        drain(inproj_gen(order[0], bs, wslot))
        for oi, n in enumerate(order):
            kind, d, g = chunk_geom(n)
            nxt = order[oi + 1] if oi + 1 < len(order) else None
            eslot = wslot
            filler = None
            if si == 0:
                if oi + 2 < len(order):
                    emit_casts(oi + 2, oi + 3, [buf("v%d" % bs)])
                if oi == 8:
                    dma("pool", wf_s, wf_d, [buf("v%d" % bs)], [buf("wf_s")], "pp5")
                    dma("pool", wo_s, wo_d, [buf("v%d" % bs)], [buf("wo_s")], "pp6")
            if n == NCH_A and not bt_done[0]:
                bt_step(1000)
            if nxt is not None:
                load_chunk_consts(nxt, wslot ^ 1)
                filler = inproj_gen(nxt, bs ^ 1, wslot ^ 1)
            if kind == "A01":
                main = att_A01(n, bs, eslot)
                nunits = (17 if d == 1 else 20)
            elif kind == "A2":
                main = att_A2(n, bs, eslot)
                nunits = 8
            else:
                main = att_B(n, bs, eslot)
                nunits = 32
            nfill = 24 + (0 if nxt is None else sum(2 if len(g_) > 2 else 1 for g_ in vgroups(nxt)))
            if si == 0 and not bt_done[0]:
                def main_bt(m):
                    for _ in m:
                        bt_step(1)
                        yield
                main = main_bt(main)
            if pending[0] is not None:
                def main_pp(m, pg):
                    for _ in m:
                        try:
                            next(pg)
                        except StopIteration:
                            pass
                        yield
                    for _ in pg:
                        pass
                main = main_pp(main, pending[0])
                pending[0] = None
            interleave(main, filler, nunits, nfill)
            if n < NCH_A and n // 3 == 2:
                pending[0] = post_A_gen(n % 3)
            bs ^= 1
            wslot ^= 1
        wslot, mrgT_, mB_ = stageF(si, wslot)
        boundary(si, mrgT_, mB_, True, si + 1 < nseq)

    P.emit()
    return nc


_ALIAS = {}


def _prep_weights(w_in, w_proj_a, w_proj_b, w_out):
    w_in = np.asarray(w_in, np.float32)[0]
    cols = {}
    offs = [0, WA, 2 * WA, 3 * WA, 4 * WA, 4 * WA + WB, 4 * WA + 2 * WB, 4 * WA + 3 * WB, 4 * WA + 4 * WB,
            4 * WA + 4 * WB + D]
    qa, ka, va, za, qb, kb, vb, zb, ga, gb = offs

    def tile_fm(c0):
        return w_in[:, c0:c0 + 128].reshape(8, 128, 128).transpose(1, 0, 2)

    wch = np.zeros((NCH, 128, 4, 8, 128), np.float32)
    for n in range(NCH):
        if n < NCH_A:
            c = n * 128
            mats = (qa + c, ka + c, za + c, va + c)
        else:
            c = (n - NCH_A) * 128
            mats = (qb + c, kb + c, zb + c, vb + c)
        for mi, c0 in enumerate(mats):
            wch[n, :, mi] = tile_fm(c0)
    wch = wch.reshape(NCH, 128, 4096)
    wa = np.asarray(w_proj_a, np.float32)[0]
    wb = np.asarray(w_proj_b, np.float32)[0]
    wf = np.zeros((8, 128, 32, 128), np.float32)
    for mo in range(8):
        ms = slice(mo * 128, (mo + 1) * 128)
        wf[mo, :, 0:9] = wa[:, ms].reshape(9, 128, 128).transpose(1, 0, 2)
        wf[mo, :, 9:16] = wb[:, ms].reshape(7, 128, 128).transpose(1, 0, 2)
        wf[mo, :, 16:24] = tile_fm(ga + mo * 128)
        wf[mo, :, 24:32] = tile_fm(gb + mo * 128)
    wf = wf.reshape(8, 128, 4096)
    wo_ = np.asarray(w_out, np.float32)[0]
    wo = np.zeros((2, 128, 8, 512), np.float32)
    for h in range(2):
        wo[h] = wo_[:, h * 512:(h + 1) * 512].reshape(8, 128, 512).transpose(1, 0, 2)
    wo = wo.reshape(2, 128, 4096)
    return np.ascontiguousarray(wch), np.ascontiguousarray(wf), np.ascontiguousarray(wo)


_NC_CACHE = {}


def run_layer(x_cores, norm_pre, w_in, b_gate, rpb, w_proj_a, w_proj_b, w_out, norm_post, core_ids=None):
    nseq = x_cores[0].shape[0]
    if nseq not in _NC_CACHE:
        _NC_CACHE[nseq] = build(nseq)
    nc = _NC_CACHE[nseq]
    wch, wf, wo = _prep_weights(w_in, w_proj_a, w_proj_b, w_out)
    eta = _alibi_tables()
    bbias, bmask, _ = _b_tables(np.asarray(rpb, np.float32)[0])
    gpre = np.ascontiguousarray(np.asarray(norm_pre, np.float32)[0].reshape(8, 128).T)
    bg = np.asarray(b_gate, np.float32)[0]
    bgl = np.ascontiguousarray(bg.reshape(2, 8, 128).transpose(2, 0, 1).reshape(128, 16))
    gpost = np.ascontiguousarray(np.broadcast_to(np.asarray(norm_post, np.float32)[0][None, :], (128, D)))
    ident = np.eye(128, dtype=np.float32)
    common = {"wch": wch, "wf": wf, "wo": wo, "eta": eta, "bbias": bbias, "bmask": bmask, "gpre": gpre,
              "bgate": bgl, "gpost": gpost, "ident": ident}
    in_maps = []
    for xc in x_cores:
        m = dict(common)
        m["x"] = np.ascontiguousarray(xc, dtype=np.float32)
        in_maps.append(m)
    if core_ids is None:
        core_ids = list(range(len(x_cores)))
    res = run_bass_kernel_spmd(nc, in_maps, core_ids=core_ids)
    return [r["y"] for r in res.results]


def kernel(x_prompt, x_sample, norm_pre, w_in, b_gate, rpb, w_proj_a, w_proj_b, w_out, norm_post):
    xp = np.asarray(x_prompt, np.float32)
    xs = np.asarray(x_sample, np.float32)
    x_cores = []
    for c in range(NCORES):
        x_cores.append(np.concatenate([xp[2 * c:2 * c + 2], xs[4 * c:4 * c + 4]], axis=0))
    ys = run_layer(x_cores, norm_pre, w_in, b_gate, rpb, w_proj_a, w_proj_b, w_out, norm_post)
    yp = np.concatenate([y[0:2] for y in ys], axis=0)
    ysm = np.concatenate([y[2:6] for y in ys], axis=0)
    return (yp.astype(np.float32), ysm.astype(np.float32))
```
